# Optimizing a Trainium2 kernel written in Bass

```python
import math
import jax
import jax.numpy as jnp
from jax import lax
import numpy as np

D_MODEL = 2048
BATCH = 4
SEQ = 8192
DEPTH = 4

GRID_W = 64
CTX_LEN = 256
N_MIXERS = 3
EPS = 1e-6
D_FF = 4 * D_MODEL
N_A = (DEPTH + 2) // 3
N_B = (DEPTH + 1) // 3
N_C = DEPTH // 3

D_A = D_MODEL
DK_A = 128
H_A = D_A // DK_A
CHUNK_A = 32

DH_B = 128
HQ_B = D_MODEL // DH_B
HKV_B = HQ_B // 4
GRP_B = HQ_B // HKV_B
QW_B = HQ_B * DH_B
KW_B = HKV_B * DH_B
WINDOW_B = 128
BLK_B = 128
SCALE_B = DH_B ** -0.5
ROPE_BASE = 10000.0
ROPE_PAIRS = DH_B // 4

D_INNER_C = 2 * D_MODEL
P_C = 64
H_C = D_INNER_C // P_C
N_GROUPS_C = 8
HPG_C = H_C // N_GROUPS_C
N_STATE_C = 128
GN_C = N_GROUPS_C * N_STATE_C
CONV_C = 5
CONV_CH_C = D_INNER_C + 2 * GN_C
IN_C = D_INNER_C + CONV_CH_C + 2 * H_C
CHUNK_C = 128
DT_MIN = 1e-3
DT_MAX = 1e-1

kernel_name = 'hybrid_hgrn2_swa_ssd_flow_block'


def _rmsnorm(x, g):
    xf = x.astype(jnp.float32)
    y = xf * lax.rsqrt(jnp.mean(xf * xf, axis=-1, keepdims=True) + EPS)
    return (y * g.astype(jnp.float32)).astype(x.dtype)


def _modulate(h, g, shift, scale):
    return _rmsnorm(h, g) * (1.0 + scale) + shift


def _mlp(u, w1, w2):
    return jnp.square(jax.nn.relu(u @ w1)) @ w2


def _identity(a):
    return a


def _flip_t(a):
    return jnp.flip(a, axis=1)


def _bidir_scan(scan_fn, init, ctx_dirs, lat_dirs):
    y_c, y_l = None, None
    for d in range(2):
        fl = _flip_t if d else _identity
        o_c, s_c = scan_fn(init, *[fl(a) for a in ctx_dirs[d]])
        o_l, _ = scan_fn(s_c, *[fl(a) for a in lat_dirs[d]])
        y_l = fl(o_l) if y_l is None else y_l + fl(o_l)
        if o_c is not None:
            y_c = fl(o_c) if y_c is None else y_c + fl(o_c)
    return y_c, y_l


def _glr_scan(s0, k, v, logf, q=None):
    emit = q is not None
    bsz, t_len, heads, _ = k.shape
    n = t_len // CHUNK_A

    def chunks(a):
        return a.reshape(bsz, n, CHUNK_A, heads, a.shape[-1]).transpose(1, 0, 3, 2, 4)

    tri = jnp.tril(jnp.ones((CHUNK_A, CHUNK_A), dtype=bool))[:, :, None]

    def step(S, xs):
        kc, vc, lc = xs[:3]
        G = jnp.cumsum(lc, axis=2)
        G_end = G[:, :, -1]
        S_new = jnp.exp(G_end)[..., None] * S + jnp.einsum(
            'bhcd,bhce->bhde', kc * jnp.exp(G_end[:, :, None] - G), vc)
        if not emit:
            return S_new, None
        qc = xs[3]
        decay = jnp.exp(jnp.where(tri, G[:, :, :, None] - G[:, :, None], -jnp.inf))
        attn = jnp.einsum('bhtd,bhtsd,bhsd->bhts', qc, decay, kc)
        o = jnp.einsum('bhts,bhse->bhte', attn, vc) + jnp.einsum('bhtd,bhde->bhte', qc * jnp.exp(G), S)
        return S_new, o

    xs = tuple(chunks(a) for a in ((k, v, logf, q) if emit else (k, v, logf)))
    S_fin, o = lax.scan(step, s0, xs)
    if emit:
        o = o.transpose(1, 0, 3, 2, 4).reshape(bsz, t_len, heads, -1)
    return o, S_fin


def _ssd_scan(h0, x, dt, la, bm, cm=None):
    emit = cm is not None
    bsz, t_len = x.shape[:2]
    n = t_len // CHUNK_C

    def chunks(a):
        return jnp.moveaxis(a.reshape(bsz, n, CHUNK_C, *a.shape[2:]), 1, 0)

    tri = jnp.tril(jnp.ones((CHUNK_C, CHUNK_C), dtype=bool))

    def step(h, xs):
        xc, dtc, lac, bc = xs[:4]
        cum = jnp.cumsum(lac, axis=1)
        cum_end = cum[:, -1]
        h_new = jnp.exp(cum_end)[..., None, None] * h + jnp.einsum(
            'bsgn,bsge,bsgep->bgepn', bc, jnp.exp(cum_end[:, None] - cum) * dtc, xc)
        if not emit:
            return h_new, None
        cc = xs[4]
        cum_t = jnp.moveaxis(cum, 1, -1)
        seg = jnp.exp(jnp.where(tri, cum_t[..., :, None] - cum_t[..., None, :], -jnp.inf))
        cb = jnp.einsum('btgn,bsgn->bgts', cc, bc)
        y = jnp.einsum('bgts,bgets,bsge,bsgep->btgep', cb, seg, dtc, xc)
        y = y + jnp.einsum('btgn,bgepn,btge->btgep', cc, h, jnp.exp(cum))
        return h_new, y

    xs = tuple(chunks(a) for a in ((x, dt, la, bm, cm) if emit else (x, dt, la, bm)))
    h_fin, y = lax.scan(step, h0, xs)
    if emit:
        y = jnp.moveaxis(y, 0, 1).reshape(x.shape)
    return y, h_fin


def _dwconv_centred(u, w, b):
    ch = u.shape[-1]
    y = lax.conv_general_dilated(u, w[:, None, :], window_strides=(1,),
                                 padding=((CONV_C // 2, CONV_C // 2),),
                                 dimension_numbers=('NWC', 'WIO', 'NWC'),
                                 feature_group_count=ch)
    return y + b


def _rope_axis(x, ang):
    cos = jnp.cos(ang)[None, :, None, :]
    sin = jnp.sin(ang)[None, :, None, :]
    x1, x2 = jnp.split(x, 2, axis=-1)
    return jnp.concatenate([x1 * cos - x2 * sin, x2 * cos + x1 * sin], axis=-1)


def _rope_2d(x, ang_row, ang_col):
    xr, xc = jnp.split(x, 2, axis=-1)
    return jnp.concatenate([_rope_axis(xr, ang_row), _rope_axis(xc, ang_col)], axis=-1).astype(x.dtype)


def _sink_softmax(s, sink):
    sk = jnp.broadcast_to(sink[None, :, :, None, None], s.shape[:-1] + (1,))
    return jax.nn.softmax(jnp.concatenate([s, sk], axis=-1), axis=-1)[..., :-1]


def _banded_attention(q, k, v, kc, vc, sink):
    bsz, t_len = q.shape[:2]
    nb = t_len // BLK_B
    pad = ((0, 0), (BLK_B, BLK_B), (0, 0), (0, 0))
    kp = jnp.pad(k, pad).reshape(bsz, nb + 2, BLK_B, HKV_B, DH_B)
    vp = jnp.pad(v, pad).reshape(bsz, nb + 2, BLK_B, HKV_B, DH_B)
    kband = jnp.concatenate([kp[:, :-2], kp[:, 1:-1], kp[:, 2:]], axis=2)
    vband = jnp.concatenate([vp[:, :-2], vp[:, 1:-1], vp[:, 2:]], axis=2)
    qb = q.reshape(bsz, nb, BLK_B, HKV_B, GRP_B, DH_B)
    offs = jnp.arange(3 * BLK_B) - BLK_B
    qoff = jnp.arange(BLK_B)

    def block(args):
        qj, kj, vj, j = args
        kpos = j * BLK_B + offs
        qpos = j * BLK_B + qoff
        valid = (jnp.abs(kpos[None, :] - qpos[:, None]) <= WINDOW_B) & (kpos >= 0)[None, :] & (kpos < t_len)[None, :]
        s_loc = jnp.einsum('bqkgd,bskd->bkgqs', qj, kj).astype(jnp.float32) * SCALE_B
        s_loc = jnp.where(valid, s_loc, -jnp.inf)
        s_ctx = jnp.einsum('bqkgd,bskd->bkgqs', qj, kc).astype(jnp.float32) * SCALE_B
        p = _sink_softmax(jnp.concatenate([s_loc, s_ctx], axis=-1), sink)
        o = (jnp.einsum('bkgqs,bskd->bqkgd', p[..., :3 * BLK_B], vj)
             + jnp.einsum('bkgqs,bskd->bqkgd', p[..., 3 * BLK_B:], vc))
        return o.reshape(bsz, BLK_B, QW_B).astype(q.dtype)

    xs = (jnp.moveaxis(qb, 1, 0), jnp.moveaxis(kband, 1, 0), jnp.moveaxis(vband, 1, 0), jnp.arange(nb))
    o = lax.map(block, xs)
    return jnp.moveaxis(o, 0, 1).reshape(bsz, t_len, QW_B)


def _hgrn2_scan_inputs(u, w_scan, lb):
    bsz, t_len, _ = u.shape
    z = (u @ w_scan).astype(jnp.float32).reshape(bsz, t_len, 3, H_A, DK_A)
    v = z[:, :, 0]
    zf = z[:, :, 1:]
    lbf = lb.astype(jnp.float32).reshape(2, H_A, DK_A)
    logf = jnp.logaddexp(jnp.log(lbf), jnp.log1p(-lbf) + jax.nn.log_sigmoid(zf))
    k = (1.0 - lbf) * jax.nn.sigmoid(-zf)
    return [(k[:, :, d], v, logf[:, :, d]) for d in range(2)]


def _hgrn2_mixer(uc, ul, w_in, lb, onorm, w_out, emit_ctx):
    w_read, w_scan = w_in[:, :2 * D_A], w_in[:, 2 * D_A:]

    def readout_inputs(u):
        qg = u @ w_read
        q = jax.nn.silu(qg[..., :D_A].astype(jnp.float32)).reshape(u.shape[0], u.shape[1], H_A, DK_A)
        return q, qg[..., D_A:]

    def readout(o, g, dtype):
        o = _rmsnorm(o, onorm.reshape(H_A, DK_A)).reshape(o.shape[0], o.shape[1], D_A)
        return (o * jax.nn.silu(g.astype(jnp.float32))).astype(dtype) @ w_out

    s0 = jnp.zeros((ul.shape[0], H_A, DK_A, DK_A), jnp.float32)
    ql, gl = readout_inputs(ul)
    lat_dirs = [t + (ql,) for t in _hgrn2_scan_inputs(ul, w_scan, lb)]
    ctx_dirs = _hgrn2_scan_inputs(uc, w_scan, lb)
    if emit_ctx:
        qc, gc = readout_inputs(uc)
        ctx_dirs = [t + (qc,) for t in ctx_dirs]
    oc, ol = _bidir_scan(_glr_scan, s0, ctx_dirs, lat_dirs)
    yc = readout(oc, gc, uc.dtype) if emit_ctx else None
    return yc, readout(ol, gl, ul.dtype)


def _swa_mixer(uc, ul, w_qkv, sink, w_out, ang_row, ang_col, emit_ctx):
    bsz, t_len, _ = ul.shape
    l_len = uc.shape[1]
    ql, kl, vl = jnp.split(ul @ w_qkv, [QW_B, QW_B + KW_B], axis=-1)
    ql = _rope_2d(ql.reshape(bsz, t_len, HQ_B, DH_B), ang_row, ang_col)
    kl = _rope_2d(kl.reshape(bsz, t_len, HKV_B, DH_B), ang_row, ang_col)
    vl = vl.reshape(bsz, t_len, HKV_B, DH_B)
    kc, vc = jnp.split(uc @ w_qkv[:, QW_B:], [KW_B], axis=-1)
    kc = kc.reshape(bsz, l_len, HKV_B, DH_B)
    vc = vc.reshape(bsz, l_len, HKV_B, DH_B)
    sink_f = sink.astype(jnp.float32).reshape(HKV_B, GRP_B)
    yl = _banded_attention(ql, kl, vl, kc, vc, sink_f) @ w_out
    yc = None
    if emit_ctx:
        qc = (uc @ w_qkv[:, :QW_B]).reshape(bsz, l_len, HKV_B, GRP_B, DH_B)
        s = jnp.einsum('bqkgd,bskd->bkgqs', qc, kc).astype(jnp.float32) * SCALE_B
        p = _sink_softmax(s, sink_f)
        yc = jnp.einsum('bkgqs,bskd->bqkgd', p, vc).reshape(bsz, l_len, QW_B).astype(uc.dtype) @ w_out
    return yc, yl


def _ssd_mixer(uc, ul, w_in, conv_w, conv_b, dt_bias, a_log, d_skip, onorm, w_out, emit_ctx):
    a_neg = -jnp.exp(a_log.astype(jnp.float32))
    d_f = d_skip.astype(jnp.float32).reshape(N_GROUPS_C, HPG_C, 1)

    def inputs(u):
        bsz, t_len, _ = u.shape
        z, xbc, dt_raw = jnp.split(u @ w_in, [D_INNER_C, D_INNER_C + CONV_CH_C], axis=-1)
        xbc = jax.nn.silu(_dwconv_centred(xbc, conv_w, conv_b)).astype(jnp.float32)
        xs, bm, cm = jnp.split(xbc, [D_INNER_C, D_INNER_C + GN_C], axis=-1)
        xs = xs.reshape(bsz, t_len, N_GROUPS_C, HPG_C, P_C)
        bm = bm.reshape(bsz, t_len, N_GROUPS_C, N_STATE_C)
        cm = cm.reshape(bsz, t_len, N_GROUPS_C, N_STATE_C)
        dt = jax.nn.softplus(dt_raw.astype(jnp.float32).reshape(bsz, t_len, 2, H_C) + dt_bias.astype(jnp.float32))
        la = (dt * a_neg).reshape(bsz, t_len, 2, N_GROUPS_C, HPG_C)
        dt = dt.reshape(bsz, t_len, 2, N_GROUPS_C, HPG_C)
        return z, xs, bm, cm, dt, la

    def readout(y, xs, z, dtype):
        bsz, t_len = y.shape[:2]
        y = (y + d_f * xs).reshape(bsz, t_len, D_INNER_C) * jax.nn.silu(z.astype(jnp.float32))
        y = _rmsnorm(y.reshape(bsz, t_len, N_GROUPS_C, -1), onorm.reshape(N_GROUPS_C, -1))
        return y.reshape(bsz, t_len, D_INNER_C).astype(dtype) @ w_out

    zl, xl, bl, cl, dtl, lal = inputs(ul)
    zc, xc, bc, cc, dtc, lac = inputs(uc)
    lat_dirs = [(xl, dtl[:, :, d], lal[:, :, d], bl, cl) for d in range(2)]
    ctx_dirs = [(xc, dtc[:, :, d], lac[:, :, d], bc) + ((cc,) if emit_ctx else ()) for d in range(2)]
    h0 = jnp.zeros((ul.shape[0], N_GROUPS_C, HPG_C, P_C, N_STATE_C), jnp.float32)
    yc, yl = _bidir_scan(_ssd_scan, h0, ctx_dirs, lat_dirs)
    out_c = readout(yc, xc, zc, uc.dtype) if emit_ctx else None
    return out_c, readout(yl, xl, zl, ul.dtype)


def _residual_update(h, y, m, g, w1, w2):
    h = h + m[2] * _rmsnorm(y, g[1])
    u = _modulate(h, g[2], m[3], m[4])
    return h + m[5] * _rmsnorm(_mlp(u, w1, w2), g[3])


def setup_inputs(seed: int = 0) -> dict:
    key = jax.random.key(seed)
    ks = jax.random.split(key, 24)
    D = D_MODEL

    def nrm(k, shape, scale):
        return jax.random.normal(k, shape, jnp.float32) * scale

    dt0 = jnp.exp(jax.random.uniform(ks[19], (N_C, 2, H_C), jnp.float32,
                                     minval=math.log(DT_MIN), maxval=math.log(DT_MAX)))
    return {
        'x': nrm(ks[0], (BATCH, SEQ, D), 1.0),
        'c': nrm(ks[1], (BATCH, D), 1.0),
        'ctx': nrm(ks[2], (BATCH, CTX_LEN, D), 1.0),
        'c_ctx': nrm(ks[3], (D,), 1.0),
        'ada_w': nrm(ks[4], (DEPTH, D, 6 * D), D ** -0.5),
        'ada_b': nrm(ks[5], (DEPTH, 6 * D), 0.01),
        'norm_g': 1.0 + nrm(ks[6], (DEPTH, 4, D), 0.02),
        'mlp_w1': nrm(ks[7], (DEPTH, D, D_FF), D ** -0.5),
        'mlp_w2': nrm(ks[8], (DEPTH, D_FF, D), D_FF ** -0.5),
        'a_w_in': nrm(ks[9], (N_A, D, 5 * D_A), D ** -0.5),
        'a_lb_logits': nrm(ks[10], (DEPTH, 2, D_A), 0.1),
        'a_onorm': 1.0 + nrm(ks[11], (N_A, D_A), 0.02),
        'a_w_out': nrm(ks[12], (N_A, D_A, D), D_A ** -0.5),
        'b_w_qkv': nrm(ks[13], (N_B, D, QW_B + 2 * KW_B), D ** -0.5),
        'b_sink': nrm(ks[14], (N_B, HQ_B), 0.5),
        'b_w_out': nrm(ks[15], (N_B, QW_B, D), QW_B ** -0.5),
        'c_w_in': nrm(ks[16], (N_C, D, IN_C), D ** -0.5),
        'c_conv_w': nrm(ks[17], (N_C, CONV_C, CONV_CH_C), CONV_C ** -0.5),
        'c_conv_b': nrm(ks[18], (N_C, CONV_CH_C), 0.01),
        'c_dt_bias': dt0 + jnp.log(-jnp.expm1(-dt0)),
        'c_a_log': jnp.log(jax.random.uniform(ks[20], (N_C, 2, H_C), jnp.float32, minval=1.0, maxval=16.0)),
        'c_d': 1.0 + nrm(ks[21], (N_C, H_C), 0.1),
        'c_onorm': 1.0 + nrm(ks[22], (N_C, D_INNER_C), 0.02),
        'c_w_out': nrm(ks[23], (N_C, D_INNER_C, D), D_INNER_C ** -0.5),
    }


def reference(x, c, ctx, c_ctx, ada_w, ada_b, norm_g, mlp_w1, mlp_w2, a_w_in, a_lb_logits, a_onorm,
              a_w_out, b_w_qkv, b_sink, b_w_out, c_w_in, c_conv_w, c_conv_b, c_dt_bias, c_a_log, c_d,
              c_onorm, c_w_out):
    t_len = x.shape[1]
    rows = t_len // GRID_W
    inv_freq = ROPE_BASE ** (-jnp.arange(ROPE_PAIRS, dtype=jnp.float32) / ROPE_PAIRS)
    row = jnp.repeat(jnp.arange(rows, dtype=jnp.float32), GRID_W)
    col = jnp.tile(jnp.arange(GRID_W, dtype=jnp.float32), rows)
    ang_row = row[:, None] * inv_freq
    ang_col = col[:, None] * inv_freq

    p_lb = jax.nn.softmax(a_lb_logits.astype(jnp.float32), axis=0)
    lower_bounds = jnp.cumsum(p_lb, axis=0) - p_lb

    s_lat = jax.nn.silu(c)
    s_ctx = jax.nn.silu(c_ctx)
    hl, hc = x, ctx
    for i in range(DEPTH):
        emit_ctx = i < DEPTH - 1
        m_l = jnp.split((s_lat @ ada_w[i] + ada_b[i])[:, None, :], 6, axis=-1)
        m_c = jnp.split(s_ctx @ ada_w[i] + ada_b[i], 6, axis=-1)
        ul = _modulate(hl, norm_g[i, 0], m_l[0], m_l[1])
        uc = _modulate(hc, norm_g[i, 0], m_c[0], m_c[1])
        kind, j = i % N_MIXERS, i // N_MIXERS
        if kind == 0:
            yc, yl = _hgrn2_mixer(uc, ul, a_w_in[j], lower_bounds[i], a_onorm[j], a_w_out[j], emit_ctx)
        elif kind == 1:
            yc, yl = _swa_mixer(uc, ul, b_w_qkv[j], b_sink[j], b_w_out[j], ang_row, ang_col, emit_ctx)
        else:
            yc, yl = _ssd_mixer(uc, ul, c_w_in[j], c_conv_w[j], c_conv_b[j], c_dt_bias[j], c_a_log[j],
                                c_d[j], c_onorm[j], c_w_out[j], emit_ctx)
        hl = _residual_update(hl, yl, m_l, norm_g[i], mlp_w1[i], mlp_w2[i])
        if emit_ctx:
            hc = _residual_update(hc, yc, m_c, norm_g[i], mlp_w1[i], mlp_w2[i])
    return hl
```

```python
import math
import os
import numpy as np
from contextlib import ExitStack
import concourse.bass as bass
import concourse.mybir as mybir
from concourse.bass_utils import run_bass_kernel_spmd

F32 = mybir.dt.float32
BF16 = mybir.dt.bfloat16
AF = mybir.ActivationFunctionType
ALU = mybir.AluOpType

D = 2048
KC = 16
DFF = 8192
TCTX = 256
W = 256
EPS = 1e-6
KR = 8
CH = 16
NCH = 128 // CH


class Buf:
    __slots__ = ("t", "wtok", "readers")

    def __init__(self, t=None):
        self.t = t
        self.wtok = None
        self.readers = {}

    def __getitem__(self, k):
        return self.t[k]


class Rot:
    def __init__(self, bufs):
        self.bufs = bufs
        self.i = 0

    def next(self):
        b = self.bufs[self.i % len(self.bufs)]
        self.i += 1
        return b


class Arena:
    def __init__(self, P, name, n, dt):
        self.buf = P.sb(name, [128, n], dt)
        self.n = n
        self.off = 0

    def reset(self):
        self.off = 0

    def take(self, shape):
        n = int(np.prod(shape))
        assert self.off + n <= self.n, (self.off, n, self.n)
        ap = self.buf.t[:, self.off:self.off + n]
        self.off += n
        if len(shape) == 2:
            ap = ap.rearrange("p (a b) -> p a b", b=shape[1])
        elif len(shape) == 3:
            ap = ap.rearrange("p (a b c) -> p a b c", b=shape[1], c=shape[2])
        return Buf(ap)

    def takes(self, shape, n):
        return Rot([self.take(shape) for _ in range(n)])


class Prog:
    ENG = ("pe", "act", "dve", "pool", "sp")
    DMAQ = ("sp", "act", "pool")

    def __init__(self, nc, es):
        self.nc = nc
        self.es = es
        self.semh = {}
        for k in self.ENG:
            self.semh[k] = es.enter_context(nc.semaphore("s_" + k))
        for q in self.DMAQ:
            for i in range(KR):
                self.semh[("d", q, i)] = es.enter_context(nc.semaphore(f"d_{q}_{i}"))
        self.cnt = {k: 0 for k in self.ENG}
        self.dcnt = {q: 0 for q in self.DMAQ}
        self.seen = {k: {} for k in self.ENG}
        self.ops = {k: [] for k in self.ENG}
        self.nuniq = 0

    def sb(self, name, shape, dt):
        self.nuniq += 1
        return Buf(self.nc.alloc_sbuf_tensor(f"sb_{name}_{self.nuniq}", list(shape), dt))

    def sbs(self, name, shape, dt, n):
        return Rot([self.sb(name, shape, dt) for _ in range(n)])

    def ps(self, name, shape, dt):
        self.nuniq += 1
        return Buf(self.nc.alloc_psum_tensor(f"ps_{name}_{self.nuniq}", list(shape), dt))

    def _deps(self, eng, r, w):
        need = {}
        for b in r:
            if b.wtok is not None:
                k, v = b.wtok
                if need.get(k, 0) < v:
                    need[k] = v
        for b in w:
            if b.wtok is not None:
                k, v = b.wtok
                if need.get(k, 0) < v:
                    need[k] = v
            for k, v in b.readers.items():
                if need.get(k, 0) < v:
                    need[k] = v
        seen = self.seen[eng]
        waits = []
        for k, v in need.items():
            if k == eng and eng == "pe":
                continue
            if seen.get(k, 0) >= v:
                continue
            seen[k] = v
            waits.append((k, v))
        return waits

    def op(self, eng, fn, r=(), w=()):
        waits = self._deps(eng, r, w)
        self.cnt[eng] += 1
        tok = (eng, self.cnt[eng])
        self.ops[eng].append((waits, fn, (eng, 1)))
        for b in r:
            if b.readers.get(eng, 0) < tok[1]:
                b.readers[eng] = tok[1]
        for b in w:
            b.wtok = tok
            b.readers = {}
        return tok

    def dma(self, q, out, in_, r=(), w=(), **kw):
        i = self.dcnt[q]
        self.dcnt[q] += 1
        sk = ("d", q, i % KR)
        val = 16 * (i // KR + 1)
        waits = self._deps(q, r, w)
        if i >= KR and self.seen[q].get(sk, 0) < val - 16:
            self.seen[q][sk] = val - 16
            waits.append((sk, val - 16))
        self.ops[q].append((waits, (lambda e: e.dma_start(out=out, in_=in_, **kw)), (sk, 16)))
        tok = (sk, val)
        for b in r:
            if b.readers.get(sk, 0) < val:
                b.readers[sk] = val
        for b in w:
            b.wtok = tok
            b.readers = {}
        return tok

    def barrier(self):
        toks = [(k, self.cnt[k]) for k in ("pe", "act", "dve", "pool") if self.cnt[k] > 0]
        for q in self.DMAQ:
            n = self.dcnt[q]
            for j in range(KR):
                if n > j:
                    last = ((n - 1 - j) // KR) * KR + j
                    toks.append((("d", q, j), 16 * (last // KR + 1)))
        for eng in self.ENG:
            seen = self.seen[eng]
            waits = []
            for k, v in toks:
                if seen.get(k, 0) < v:
                    seen[k] = v
                    waits.append((k, v))
            if waits:
                self.ops[eng].append((waits, None, None))

    def finish(self):
        waits = []
        for q in self.DMAQ:
            n = self.dcnt[q]
            for j in range(KR):
                if n > j:
                    last = ((n - 1 - j) // KR) * KR + j
                    waits.append((("d", q, j), 16 * (last // KR + 1)))
        for k in ("pe", "act", "dve", "pool"):
            if self.cnt[k] > 0:
                waits.append((k, self.cnt[k]))
        self.ops["sp"].append((waits, None, None))

    def emit(self):
        nc = self.nc
        block = self.es.enter_context(nc.Block())
        semh = self.semh

        def replay(key, e):
            for waits, fn, inc in self.ops[key]:
                if fn is None:
                    for (k, v) in waits:
                        e.wait_ge(semh[k], v)
                    continue
                for (k, v) in waits[1:]:
                    e.wait_ge(semh[k], v)
                ins = fn(e)
                if waits:
                    ins._wait_ge(semh[waits[0][0]], waits[0][1])
                ins.then_inc(semh[inc[0]], inc[1])

        @block.tensor
        def _(e):
            replay("pe", e)

        @block.scalar
        def _(e):
            replay("act", e)

        @block.vector
        def _(e):
            replay("dve", e)

        @block.gpsimd
        def _(e):
            replay("pool", e)

        @block.sync
        def _(e):
            replay("sp", e)

    def tt(self, eng, out, in0, in1, op, r, w):
        return self.op(eng, lambda e: e.tensor_tensor(out=out, in0=in0, in1=in1, op=op), r, w)

    def ts(self, eng, out, in0, s1, op0, r, w, s2=None, op1=None):
        if op1 is None:
            return self.op(eng, lambda e: e.tensor_scalar(out=out, in0=in0, scalar1=s1, scalar2=None, op0=op0), r, w)
        return self.op(eng, lambda e: e.tensor_scalar(out=out, in0=in0, scalar1=s1, scalar2=s2, op0=op0, op1=op1), r, w)

    def stt(self, eng, out, in0, scalar, in1, op0, op1, r, w):
        return self.op(eng, lambda e: e.scalar_tensor_tensor(out=out, in0=in0, scalar=scalar, in1=in1, op0=op0, op1=op1), r, w)

    def act(self, out, in_, func, r, w, scale=None, bias=None):
        kw = {}
        if scale is not None:
            kw["scale"] = scale
        if bias is not None:
            kw["bias"] = bias
        return self.op("act", lambda e: e.activation(out=out, in_=in_, func=func, **kw), r, w)

    def cp(self, eng, out, in_, r, w):
        if eng == "act":
            return self.op("act", lambda e: e.copy(out=out, in_=in_), r, w)
        return self.op(eng, lambda e: e.tensor_copy(out=out, in_=in_), r, w)

    def mm(self, out, lhsT, rhs, start, stop, r, w, skip=False):
        if skip:
            return self.op("pe", lambda e: e.matmul(out, lhsT=lhsT, rhs=rhs, start=start, stop=stop,
                                                    skip_group_check=True), r, w)
        return self.op("pe", lambda e: e.matmul(out, lhsT=lhsT, rhs=rhs, start=start, stop=stop), r, w)

    def tr(self, out, in_, ident, r, w):
        return self.op("pe", lambda e: e.transpose(out=out, in_=in_, identity=ident), r, w)

    def memset(self, eng, ap, val, w):
        return self.op(eng, lambda e: e.memset(ap, val), (), w)

    def scan(self, out, d0, d1, r, w):
        return self.op("dve", lambda e: e.tensor_tensor_scan(out=out, data0=d0, data1=d1, initial=0.0,
                                                              op0=ALU.mult, op1=ALU.add), r, w)


class Ctx:
    pass


def build(TL, kinds):
    depth = len(kinds)
    nA = sum(1 for k in kinds if k == "A")
    NT = TCTX + TL
    NG = NT // W
    nc = bass.Bass("TRN2", target_bir_lowering=False)
    es = ExitStack()
    P = Prog(nc, es)
    C = Ctx()
    C.P, C.nc, C.NT, C.NG, C.TL, C.depth = P, nc, NT, NG, TL, depth

    def din(name, shape, dt=F32):
        return nc.dram_tensor(name, list(shape), dt, kind="ExternalInput").ap()

    def dscr(name, shape, dt):
        return nc.dram_tensor(name, list(shape), dt, kind="Internal").ap()

    hT0 = din("hT0", [KC, 128, NT])
    ccol_d = din("ccol", [128, KC, 2])
    adaw_d = din("ada_w", [depth, D, 6 * D])
    adab_d = din("adab", [128, depth, 96])
    ng_d = din("ng", [128, depth, 4, KC])
    w1_d = din("mlp_w1", [depth, D, DFF])
    w2_d = din("mlp_w2", [depth, DFF, D])
    consts_d = din("consts", [128, 128 + W + 1024 + NCH])
    trib_d = din("trib", [128, 512])
    if nA:
        lbl_d = din("lbl", [128, depth, 2, KC])
        aon_d = din("aon", [128, nA, KC])
        awin_d = din("a_w_in", [nA, D, 5 * D])
        awout_d = din("a_w_out", [nA, D, D])
    nB = sum(1 for k in kinds if k == "B")
    if nB:
        C.bwqkv_d = din("b_w_qkv", [nB, D, 3072])
        C.bwout_d = din("b_w_out", [nB, D, D])
        C.bsink_d = din("bsink", [128, nB * 16])
        C.swac_d = din("swac", [128, 128 + 1024])
        C.ropec_d = din("ropec", [128, TL])
        C.ropes_d = din("ropes", [128, TL])
        C.QT = dscr("qT", [KC, 128, NT], BF16)
        C.KT = dscr("kT", [4, 128, NT], BF16)
        C.VTM = dscr("vtm", [NT, 512], BF16)
        C.qbuf = [[Buf() for _ in range(4)] for _ in range(NG)]
        C.kbuf = [Buf() for _ in range(NG)]
        C.vbuf = [[Buf() for _ in range(2)] for _ in range(NG)]
    nC = sum(1 for k in kinds if k == "C")
    if nC:
        C.cwin_d = din("c_w_in", [nC, D, 10368])
        C.cwout_d = din("c_w_out", [nC, 4096, D])
        C.cdtb_d = din("cdtb", [128, nC])
        C.calog_d = din("calog", [128, nC])
        C.cd_d = din("cdrep", [128, nC * 64])
        C.cconvw_d = din("cconvw", [128, nC, 48, 5])
        C.cconvb_d = din("cconvb", [128, nC, 48])
        C.conorm_d = din("conorm", [128, nC, 32])
        tri128_d = din("tri128", [128, 256])
        C.XP = dscr("xp", [48, 128, NT], F32)
        C.SZ = dscr("sz", [NT, 4096], BF16)
        C.DTM = dscr("dtm", [NT, 256], F32)
        C.XTM = dscr("xtm", [NT, 4096], BF16)
        C.BTM = dscr("btm", [NT, 1024], BF16)
        C.BT = dscr("bT", [8, 128, NT], BF16)
        C.CT = dscr("cT", [8, 128, NT], BF16)
        C.Y = [dscr(f"ysc{d_}", [NT, 4096], F32) for d_ in range(2)]
        C.xpbuf = [[Buf() for _ in range(12)] for _ in range(NG)]
        C.dtmbuf = [[Buf() for _ in range(2)] for _ in range(NG)]
        C.szbuf = [[[Buf() for _ in range(8)] for _ in range(2)] for _ in range(NG)]
        C.xtbuf = [[[Buf() for _ in range(4)] for _ in range(2)] for _ in range(NG)]
        C.btmbuf = [[Buf() for _ in range(2)] for _ in range(NG)]
        C.btbuf = [Buf() for _ in range(NG)]
        C.ctbuf = [Buf() for _ in range(NG)]
        C.ybuf = [[[[Buf() for _ in range(8)] for _ in range(2)] for _ in range(NG)] for _ in range(2)]
    outT = nc.dram_tensor("outT", [KC, 128, TL], F32, kind="ExternalOutput").ap()

    hT = dscr("hT", [KC, 128, NT], F32)
    C.hbuf = [Buf() for _ in range(NG)]
    if nA:
        SCR = [dscr(f"scrA{ty}", [KC, 128, NT], BF16) for ty in range(8)]
        DEC = dscr("decA", [128, 2, KC, NT // CH], F32)
        OSC = dscr("oA", [2, KC, 128, NT], F32)
        C.scrbuf = [[[Buf() for _ in range(KC)] for _ in range(8)] for _ in range(NG)]
        C.decbuf = [Buf() for _ in range(NG)]
        C.obuf = [[[Buf() for _ in range(4)] for _ in range(NG)] for _ in range(2)]

    ones = P.sb("ones", [128, 128], F32)
    identf = P.sb("identf", [128, 128], F32)
    identb = P.sb("identb", [128, 128], BF16)
    mskr = P.sb("mskr", [128, W], F32)
    trif = P.sb("trif", [128, 512], F32)
    trib = P.sb("trib", [128, 512], F32)
    rowm = P.sb("rowm", [128, NCH], F32)
    sT = P.sb("sT", [128, KC, 2], F32)
    adab = P.sb("adab", [128, depth, 96], F32)
    ngs = P.sb("ngs", [128, depth, 4, KC], F32)
    mcol = P.sb("mcol", [128, 96, 2], F32)
    mods = P.sb("mods", [128, 2, 6, KC], F32)

    P.memset("pool", ones[:], 1.0, [ones])
    P.dma("sp", identf[:], consts_d[:, 0:128], w=[identf])
    P.dma("sp", mskr[:], consts_d[:, 128:128 + W], w=[mskr])
    P.dma("sp", trif[:], consts_d[:, 128 + W:128 + W + 512], w=[trif])
    P.dma("sp", rowm[:], consts_d[:, 128 + W + 1024:128 + W + 1024 + NCH], w=[rowm])
    P.dma("sp", trib[:], trib_d, w=[trib])
    P.cp("dve", identb[:], identf[:], [identf], [identb])
    P.dma("sp", sT[:], ccol_d, w=[sT])
    P.act(sT[:], sT[:], AF.Silu, [sT], [sT])
    P.dma("sp", adab[:], adab_d, w=[adab])
    P.dma("sp", ngs[:], ng_d, w=[ngs])
    P.ts("dve", ngs[:], ngs[:], math.sqrt(D), ALU.mult, [ngs], [ngs])

    C.hg = P.sb("hg", [128, KC, W], F32)
    C.yb = P.sb("yb", [128, KC, W], F32)
    C.uT = P.sb("uT", [128, KC, W], BF16)
    C.h1T = P.sb("h1T", [128, 64, W], BF16)
    C.wb = P.sbs("wb", [128, KC * 512], BF16, 2)
    C.rs = P.sbs("rs", [128, W], F32, 2)
    C.tmpf = P.sbs("tmpf", [128, W], F32, 4)
    C.sq = P.sbs("sq", [128, W], F32, 3)
    C.psA = Rot([P.ps("psA", [128, 512], F32) for _ in range(2)])
    C.psN = C.psA.bufs[0]
    C.ones, C.identb, C.mskr, C.trif, C.trib, C.rowm = ones, identb, mskr, trif, trib, rowm
    C.mods, C.mcol, C.sT, C.adab, C.ngs = mods, mcol, sT, adab, ngs
    C.hT, C.hT0, C.outT = hT, hT0, outT
    C.wada = P.sbs("wada", [128, 1536], F32, 2)

    if nA:
        lbl = P.sb("lbl", [128, depth, 2, KC], F32)
        lbe = P.sb("lbe", [128, depth, 2, KC], F32)
        lbs = P.sb("lbs", [128, 2, KC], F32)
        C.lb = P.sb("lb", [128, depth, 2, KC], F32)
        C.oml = P.sb("oml", [128, depth, 2, KC], F32)
        C.aon = P.sb("aon", [128, nA, KC], F32)
        P.dma("sp", lbl[:], lbl_d, w=[lbl])
        P.dma("sp", C.aon[:], aon_d, w=[C.aon])
        P.ts("dve", C.aon[:], C.aon[:], math.sqrt(128.0), ALU.mult, [C.aon], [C.aon])
        P.act(lbe[:], lbl[:], AF.Exp, [lbl], [lbe])
        P.cp("dve", lbs[:], lbe[:, 0], [lbe], [lbs])
        for i in range(1, depth):
            P.tt("dve", lbs[:], lbs[:], lbe[:, i], ALU.add, [lbe, lbs], [lbs])
        P.op("dve", lambda e: e.reciprocal(out=lbs[:], in_=lbs[:]), [lbs], [lbs])
        for i in range(depth):
            P.tt("dve", lbe[:, i], lbe[:, i], lbs[:], ALU.mult, [lbe, lbs], [lbe])
        P.memset("dve", C.lb[:, 0], 0.0, [C.lb])
        for i in range(1, depth):
            P.tt("dve", C.lb[:, i], C.lb[:, i - 1], lbe[:, i - 1], ALU.add, [C.lb, lbe], [C.lb])
        P.ts("dve", C.oml[:], C.lb[:], -1.0, ALU.mult, [C.lb], [C.oml], s2=1.0, op1=ALU.add)
        C.SCR, C.DEC, C.OSC = SCR, DEC, OSC
        C.awin_d, C.awout_d = awin_d, awout_d

    C.identf = identf
    if nC:
        t0_ = P.sb("tri128f", [128, 128], F32)
        t1_ = P.sb("tri128b", [128, 128], F32)
        P.dma("sp", t0_[:], tri128_d[:, 0:128], w=[t0_])
        P.dma("sp", t1_[:], tri128_d[:, 128:256], w=[t1_])
        C.tri128 = [t0_, t1_]
    C.arf = Arena(P, "arf", 9216, F32)
    C.arb = Arena(P, "arb", 14336, BF16)
    C.psO = [P.ps(f"psO{h}", [128, 8, 128], F32) for h in range(2)]
    C.psS = P.ps("psS", [128, 4, 128], F32)
    C.psT = P.ps("psT", [128, 1024], BF16)

    C.S_w1, C.S_w2, C.S_in, C.S_out = [], [], [], []
    ja = jb = jc = 0
    for i, kind in enumerate(kinds):
        if kind == "A":
            C.S_in.append(prep_weight(C, f"S_in{i}", awin_d[ja], D, 5 * D))
            C.S_out.append(prep_weight(C, f"S_out{i}", awout_d[ja], D, D))
            ja += 1
        elif kind == "B":
            C.S_in.append(prep_weight(C, f"S_in{i}", C.bwqkv_d[jb], D, 3072))
            C.S_out.append(prep_weight(C, f"S_out{i}", C.bwout_d[jb], D, D))
            jb += 1
        else:
            C.S_in.append(prep_weight(C, f"S_in{i}", C.cwin_d[jc], D, 10368))
            C.S_out.append(prep_weight(C, f"S_out{i}", C.cwout_d[jc], 4096, D))
            jc += 1
        C.S_w1.append(prep_weight(C, f"S_w1{i}", w1_d[i], D, DFF))
        C.S_w2.append(prep_weight(C, f"S_w2{i}", w2_d[i], DFF, D))

    jA = 0
    jB = 0
    jC = 0
    for i, kind in enumerate(kinds):
        emit_ctx = i < depth - 1
        src = hT0 if i == 0 else hT
        last = (i == depth - 1)
        ada_phase(C, i, adaw_d)
        if os.environ.get("KSTOP") == "ada":
            break
        if kind == "A":
            hgrn2_layer(C, i, jA, src, last, emit_ctx, w1_d, w2_d)
            jA += 1
        elif kind == "B":
            swa_layer(C, i, jB, src, last, emit_ctx, w1_d, w2_d)
            jB += 1
        elif kind == "C":
            ssd_layer(C, i, jC, src, last, emit_ctx, w1_d, w2_d)
            jC += 1
        else:
            raise NotImplementedError(kind)
        mlp_phase(C, i, emit_ctx, last, w1_d, w2_d)
    P.finish()
    P.emit()
    return nc


def ada_phase(C, i, adaw_d):
    P = C.P
    for nb in range(8):
        for kc in range(KC):
            wt = C.wada.next()
            P.dma("sp", wt[:], adaw_d[i, kc * 128:(kc + 1) * 128, nb * 1536:(nb + 1) * 1536], w=[wt])
            for j in range(12):
                P.mm(C.psN[:, 2 * j:2 * j + 2], wt[:, j * 128:(j + 1) * 128], C.sT[:, kc, :],
                     kc == 0 and j == 0, kc == KC - 1, [wt, C.sT], [C.psN], skip=True)
        P.tt("dve", C.mcol[:, nb * 12:(nb + 1) * 12, :],
             C.psN[:, 0:24].rearrange("p (j s) -> p j s", s=2),
             C.adab[:, i, nb * 12:(nb + 1) * 12].unsqueeze(2).to_broadcast([128, 12, 2]),
             ALU.add, [C.psN, C.adab], [C.mcol])
    m = lambda j, s: C.mcol[:, j * 16:(j + 1) * 16, s]
    g = lambda n: C.ngs[:, i, n, :]
    rr = [C.mcol, C.ngs]
    for s in range(2):
        P.stt("dve", C.mods[:, s, 0, :], m(1, s), 1.0, g(0), ALU.add, ALU.mult, rr, [C.mods])
        P.cp("dve", C.mods[:, s, 1, :], m(0, s), rr, [C.mods])
        P.tt("dve", C.mods[:, s, 2, :], m(2, s), g(1), ALU.mult, rr, [C.mods])
        P.stt("dve", C.mods[:, s, 3, :], m(4, s), 1.0, g(2), ALU.add, ALU.mult, rr, [C.mods])
        P.cp("dve", C.mods[:, s, 4, :], m(3, s), rr, [C.mods])
        P.tt("dve", C.mods[:, s, 5, :], m(5, s), g(3), ALU.mult, rr, [C.mods])


def rstd_of(C, srcbuf, n_chunks, addc):
    P = C.P
    psn = C.psA.next()
    for kc in range(n_chunks):
        sq = C.sq.next()
        P.act(sq[:], srcbuf[:, kc, :], AF.Square, [srcbuf], [sq])
        P.mm(psn[:, 0:W], C.ones[:], sq[:], kc == 0, kc == n_chunks - 1, [C.ones, sq], [psn])
    rs = C.rs.next()
    P.ts("dve", rs[:], psn[:, 0:W], addc, ALU.add, [psn], [rs])
    P.act(rs[:], rs[:], AF.Ln, [rs], [rs])
    P.act(rs[:], rs[:], AF.Exp, [rs], [rs], scale=-0.5)
    return rs


def modulate(C, src, s, ia, ish):
    P = C.P
    rs = rstd_of(C, src, KC, D * EPS)
    for kc in range(KC):
        t = C.tmpf.next()
        P.tt("dve", t[:], src[:, kc, :], rs[:], ALU.mult, [src, rs], [t])
        P.ts("pool", C.uT[:, kc, :], t[:], C.mods[:, s, ia, kc:kc + 1], ALU.mult, [t, C.mods], [C.uT],
             s2=C.mods[:, s, ish, kc:kc + 1], op1=ALU.add)


def prep_weight(C, name, w2d, K, N):
    nk, nch = K // 128, N // 128
    S = C.nc.dram_tensor(name, [nch, 128, nk, 128], BF16, kind="Internal").ap()
    bufs = [Buf() for _ in range(nch)]
    for j in range(nch):
        C.P.dma("pool", S[j], w2d[:, j * 128:(j + 1) * 128].rearrange("(k p) n -> p k n", p=128), w=[bufs[j]])
    return (S, bufs, nk)


def load_w(C, wt, cslot, Sw, j0, nch, k0=0, nk=None):
    S, bufs, nkfull = Sw
    if nk is None:
        nk = nkfull
    wv = wt[:].rearrange("p (c k n) -> p c k n", k=nk, n=128)
    C.P.dma("sp", wv[:, cslot:cslot + nch], S[j0:j0 + nch, :, k0:k0 + nk, :].rearrange("c p k n -> p c k n"),
            r=bufs[j0:j0 + nch], w=[wt])
    return wv


def load_w_cols(C, wt, slot, w2d, col0, ncols, nk=KC):
    P = C.P
    P.dma("pool", wt[:].rearrange("p (k n) -> p k n", k=nk)[:, :, slot:slot + ncols],
          w2d[:, col0:col0 + ncols].rearrange("(k p) n -> p k n", p=128), w=[wt])


def resid_update(C, g, s, igg, hsrc, dst_ap):
    P = C.P
    rs = rstd_of(C, C.yb, KC, D * EPS)
    for kc in range(KC):
        t = C.tmpf.next()
        P.tt("dve", t[:], C.yb[:, kc, :], rs[:], ALU.mult, [C.yb, rs], [t])
        P.stt("dve", C.hg[:, kc, :], t[:], C.mods[:, s, igg, kc:kc + 1], C.hg[:, kc, :], ALU.mult, ALU.add,
              [t, C.mods, C.hg], [C.hg])


def mlp_sublayer(C, i, s, w1_d, w2_d):
    P = C.P
    modulate(C, C.hg, s, 3, 4)
    for jb in range(16):
        wt = C.wb.next()
        load_w_cols(C, wt, 0, w1_d[i], jb * 512, 512)
        wv = wt[:].rearrange("p (k n) -> p k n", k=KC)
        for jj in range(4):
            ps = C.psA.next()
            for kc in range(KC):
                P.mm(ps[:, 0:W], wv[:, kc, jj * 128:(jj + 1) * 128], C.uT[:, kc, :], kc == 0, kc == KC - 1,
                     [wt, C.uT], [ps])
            t = C.tmpf.next()
            P.act(t[:], ps[:, 0:W], AF.Relu, [ps], [t])
            P.tt("pool", C.h1T[:, jb * 4 + jj, :], t[:], t[:], ALU.mult, [t], [C.h1T])
    for fo in range(KC):
        wt = C.wb.next()
        load_w_cols(C, wt, 0, w2_d[i], fo * 128, 128, nk=64)
        wv = wt[:].rearrange("p (k n) -> p k n", k=64)
        ps = C.psA.next()
        for fc in range(64):
            P.mm(ps[:, 0:W], wv[:, fc, 0:128], C.h1T[:, fc, :], fc == 0, fc == 63, [wt, C.h1T], [ps])
        P.cp("act", C.yb[:, fo, :], ps[:, 0:W], [ps], [C.yb])
    resid_update(C, None, s, 5, None, None)


def layer_tail(C, i, g, s, src, wout, last, w1_d, w2_d, nkc=KC):
    P = C.P
    c0 = g * W
    ncol = 8192 // nkc
    for fb in range(D // ncol):
        wt = C.wb.next()
        wv = load_w(C, wt, 0, C.S_out[i], fb * (ncol // 128), ncol // 128)
        for jj in range(ncol // 128):
            ps = C.psA.next()
            for kc in range(nkc):
                P.mm(ps[:, 0:W], wv[:, jj, kc, :], C.uTx[:, kc, :], kc == 0, kc == nkc - 1,
                     [wt, C.uTx], [ps])
            P.cp("act", C.yb[:, fb * (ncol // 128) + jj, :], ps[:, 0:W], [ps], [C.yb])
    P.dma("sp", C.hg[:], src[:, :, c0:c0 + W].rearrange("k p n -> p k n"), r=[C.hbuf[g]], w=[C.hg])
    resid_update(C, g, s, 2, None, None)
    P.dma("act", C.hT[:, :, c0:c0 + W].rearrange("k p n -> p k n"), C.hg[:], r=[C.hg], w=[C.hbuf[g]])


def mlp_phase(C, i, emit_ctx, last, w1_d, w2_d):
    P = C.P
    P.barrier()
    WW = 512
    H = Buf(C.arf.buf.t[:, 0:8192].rearrange("p (k n) -> p k n", n=WW))
    U = Buf(C.arb.buf.t[:, 0:8192].rearrange("p (k n) -> p k n", n=WW))
    Ylo = Buf(C.hg.t[:].rearrange("p k n -> p (k n)").rearrange("p (k n) -> p k n", n=WW))
    Yhi = Buf(C.yb.t[:].rearrange("p k n -> p (k n)").rearrange("p (k n) -> p k n", n=WW))
    H1 = Buf(C.h1T.t[:].rearrange("p a b -> p (a b)").rearrange("p (k n) -> p k n", n=WW))
    w0, w1_ = C.wada.bufs[0].t, C.wada.bufs[1].t
    rs = Buf(w0[:, 0:512])
    tmps = Rot([Buf(w0[:, 512:1024]), Buf(w0[:, 1024:1536]), Buf(w1_[:, 1024:1536])])
    sqs = Rot([Buf(w1_[:, 0:512]), Buf(w1_[:, 512:1024])])
    groups = []
    if emit_ctx:
        groups.append((0, 256, 1))
    for j in range(C.TL // WW):
        groups.append((TCTX + j * WW, WW, 0))

    def Y(fo):
        return (Ylo if fo < 8 else Yhi), fo % 8

    def norm(getchunk, rbufs):
        psn = C.psA.next()
        for kc in range(KC):
            sq = sqs.next()
            P.act(sq[:, 0:wd], getchunk(kc), AF.Square, rbufs, [sq])
            P.mm(psn[:, 0:wd], C.ones[:], sq[:, 0:wd], kc == 0, kc == KC - 1, [C.ones, sq], [psn])
        P.ts("dve", rs[:, 0:wd], psn[:, 0:wd], D * EPS, ALU.add, [psn], [rs])
        P.act(rs[:, 0:wd], rs[:, 0:wd], AF.Ln, [rs], [rs])
        P.act(rs[:, 0:wd], rs[:, 0:wd], AF.Exp, [rs], [rs], scale=-0.5)

    for (c0, wd, s) in groups:
        hb = [C.hbuf[gg] for gg in range(c0 // W, (c0 + wd - 1) // W + 1)]
        P.dma("sp", H[:, :, 0:wd], C.hT[:, :, c0:c0 + wd].rearrange("k p n -> p k n"), r=hb, w=[H])
        norm(lambda kc: H[:, kc, 0:wd], [H])
        for kc in range(KC):
            t = tmps.next()
            P.tt("dve", t[:, 0:wd], H[:, kc, 0:wd], rs[:, 0:wd], ALU.mult, [H, rs], [t])
            P.ts("pool", U[:, kc, 0:wd], t[:, 0:wd], C.mods[:, s, 3, kc:kc + 1], ALU.mult, [t, C.mods], [U],
                 s2=C.mods[:, s, 4, kc:kc + 1], op1=ALU.add)
        for half in range(2):
            for jb in range(8):
                wt = C.wb.next()
                wv = load_w(C, wt, 0, C.S_w1[i], half * 32 + jb * 4, 4)
                for jj in range(4):
                    ps = C.psA.next()
                    for kc in range(KC):
                        P.mm(ps[:, 0:wd], wv[:, jj, kc, :], U[:, kc, 0:wd], kc == 0, kc == KC - 1,
                             [wt, U], [ps])
                    t = tmps.next()
                    P.act(t[:, 0:wd], ps[:, 0:wd], AF.Relu, [ps], [t])
                    P.tt("pool", H1[:, jb * 4 + jj, 0:wd], t[:, 0:wd], t[:, 0:wd], ALU.mult, [t], [H1])
            for fo in range(KC):
                wt = C.wb.next()
                wv = load_w(C, wt, 0, C.S_w2[i], fo, 1, k0=half * 32, nk=32)
                ps = C.psA.next()
                for fc in range(32):
                    P.mm(ps[:, 0:wd], wv[:, 0, fc, :], H1[:, fc, 0:wd], fc == 0, fc == 31, [wt, H1], [ps])
                yb_, yi = Y(fo)
                if half == 0:
                    P.cp("act", yb_[:, yi, 0:wd], ps[:, 0:wd], [ps], [yb_])
                else:
                    P.tt("dve", yb_[:, yi, 0:wd], yb_[:, yi, 0:wd], ps[:, 0:wd], ALU.add, [yb_, ps], [yb_])
        norm(lambda kc: Y(kc)[0][:, Y(kc)[1], 0:wd], [Ylo, Yhi])
        for kc in range(KC):
            yb_, yi = Y(kc)
            t = tmps.next()
            P.tt("dve", t[:, 0:wd], yb_[:, yi, 0:wd], rs[:, 0:wd], ALU.mult, [yb_, rs], [t])
            P.stt("dve", H[:, kc, 0:wd], t[:, 0:wd], C.mods[:, s, 5, kc:kc + 1], H[:, kc, 0:wd], ALU.mult, ALU.add,
                  [t, C.mods, H], [H])
        if last:
            P.dma("act", C.outT[:, :, c0 - TCTX:c0 - TCTX + wd].rearrange("k p n -> p k n"), H[:, :, 0:wd], r=[H])
        else:
            P.dma("act", C.hT[:, :, c0:c0 + wd].rearrange("k p n -> p k n"), H[:, :, 0:wd], r=[H], w=hb)
    P.barrier()


SCALE_B = 128.0 ** -0.5


def swa_layer(C, i, jB, src, last, emit_ctx, w1_d, w2_d):
    P = C.P
    NG, NT = C.NG, C.NT
    P.barrier()
    C.arf.reset()
    C.arb.reset()
    wqkv = C.bwqkv_d[jB]
    wout = C.bwout_d[jB]
    A, Bf = C.arf, C.arb
    rd = A.takes([512], 2)
    cosb = A.takes([W], 2)
    sinb = A.takes([W], 2)
    xf = A.takes([W], 2)
    mprev = A.take([512])
    mnext = A.take([512])
    sinkexp = A.take([16])
    pmTf = A.take([128])
    sq = Bf.take([16 * 128])
    kctx = Bf.take([4, 256])
    vctx = Bf.take([2, 512])
    kloc = Bf.take([4, 384])
    vloc = Bf.take([3, 512])
    eb = Bf.takes([512], 3)
    xb = Bf.takes([W], 2)
    stq = Bf.takes([4, W], 2)
    stv = Bf.take([4, W])
    vt = Bf.takes([512], 2)
    pmTb = Bf.take([128])
    onesb = Bf.take([128])
    psO0 = C.psO[0].t[:].rearrange("p a b -> p (a b)")
    psD = Buf(psO0[:, 0:512])
    psOt = Buf(psO0[:, 512:1024])
    P.dma("sp", pmTf[:], C.swac_d[:, 0:128], w=[pmTf])
    P.dma("sp", mprev[:], C.swac_d[:, 128:640], w=[mprev])
    P.dma("sp", mnext[:], C.swac_d[:, 640:1152], w=[mnext])
    P.dma("sp", sinkexp[:], C.bsink_d[:, jB * 16:(jB + 1) * 16], w=[sinkexp])
    P.act(sinkexp[:], sinkexp[:], AF.Exp, [sinkexp], [sinkexp])
    P.cp("dve", pmTb[:], pmTf[:], [pmTf], [pmTb])
    P.memset("pool", onesb[:], 1.0, [onesb])

    for g in range(NG):
        s = 1 if g == 0 else 0
        c0 = g * W
        P.dma("sp", C.hg[:], src[:, :, c0:c0 + W].rearrange("k p n -> p k n"), r=[C.hbuf[g]], w=[C.hg])
        modulate(C, C.hg, s, 0, 1)
        if g > 0:
            cb = cosb.next()
            sb_ = sinb.next()
            P.dma("sp", cb[:], C.ropec_d[:, c0 - TCTX:c0 - TCTX + W], w=[cb])
            P.dma("sp", sb_[:], C.ropes_d[:, c0 - TCTX:c0 - TCTX + W], w=[sb_])
        for blk in range(6):
            wt = C.wb.next()
            wv = load_w(C, wt, 0, C.S_in[i], blk * 4, 4)
            st = stq.next() if blk < 5 else stv
            for jj in range(4):
                ps = C.psA.next()
                for kc in range(KC):
                    P.mm(ps[:, 0:W], wv[:, jj, kc, :], C.uT[:, kc, :], kc == 0, kc == KC - 1,
                         [wt, C.uT], [ps])
                if blk == 5 or g == 0:
                    P.cp("act", st[:, jj, :], ps[:, 0:W], [ps], [st])
                else:
                    x = xf.next()
                    P.cp("act", x[:], ps[:, 0:W], [ps], [x])
                    xbb = xb.next()
                    P.cp("pool", xbb[:], x[:], [x], [xbb])
                    ps2 = C.psA.next()
                    P.mm(ps2[:, 0:W], pmTb[:], xbb[:], True, True, [pmTb, xbb], [ps2])
                    t1 = C.tmpf.next()
                    P.tt("dve", t1[:], x[:], cb[:], ALU.mult, [x, cb], [t1])
                    t2 = C.tmpf.next()
                    P.tt("dve", t2[:], ps2[:, 0:W], sb_[:], ALU.mult, [ps2, sb_], [t2])
                    P.tt("pool", st[:, jj, :], t1[:], t2[:], ALU.add, [t1, t2], [st])
            if blk < 4:
                P.dma("act", C.QT[blk * 4:(blk + 1) * 4, :, c0:c0 + W].rearrange("h p n -> p h n"), st[:],
                      r=[st], w=[C.qbuf[g][blk]])
            elif blk == 4:
                P.dma("act", C.KT[:, :, c0:c0 + W].rearrange("h p n -> p h n"), st[:], r=[st], w=[C.kbuf[g]])
            else:
                for t in range(2):
                    for gk in range(4):
                        P.tr(C.psT[:, gk * 128:(gk + 1) * 128], stv[:, gk, t * 128:(t + 1) * 128], C.identb[:],
                             [stv, C.identb], [C.psT])
                    v = vt.next()
                    P.cp("act", v[:], C.psT[:, 0:512], [C.psT], [v])
                    P.dma("act", C.VTM[c0 + t * 128:c0 + (t + 1) * 128, :], v[:], r=[v], w=[C.vbuf[g][t]])

    P.dma("sp", kctx[:], C.KT[:, :, 0:TCTX].rearrange("h p n -> p h n"), r=[C.kbuf[0]], w=[kctx])
    P.dma("sp", vctx[:], C.VTM[0:TCTX, :].rearrange("(t p) n -> p t n", p=128), r=C.vbuf[0], w=[vctx])
    for g in range(NG):
        if g == 0 and not emit_ctx:
            continue
        s = 1 if g == 0 else 0
        c0 = g * W
        for t in range(2):
            qc0 = c0 + t * 128
            P.dma("sp", sq[:].rearrange("p (h n) -> p h n", h=16),
                  C.QT[:, :, qc0:qc0 + 128].rearrange("h p n -> p h n"), r=C.qbuf[g], w=[sq])
            blocks = [("c", 0, 0), ("c", 1, 0)]
            if g > 0:
                lo = max(TCTX, qc0 - 128)
                hi = min(NT, qc0 + 256)
                nl = (hi - lo) // 128
                P.dma("sp", kloc[:, :, 0:hi - lo], C.KT[:, :, lo:hi].rearrange("h p n -> p h n"),
                      r=[C.kbuf[gg] for gg in range(lo // W, (hi - 1) // W + 1)], w=[kloc])
                P.dma("sp", vloc[:, 0:nl, :], C.VTM[lo:hi, :].rearrange("(t p) n -> p t n", p=128),
                      r=[C.vbuf[tt // 2][tt % 2] for tt in range(lo // 128, (hi - 1) // 128 + 1)], w=[vloc])
                for bi in range(nl):
                    blocks.append(("l", bi, (lo + bi * 128 - qc0) // 128))
            for gk in range(4):
                for bi, (kind, idx, rel) in enumerate(blocks):
                    if kind == "c":
                        kap = kctx[:, gk, idx * 128:(idx + 1) * 128]
                        vap = vctx[:, idx, gk * 128:(gk + 1) * 128]
                        rk, rv = kctx, vctx
                    else:
                        kap = kloc[:, gk, idx * 128:(idx + 1) * 128]
                        vap = vloc[:, idx, gk * 128:(gk + 1) * 128]
                        rk, rv = kloc, vloc
                    ps = C.psA.next()
                    P.mm(ps[:, 0:512], kap, sq[:, gk * 512:(gk + 1) * 512], True, True, [rk, sq], [ps])
                    e = eb.next()
                    P.act(e[:], ps[:, 0:512], AF.Exp, [ps], [e], scale=SCALE_B)
                    if rel == -1:
                        P.tt("pool", e[:], e[:], mprev[:], ALU.mult, [e, mprev], [e])
                    elif rel == 1:
                        P.tt("pool", e[:], e[:], mnext[:], ALU.mult, [e, mnext], [e])
                    first = bi == 0
                    lastb = bi == len(blocks) - 1
                    P.mm(psD[:], onesb[:], e[:], first, lastb, [onesb, e], [psD])
                    for hq in range(4):
                        P.mm(psOt[:, hq * 128:(hq + 1) * 128], vap, e[:, hq * 128:(hq + 1) * 128],
                             first and hq == 0, lastb, [rv, e], [psOt], skip=True)
                r = rd.next()
                for hq in range(4):
                    h = gk * 4 + hq
                    P.ts("dve", r[:, hq * 128:(hq + 1) * 128], psD[:, hq * 128:(hq + 1) * 128],
                         sinkexp[:, h:h + 1], ALU.add, [psD, sinkexp], [r])
                P.op("dve", (lambda rr: (lambda e_: e_.reciprocal(out=rr, in_=rr)))(r[:]), [r], [r])
                P.tt("dve", C.uT[:, gk * 4:(gk + 1) * 4, t * 128:(t + 1) * 128],
                     psOt[:].rearrange("p (a b) -> p a b", b=128), r[:].rearrange("p (a b) -> p a b", b=128),
                     ALU.mult, [psOt, r], [C.uT])
        C.uTx = C.uT
        layer_tail(C, i, g, s, src, wout, last, w1_d, w2_d)


def ssd_layer(C, i, jC, src, last, emit_ctx, w1_d, w2_d):
    P = C.P
    NG, NT = C.NG, C.NT
    win = C.cwin_d[jC]
    wout = C.cwout_d[jC]
    A, Bf = C.arf, C.arb
    P.barrier()
    A.reset()
    Bf.reset()
    dtb = A.take([1])
    aneg = A.take([1])
    cdt = A.take([64])
    convw = A.take([48, 5])
    convb = A.take([48])
    onc = A.take([32])
    P.dma("sp", dtb[:], C.cdtb_d[:, jC:jC + 1], w=[dtb])
    P.dma("sp", aneg[:], C.calog_d[:, jC:jC + 1], w=[aneg])
    P.act(aneg[:], aneg[:], AF.Exp, [aneg], [aneg])
    P.ts("dve", aneg[:], aneg[:], -1.0, ALU.mult, [aneg], [aneg])
    P.dma("sp", cdt[:], C.cd_d[:, jC * 64:(jC + 1) * 64], w=[cdt])
    P.dma("sp", convw[:], C.cconvw_d[:, jC], w=[convw])
    P.dma("sp", convb[:], C.cconvb_d[:, jC], w=[convb])
    P.dma("sp", onc[:], C.conorm_d[:, jC], w=[onc])
    base_a, base_b = A.off, Bf.off

    xp = A.takes([4, W], 2)
    dtl = A.takes([2, W], 2)
    dtm = A.takes([256], 2)
    ee = A.takes([W], 2)
    zt = Bf.takes([512], 3)
    for g in range(NG):
        s = 1 if g == 0 else 0
        c0 = g * W
        P.dma("sp", C.hg[:], src[:, :, c0:c0 + W].rearrange("k p n -> p k n"), r=[C.hbuf[g]], w=[C.hg])
        modulate(C, C.hg, s, 0, 1)
        for blk in range(12):
            wt = C.wb.next()
            wv = load_w(C, wt, 0, C.S_in[i], 32 + blk * 4, 4)
            st = xp.next()
            for jj in range(4):
                ps = C.psA.next()
                for kc in range(KC):
                    P.mm(ps[:, 0:W], wv[:, jj, kc, :], C.uT[:, kc, :], kc == 0, kc == KC - 1,
                         [wt, C.uT], [ps])
                P.cp("act", st[:, jj, :], ps[:, 0:W], [ps], [st])
            P.dma("act", C.XP[blk * 4:(blk + 1) * 4, :, c0:c0 + W].rearrange("c p n -> p c n"), st[:],
                  r=[st], w=[C.xpbuf[g][blk]])
        wt = C.wb.next()
        wv = load_w(C, wt, 0, C.S_in[i], 80, 1)
        ps = C.psA.next()
        for kc in range(KC):
            P.mm(ps[:, 0:W], wv[:, 0, kc, :], C.uT[:, kc, :], kc == 0, kc == KC - 1, [wt, C.uT], [ps])
        e1 = ee.next()
        P.act(e1[:], ps[:, 0:W], AF.Exp, [ps, dtb], [e1], bias=dtb[:, 0:1])
        P.ts("dve", e1[:], e1[:], 1.0, ALU.add, [e1], [e1])
        dl = dtl.next()
        P.act(dl[:, 0, :], e1[:], AF.Ln, [e1], [dl])
        P.ts("dve", dl[:, 1, :], dl[:, 0, :], aneg[:, 0:1], ALU.mult, [dl, aneg], [dl])
        for t in range(2):
            ps = C.psA.next()
            for a in range(2):
                P.mm(ps[:, a * 128:(a + 1) * 128], dl[:, a, t * 128:(t + 1) * 128], C.identf[:], True, True,
                     [dl, C.identf], [ps])
            dm = dtm.next()
            P.cp("act", dm[:], ps[:, 0:256], [ps], [dm])
            P.dma("act", C.DTM[c0 + t * 128:c0 + (t + 1) * 128, :], dm[:], r=[dm], w=[C.dtmbuf[g][t]])
        for nb in range(8):
            wt = C.wb.next()
            wv = load_w(C, wt, 0, C.S_in[i], nb * 4, 4)
            for t in range(2):
                ps = C.psA.next()
                for kc in range(KC):
                    P.mm(ps[:, 0:512].rearrange("p (c n) -> p c n", n=128), C.uT[:, kc, t * 128:(t + 1) * 128],
                         wv[:, :, kc, :], kc == 0, kc == KC - 1, [wt, C.uT], [ps])
                z = zt.next()
                P.act(z[:], ps[:, 0:512], AF.Silu, [ps], [z])
                P.dma("act", C.SZ[c0 + t * 128:c0 + (t + 1) * 128, nb * 512:(nb + 1) * 512], z[:], r=[z],
                      w=[C.szbuf[g][t][nb]])

    P.barrier()
    A.off, Bf.off = base_a, base_b
    xin = A.takes([W + 4], 3)
    acc = A.takes([W], 2)
    tms = Bf.takes([1024], 2)
    for g in range(NG):
        c0 = g * W
        seq_lo = 0 if g == 0 else TCTX
        seq_hi = TCTX if g == 0 else NT
        lo = max(seq_lo, c0 - 2)
        hi = min(seq_hi, c0 + W + 2)
        gl = list(range(max(0, g - 1), min(NG, g + 2)))
        for cc in range(48):
            xi = xin.next()
            if lo > c0 - 2:
                P.memset("pool", xi[:, 0:2], 0.0, [xi])
            if hi < c0 + W + 2:
                P.memset("pool", xi[:, W + 2:W + 4], 0.0, [xi])
            P.dma("sp", xi[:, lo - (c0 - 2):hi - (c0 - 2)], C.XP[cc, :, lo:hi],
                  r=[C.xpbuf[gg][cc // 4] for gg in gl], w=[xi])
            ac = acc.next()
            P.ts("dve", ac[:], xi[:, 0:W], convw[:, cc, 0:1], ALU.mult, [xi, convw], [ac])
            for j in range(1, 5):
                P.stt("dve", ac[:], xi[:, j:j + W], convw[:, cc, j:j + 1], ac[:], ALU.mult, ALU.add,
                      [xi, convw, ac], [ac])
            P.act(C.h1T[:, cc, :], ac[:], AF.Silu, [ac, convb], [C.h1T], bias=convb[:, cc:cc + 1])
        P.dma("act", C.BT[:, :, c0:c0 + W].rearrange("g p n -> p g n"), C.h1T[:, 32:40, :], r=[C.h1T],
              w=[C.btbuf[g]])
        P.dma("act", C.CT[:, :, c0:c0 + W].rearrange("g p n -> p g n"), C.h1T[:, 40:48, :], r=[C.h1T],
              w=[C.ctbuf[g]])
        for t in range(2):
            for b8 in range(5):
                for j in range(8):
                    P.tr(C.psT[:, j * 128:(j + 1) * 128], C.h1T[:, b8 * 8 + j, t * 128:(t + 1) * 128], C.identb[:],
                         [C.h1T, C.identb], [C.psT])
                tm = tms.next()
                P.cp("act", tm[:], C.psT[:], [C.psT], [tm])
                if b8 < 4:
                    P.dma("act", C.XTM[c0 + t * 128:c0 + (t + 1) * 128, b8 * 1024:(b8 + 1) * 1024], tm[:], r=[tm],
                          w=[C.xtbuf[g][t][b8]])
                else:
                    P.dma("act", C.BTM[c0 + t * 128:c0 + (t + 1) * 128, :], tm[:], r=[tm], w=[C.btmbuf[g][t]])

    P.barrier()
    A.off, Bf.off = base_a, base_b
    h1 = C.h1T.t[:].rearrange("p a b -> p (a b)")
    x_tm, xdt, xw = Buf(h1[:, 0:4096]), Buf(h1[:, 4096:8192]), Buf(h1[:, 8192:12288])
    hSb = Buf(h1[:, 12288:16384].rearrange("p (g n) -> p g n", g=8))
    hS = A.take([8, 512])
    seg = A.takes([1024], 2)
    cbm = A.takes([128], 2)
    ysb = A.takes([512], 2)
    dtm = A.takes([256], 2)
    sm = {n: A.takes([64], 2) for n in ("cum", "tot", "ecum", "ecend", "fdec", "dtf")}
    btm = Bf.takes([1024], 2)
    bT = Bf.takes([8, 128], 2)
    cT = Bf.takes([8, 128], 2)
    wm = Bf.takes([1024], 2)
    psO0 = Buf(C.psO[0].t)
    psO1 = C.psO[1].t[:].rearrange("p a b -> p (a b)")
    psY, psI = Buf(psO1[:, 0:512]), Buf(psO1[:, 512:1024])
    psS = Buf(C.psS.t[:].rearrange("p a b -> p (a b)"))
    for d in range(2):
        order = list(range(NG)) if d == 0 else [0] + list(range(NG - 1, 0, -1))
        TRI = C.trif if d == 0 else C.trib
        TRI = C.tri128[d]
        P.memset("dve", hS[:], 0.0, [hS])
        P.memset("pool", hSb[:], 0.0, [hSb])
        for g in order:
            c0 = g * W
            for t in ([0, 1] if d == 0 else [1, 0]):
                r0 = c0 + t * 128
                P.dma("sp", x_tm[:], C.XTM[r0:r0 + 128, :], r=C.xtbuf[g][t], w=[x_tm])
                bm = btm.next()
                P.dma("sp", bm[:], C.BTM[r0:r0 + 128, :], r=[C.btmbuf[g][t]], w=[bm])
                bt_, ct_ = bT.next(), cT.next()
                P.dma("sp", bt_[:], C.BT[:, :, r0:r0 + 128].rearrange("g p n -> p g n"), r=[C.btbuf[g]], w=[bt_])
                P.dma("sp", ct_[:], C.CT[:, :, r0:r0 + 128].rearrange("g p n -> p g n"), r=[C.ctbuf[g]], w=[ct_])
                dm = dtm.next()
                P.dma("sp", dm[:], C.DTM[r0:r0 + 128, :], r=[C.dtmbuf[g][t]], w=[dm])
                dt_t = dm[:, d * 64:(d + 1) * 64]
                la_t = dm[:, 128 + d * 64:128 + (d + 1) * 64]
                ps = C.psA.next()
                P.mm(ps[:, 0:64], TRI[:], la_t, True, True, [TRI, dm], [ps])
                P.mm(ps[:, 64:128], C.ones[:], la_t, True, True, [C.ones, dm], [ps])
                cum, tot = sm["cum"].next(), sm["tot"].next()
                P.cp("act", cum[:], ps[:, 0:64], [ps], [cum])
                P.cp("act", tot[:], ps[:, 64:128], [ps], [tot])
                ecum, ecend, fdec, dtf = sm["ecum"].next(), sm["ecend"].next(), sm["fdec"].next(), sm["dtf"].next()
                P.act(ecum[:], cum[:], AF.Exp, [cum], [ecum])
                P.act(ecend[:], tot[:], AF.Exp, [tot], [ecend])
                P.tt("dve", fdec[:], tot[:], cum[:], ALU.subtract, [tot, cum], [fdec])
                P.act(fdec[:], fdec[:], AF.Exp, [fdec], [fdec])
                P.tt("dve", dtf[:], fdec[:], dt_t, ALU.mult, [fdec, dm], [dtf])
                x3 = x_tm[:].rearrange("p (h q) -> p h q", q=64)
                P.tt("pool", xdt[:].rearrange("p (h q) -> p h q", q=64), x3,
                     dt_t.unsqueeze(2).to_broadcast([128, 64, 64]), ALU.mult, [x_tm, dm], [xdt])
                P.tt("dve", xw[:].rearrange("p (h q) -> p h q", q=64), x3,
                     dtf[:].unsqueeze(2).to_broadcast([128, 64, 64]), ALU.mult, [x_tm, dtf], [xw])
                for gi in range(8):
                    psc = C.psA.next()
                    P.mm(psc[:, 0:128], bt_[:, gi, :], ct_[:, gi, :], True, True, [bt_, ct_], [psc])
                    cb = cbm.next()
                    P.tt("dve", cb[:], psc[:, 0:128], TRI[:], ALU.mult, [psc, TRI], [cb])
                    for e in range(8):
                        hh = gi * 8 + e
                        P.mm(psO0[:, e, :], la_t[:, hh:hh + 1].to_broadcast([128, 128]), TRI[:], True, True,
                             [dm, TRI], [psO0])
                    sg1 = seg.next()
                    P.tt("dve", sg1[:].rearrange("p (h q) -> p h q", q=128), psO0[:],
                         cum[:, gi * 8:(gi + 1) * 8].unsqueeze(2).to_broadcast([128, 8, 128]), ALU.subtract,
                         [psO0, cum], [sg1])
                    P.ts("pool", sg1[:], sg1[:], 0.0, ALU.min, [sg1], [sg1])
                    P.act(sg1[:], sg1[:], AF.Exp, [sg1], [sg1])
                    w_ = wm.next()
                    P.tt("pool", w_[:].rearrange("p (h q) -> p h q", q=128),
                         sg1[:].rearrange("p (h q) -> p h q", q=128),
                         cb[:].unsqueeze(1).to_broadcast([128, 8, 128]), ALU.mult, [sg1, cb], [w_])
                    for e in range(8):
                        hh = gi * 8 + e
                        P.mm(psY[:, e * 64:(e + 1) * 64], w_[:, e * 128:(e + 1) * 128], xdt[:, hh * 64:(hh + 1) * 64],
                             e == 0, True, [w_, xdt], [psY], skip=True)
                    P.mm(psI[:], ct_[:, gi, :], hSb[:, gi, :], True, True, [ct_, hSb], [psI])
                    yb_ = ysb.next()
                    P.tt("dve", yb_[:].rearrange("p (h q) -> p h q", q=64),
                         psI[:].rearrange("p (h q) -> p h q", q=64),
                         ecum[:, gi * 8:(gi + 1) * 8].unsqueeze(2).to_broadcast([128, 8, 64]), ALU.mult,
                         [psI, ecum], [yb_])
                    P.tt("dve", yb_[:], yb_[:], psY[:], ALU.add, [yb_, psY], [yb_])
                    P.dma("act", C.Y[d][r0:r0 + 128, gi * 512:(gi + 1) * 512], yb_[:], r=[yb_],
                          w=[C.ybuf[d][g][t][gi]])
                    P.mm(psS[:], bm[:, gi * 128:(gi + 1) * 128], xw[:, gi * 512:(gi + 1) * 512], True, True,
                         [bm, xw], [psS])
                    P.tt("pool", hS[:, gi, :].rearrange("p (h q) -> p h q", q=64),
                         hS[:, gi, :].rearrange("p (h q) -> p h q", q=64),
                         ecend[:, gi * 8:(gi + 1) * 8].unsqueeze(2).to_broadcast([128, 8, 64]), ALU.mult,
                         [hS, ecend], [hS])
                    P.tt("dve", hS[:, gi, :], hS[:, gi, :], psS[:], ALU.add, [hS, psS], [hS])
                    P.cp("pool", hSb[:, gi, :], hS[:, gi, :], [hS], [hSb])

    P.barrier()
    A.off, Bf.off = base_a, base_b
    y0q = A.takes([1024], 2)
    y1q = A.takes([1024], 2)
    tq = A.takes([1024], 2)
    ss = A.takes([8], 2)
    xq = Bf.takes([1024], 2)
    szq = Bf.takes([1024], 2)
    sqt = Bf.takes([512], 2)
    yyb = Bf.take([4096])
    yy2 = Bf.take([4096])
    for g in range(NG):
        if g == 0 and not emit_ctx:
            continue
        s = 1 if g == 0 else 0
        c0 = g * W
        for t in range(2):
            r0 = c0 + t * 128
            ssv = ss.next()
            for q in range(4):
                cs = slice(q * 1024, (q + 1) * 1024)
                y0, y1, xq_, sz_ = y0q.next(), y1q.next(), xq.next(), szq.next()
                P.dma("sp", y0[:], C.Y[0][r0:r0 + 128, cs], r=C.ybuf[0][g][t][2 * q:2 * q + 2], w=[y0])
                P.dma("sp", y1[:], C.Y[1][r0:r0 + 128, cs], r=C.ybuf[1][g][t][2 * q:2 * q + 2], w=[y1])
                P.dma("sp", xq_[:], C.XTM[r0:r0 + 128, cs], r=[C.xtbuf[g][t][q]], w=[xq_])
                P.dma("sp", sz_[:], C.SZ[r0:r0 + 128, cs], r=C.szbuf[g][t][2 * q:2 * q + 2], w=[sz_])
                P.tt("dve", y0[:], y0[:], y1[:], ALU.add, [y0, y1], [y0])
                tt_ = tq.next()
                P.tt("pool", tt_[:].rearrange("p (h q) -> p h q", q=64), xq_[:].rearrange("p (h q) -> p h q", q=64),
                     cdt[:, q * 16:(q + 1) * 16].unsqueeze(2).to_broadcast([128, 16, 64]), ALU.mult,
                     [xq_, cdt], [tt_])
                P.tt("dve", y0[:], y0[:], tt_[:], ALU.add, [y0, tt_], [y0])
                P.tt("dve", yyb[:, cs], y0[:], sz_[:], ALU.mult, [y0, sz_], [yyb])
                for gg in range(2):
                    sq_ = sqt.next()
                    gcs = slice(q * 1024 + gg * 512, q * 1024 + (gg + 1) * 512)
                    P.act(sq_[:], yyb[:, gcs], AF.Square, [yyb], [sq_])
                    P.op("dve", (lambda o_, i_: (lambda e_: e_.reduce_sum(out=o_, in_=i_, axis=mybir.AxisListType.X)))(
                        ssv[:, 2 * q + gg:2 * q + gg + 1], sq_[:]), [sq_], [ssv])
            P.ts("dve", ssv[:], ssv[:], 1.0 / 512.0, ALU.mult, [ssv], [ssv], s2=EPS, op1=ALU.add)
            P.act(ssv[:], ssv[:], AF.Ln, [ssv], [ssv])
            P.act(ssv[:], ssv[:], AF.Exp, [ssv], [ssv], scale=-0.5)
            for gi in range(8):
                P.ts("pool", yy2[:, gi * 512:(gi + 1) * 512], yyb[:, gi * 512:(gi + 1) * 512], ssv[:, gi:gi + 1],
                     ALU.mult, [yyb, ssv], [yy2])
            for b8 in range(4):
                for j in range(8):
                    cc = b8 * 8 + j
                    P.tr(C.psT[:, j * 128:(j + 1) * 128], yy2[:, cc * 128:(cc + 1) * 128], C.identb[:],
                         [yy2, C.identb], [C.psT])
                P.tt("dve", C.h1T[:, b8 * 8:(b8 + 1) * 8, t * 128:(t + 1) * 128],
                     C.psT[:].rearrange("p (c n) -> p c n", n=128),
                     onc[:, b8 * 8:(b8 + 1) * 8].unsqueeze(2).to_broadcast([128, 8, 128]), ALU.mult,
                     [C.psT, onc], [C.h1T])
        C.uTx = C.h1T
        layer_tail(C, i, g, s, src, wout, last, w1_d, w2_d, nkc=32)


def hgrn2_layer(C, i, jA, src, last, emit_ctx, w1_d, w2_d):
    P = C.P
    NG = C.NG
    win = C.awin_d[jA]
    wout = C.awout_d[jA]
    P.barrier()
    C.arf.reset()
    C.arb.reset()
    C.e1 = {n: C.arf.takes([W], 2) for n in
            ("qt", "sg", "f", "lf", "k", "pfx", "g", "eg", "egn", "kgf")}
    C.decst = C.arf.take([2, KC, W // CH])
    C.sDEC = C.arf.takes([KC, W // CH], 2)
    C.S = [C.arf.take([8, 128]) for h in range(2)]
    C.osb = C.arf.takes([8, 128], 1)
    C.st8 = C.arb.takes([8, W], 2)
    C.sQ, C.sK, C.sKE, C.sV = Buf(), Buf(), Buf(), Buf()
    C.ketm = C.arb.takes([8 * 128], 2)
    C.vtm = C.arb.takes([8 * 128], 2)
    C.vm = C.arb.takes([8 * 128], 3)
    C.am = C.arb.takes([512], 2)
    C.Sb = [C.arb.take([8, 128]) for h in range(2)]
    E = C.e1
    for g in range(NG):
        s = 1 if g == 0 else 0
        c0 = g * W
        P.dma("sp", C.hg[:], src[:, :, c0:c0 + W].rearrange("k p n -> p k n"), r=[C.hbuf[g]], w=[C.hg])
        modulate(C, C.hg, s, 0, 1)
        for h in range(KC):
            wa = C.wb.next()
            load_w(C, wa, 0, C.S_in[i], 0 * 16 + h, 1)
            load_w(C, wa, 1, C.S_in[i], 3 * 16 + h, 1)
            load_w(C, wa, 2, C.S_in[i], 4 * 16 + h, 1)
            wav = load_w(C, wa, 3, C.S_in[i], 2 * 16 + h, 1)
            wg = C.wb.next()
            wgv = load_w(C, wg, 0, C.S_in[i], 1 * 16 + h, 1)
            st = C.st8.next()

            def proj(wbuf, wview, slot):
                ps = C.psA.next()
                for kc in range(KC):
                    P.mm(ps[:, 0:W], wview[:, slot, kc, :], C.uT[:, kc, :], kc == 0, kc == KC - 1,
                         [wbuf, C.uT], [ps])
                return ps
            psq = proj(wa, wav, 0)
            qt = E["qt"].next()
            P.act(qt[:], psq[:, 0:W], AF.Silu, [psq], [qt])
            Dv = [dict(), dict()]

            def s0(d, v):
                v["psf"] = proj(wa, wav, 1 + d)
                v["sg"] = E["sg"].next()
                P.act(v["sg"][:], v["psf"][:, 0:W], AF.Sigmoid, [v["psf"]], [v["sg"]])

            def s1(d, v):
                v["f"] = E["f"].next()
                P.ts("dve", v["f"][:], v["sg"][:], C.oml[:, i, d, h:h + 1], ALU.mult, [v["sg"], C.oml, C.lb],
                     [v["f"]], s2=C.lb[:, i, d, h:h + 1], op1=ALU.add)

            def s2(d, v):
                v["lf"] = E["lf"].next()
                P.act(v["lf"][:], v["f"][:], AF.Ln, [v["f"]], [v["lf"]])
                v["k"] = E["k"].next()
                P.ts("pool", v["k"][:], v["f"][:], -1.0, ALU.mult, [v["f"]], [v["k"]], s2=1.0, op1=ALU.add)

            def s3(d, v):
                pfx = E["pfx"].next()
                v["pfx"] = pfx
                P.scan(pfx[:], C.mskr[:], v["lf"][:], [C.mskr, v["lf"]], [pfx])
                pfx3 = pfx[:].rearrange("p (c k) -> p c k", k=CH)
                v["pfx3"] = pfx3
                tot_b = pfx3[:, :, CH - 1:CH].to_broadcast([128, W // CH, CH])
                if d == 0:
                    v["G"] = pfx
                else:
                    G = E["g"].next()
                    v["G"] = G
                    P.tt("dve", G[:], v["lf"][:], pfx[:], ALU.subtract, [v["lf"], pfx], [G])
                    P.tt("dve", G[:].rearrange("p (c k) -> p c k", k=CH),
                         G[:].rearrange("p (c k) -> p c k", k=CH), tot_b, ALU.add, [G, pfx], [G])

            def s4(d, v):
                G = v["G"]
                v["eg"] = E["eg"].next()
                P.act(v["eg"][:], G[:], AF.Exp, [G], [v["eg"]])
                v["egn"] = E["egn"].next()
                P.act(v["egn"][:], G[:], AF.Exp, [G], [v["egn"]], scale=-1.0)
                P.act(C.decst[:, d, h, :], v["pfx3"][:, :, CH - 1], AF.Exp, [v["pfx"]], [C.decst])

            def s5(d, v):
                P.tt("pool", st[:, 3 * d + 0, :], qt[:], v["eg"][:], ALU.mult, [qt, v["eg"]], [st])
                v["kgf"] = E["kgf"].next()
                P.tt("dve", v["kgf"][:], v["k"][:], v["egn"][:], ALU.mult, [v["k"], v["egn"]], [v["kgf"]])

            def s6(d, v):
                kgf = v["kgf"]
                P.cp("pool", st[:, 3 * d + 1, :], kgf[:], [kgf], [st])
                P.tt("dve", st[:, 3 * d + 2, :].rearrange("p (c k) -> p c k", k=CH),
                     kgf[:].rearrange("p (c k) -> p c k", k=CH),
                     C.decst[:, d, h, :].unsqueeze(2).to_broadcast([128, W // CH, CH]), ALU.mult,
                     [kgf, C.decst], [st])

            for step in (s0, s1, s2, s3, s4, s5, s6):
                for d in range(2):
                    step(d, Dv[d])
            psv = proj(wa, wav, 3)
            P.cp("act", st[:, 6, :], psv[:, 0:W], [psv], [st])
            psg = proj(wg, wgv, 0)
            P.act(st[:, 7, :], psg[:, 0:W], AF.Silu, [psg], [st])
            for ty in range(8):
                P.dma("act", C.SCR[ty][h, :, c0:c0 + W], st[:, ty, :], r=[st], w=[C.scrbuf[g][ty][h]])
        P.dma("act", C.DEC[:, :, :, g * (W // CH):(g + 1) * (W // CH)], C.decst[:], r=[C.decst], w=[C.decbuf[g]])

    if os.environ.get("KSTOP") == "ph1":
        return
    P.barrier()
    h1 = C.h1T.t
    sQ, sK, sKE, sV = C.sQ, C.sK, C.sKE, C.sV
    sQ.t, sK.t, sKE.t, sV.t = h1[:, 0:16, :], h1[:, 16:32, :], h1[:, 32:48, :], h1[:, 48:64, :]
    for d in range(2):
        order = list(range(NG)) if d == 0 else [0] + list(range(NG - 1, 0, -1))
        for hh in range(2):
            P.memset("dve", C.S[hh][:], 0.0, [C.S[hh]])
            P.memset("pool", C.Sb[hh][:], 0.0, [C.Sb[hh]])
        mask = C.trif if d == 0 else C.trib
        for g in order:
            c0 = g * W
            sDEC = C.sDEC.next()
            for buf, ty in ((sQ, 3 * d + 0), (sK, 3 * d + 1), (sKE, 3 * d + 2), (sV, 6)):
                P.dma("sp", buf.t, C.SCR[ty][:, :, c0:c0 + W].rearrange("h p n -> p h n"),
                      r=C.scrbuf[g][ty], w=[buf])
            P.dma("sp", sDEC[:], C.DEC[:, d, :, g * (W // CH):(g + 1) * (W // CH)], r=[C.decbuf[g]], w=[sDEC])
            tiles = [0, 1] if d == 0 else [1, 0]
            for t in tiles:
                tc0 = t * 128
                chunks = list(range(NCH)) if d == 0 else list(range(NCH - 1, -1, -1))
                ketm = [None, None]
                vms = [None, None]
                vtms = [None, None]
                for hh in range(2):
                    for hq in range(8):
                        h = hh * 8 + hq
                        P.tr(C.psT[:, hq * 128:(hq + 1) * 128], sKE[:, h, tc0:tc0 + 128], C.identb[:],
                             [sKE, C.identb], [C.psT])
                    ketm[hh] = C.ketm.next()
                    P.cp("act", ketm[hh][:], C.psT[:], [C.psT], [ketm[hh]])
                    for hq in range(8):
                        h = hh * 8 + hq
                        P.tr(C.psT[:, hq * 128:(hq + 1) * 128], sV[:, h, tc0:tc0 + 128], C.identb[:],
                             [sV, C.identb], [C.psT])
                    vtms[hh] = C.vtm.next()
                    P.cp("act", vtms[hh][:], C.psT[:], [C.psT], [vtms[hh]])
                    for q4 in range(2):
                        psa = C.psA.next()
                        for j in range(4):
                            h = hh * 8 + q4 * 4 + j
                            P.mm(psa[:, j * 128:(j + 1) * 128], sK[:, h, tc0:tc0 + 128], sQ[:, h, tc0:tc0 + 128],
                                 True, True, [sK, sQ], [psa])
                        am = C.am.next()
                        P.tt("dve", am[:], psa[:], mask[:], ALU.mult, [psa, mask], [am])
                        for j in range(4):
                            hq = q4 * 4 + j
                            P.mm(C.psO[hh][:, hq, :], vtms[hh][:, hq * 128:(hq + 1) * 128],
                                 am[:, j * 128:(j + 1) * 128], j == 0, False, [vtms[hh], am], [C.psO[hh]],
                                 skip=True)
                for c in chunks:
                    for hh in range(2):
                        for hq in range(8):
                            h = hh * 8 + hq
                            P.mm(C.psO[hh][:, hq, c * CH:(c + 1) * CH], C.Sb[hh][:, hq, :],
                                 sQ[:, h, tc0 + c * CH:tc0 + (c + 1) * CH], False, True,
                                 [C.Sb[hh], sQ], [C.psO[hh]], skip=True)
                        ci = t * NCH + c
                        vm = C.vm.next()
                        P.ts("pool", vm[:], vtms[hh][:], C.rowm[:, c:c + 1], ALU.mult, [vtms[hh], C.rowm], [vm])
                        P.tt("dve", C.S[hh][:], C.S[hh][:],
                             sDEC[:, hh * 8:(hh + 1) * 8, ci].unsqueeze(2).to_broadcast([128, 8, 128]),
                             ALU.mult, [C.S[hh], sDEC], [C.S[hh]])
                        for q4 in range(2):
                            for j in range(4):
                                hq = q4 * 4 + j
                                P.mm(C.psS[:, j, :], ketm[hh][:, hq * 128:(hq + 1) * 128],
                                     vm[:, hq * 128:(hq + 1) * 128], True, True,
                                     [ketm[hh], vm], [C.psS])
                            P.tt("dve", C.S[hh][:, q4 * 4:(q4 + 1) * 4, :], C.S[hh][:, q4 * 4:(q4 + 1) * 4, :],
                                 C.psS[:], ALU.add, [C.S[hh], C.psS], [C.S[hh]])
                        P.cp("pool", C.Sb[hh][:], C.S[hh][:], [C.S[hh]], [C.Sb[hh]])
                for hh in range(2):
                    ob = C.osb.next()
                    P.cp("act", ob[:], C.psO[hh][:], [C.psO[hh]], [ob])
                    P.dma("act", C.OSC[d, hh * 8:(hh + 1) * 8, :, c0 + tc0:c0 + tc0 + 128].rearrange("h p n -> p h n"),
                          ob[:], r=[ob], w=[C.obuf[d][g][t * 2 + hh]])

    if os.environ.get("KSTOP") == "ph3":
        return
    P.barrier()
    for g in range(NG):
        if g == 0 and not emit_ctx:
            continue
        s = 1 if g == 0 else 0
        c0 = g * W
        P.dma("sp", C.yb[:], C.OSC[0, :, :, c0:c0 + W].rearrange("h p n -> p h n"), r=C.obuf[0][g], w=[C.yb])
        P.dma("sp", C.hg[:], C.OSC[1, :, :, c0:c0 + W].rearrange("h p n -> p h n"), r=C.obuf[1][g], w=[C.hg])
        sgbs = []
        for hh in range(2):
            sgb = C.st8.next()
            P.dma("sp", sgb[:], C.SCR[7][hh * 8:(hh + 1) * 8, :, c0:c0 + W].rearrange("h p n -> p h n"),
                  r=C.scrbuf[g][7], w=[sgb])
            sgbs.append(sgb)
        for h in range(KC):
            sgb = sgbs[h // 8]
            P.tt("dve", C.yb[:, h, :], C.yb[:, h, :], C.hg[:, h, :], ALU.add, [C.yb, C.hg], [C.yb])
            sq = C.sq.next()
            P.act(sq[:], C.yb[:, h, :], AF.Square, [C.yb], [sq])
            psn = C.psA.next()
            P.mm(psn[:, 0:W], C.ones[:], sq[:], True, True, [C.ones, sq], [psn])
            rs = C.rs.next()
            P.ts("dve", rs[:], psn[:, 0:W], 128.0 * EPS, ALU.add, [psn], [rs])
            P.act(rs[:], rs[:], AF.Ln, [rs], [rs])
            P.act(rs[:], rs[:], AF.Exp, [rs], [rs], scale=-0.5)
            t = C.tmpf.next()
            P.tt("dve", t[:], C.yb[:, h, :], rs[:], ALU.mult, [C.yb, rs], [t])
            P.stt("dve", C.uT[:, h, :], t[:], C.aon[:, jA, h:h + 1], sgb[:, h % 8, :], ALU.mult, ALU.mult,
                  [t, C.aon, sgb], [C.uT])
        C.uTx = C.uT
        layer_tail(C, i, g, s, src, wout, last, w1_d, w2_d)


def host_consts():
    c = np.zeros((128, 128 + W + 1024 + NCH), np.float32)
    c[:, 0:128] = np.eye(128, dtype=np.float32)
    m = np.ones((128, W), np.float32)
    m[:, 0::CH] = 0.0
    c[:, 128:128 + W] = m
    s = np.arange(128)[:, None]
    t = np.arange(128)[None, :]
    same = (s // CH) == (t // CH)
    trif = (same & (s <= t)).astype(np.float32)
    trib_ = (same & (s >= t)).astype(np.float32)
    c[:, 128 + W:128 + W + 512] = np.tile(trif, (1, 4))
    c[:, 128 + W + 512:128 + W + 1024] = np.tile(trib_, (1, 4))
    for cc in range(NCH):
        c[cc * CH:(cc + 1) * CH, 128 + W + 1024 + cc] = 1.0
    return c, np.tile(trib_, (1, 4)).astype(np.float32)


def colmajor(v):
    v = np.asarray(v, np.float32)
    lead = v.shape[:-1]
    a = v.reshape(*lead, KC, 128)
    a = np.moveaxis(a, -1, 0)
    return np.ascontiguousarray(a)


def make_inputs(b, x, c, ctx, c_ctx, ada_w, ada_b, norm_g, mlp_w1, mlp_w2, a_w_in, a_lb_logits, a_onorm,
                a_w_out, kinds, b_w_qkv=None, b_sink=None, b_w_out=None, c_w_in=None, c_conv_w=None,
                c_conv_b=None, c_dt_bias=None, c_a_log=None, c_d=None, c_onorm=None, c_w_out=None):
    depth = len(kinds)
    TL = x.shape[1]
    seq = np.concatenate([ctx[b], x[b]], axis=0)
    hT0 = np.ascontiguousarray(seq.T.reshape(KC, 128, seq.shape[0]))
    ccol = np.stack([colmajor(c[b]), colmajor(c_ctx)], axis=-1)
    adab = np.ascontiguousarray(np.moveaxis(np.asarray(ada_b[:depth]).reshape(depth, 96, 128), -1, 0))
    ng = colmajor(norm_g[:depth])
    consts, trib_ = host_consts()
    m = {
        "hT0": hT0, "ccol": np.ascontiguousarray(ccol), "ada_w": np.ascontiguousarray(ada_w[:depth]),
        "adab": adab, "ng": ng, "mlp_w1": np.ascontiguousarray(mlp_w1[:depth]),
        "mlp_w2": np.ascontiguousarray(mlp_w2[:depth]), "consts": consts, "trib": trib_,
    }
    nA = sum(1 for k in kinds if k == "A")
    if nA:
        m["lbl"] = colmajor(a_lb_logits)
        m["aon"] = colmajor(a_onorm[:nA])
        m["a_w_in"] = np.ascontiguousarray(a_w_in[:nA])
        m["a_w_out"] = np.ascontiguousarray(a_w_out[:nA])
    nB = sum(1 for k in kinds if k == "B")
    if nB:
        m["b_w_qkv"] = np.ascontiguousarray(b_w_qkv[:nB])
        m["b_w_out"] = np.ascontiguousarray(b_w_out[:nB])
        m["bsink"] = np.ascontiguousarray(np.broadcast_to(np.asarray(b_sink[:nB], np.float32).reshape(1, -1),
                                                          (128, nB * 16)))
        m["swac"], m["ropec"], m["ropes"] = swa_consts(TL)
    nC = sum(1 for k in kinds if k == "C")
    if nC:
        m["c_w_in"] = np.ascontiguousarray(c_w_in[:nC])
        m["c_w_out"] = np.ascontiguousarray(c_w_out[:nC])
        m["cdtb"] = np.ascontiguousarray(np.asarray(c_dt_bias[:nC], np.float32).reshape(nC, 128).T)
        m["calog"] = np.ascontiguousarray(np.asarray(c_a_log[:nC], np.float32).reshape(nC, 128).T)
        m["cdrep"] = np.ascontiguousarray(np.broadcast_to(np.asarray(c_d[:nC], np.float32).reshape(1, -1),
                                                          (128, nC * 64)))
        cw = np.asarray(c_conv_w[:nC], np.float32).reshape(nC, 5, 48, 128)
        m["cconvw"] = np.ascontiguousarray(cw.transpose(3, 0, 2, 1))
        cb = np.asarray(c_conv_b[:nC], np.float32).reshape(nC, 48, 128)
        m["cconvb"] = np.ascontiguousarray(cb.transpose(2, 0, 1))
        on = np.asarray(c_onorm[:nC], np.float32).reshape(nC, 32, 128)
        m["conorm"] = np.ascontiguousarray(on.transpose(2, 0, 1))
        si = np.arange(128)[:, None]
        ti = np.arange(128)[None, :]
        m["tri128"] = np.ascontiguousarray(np.concatenate([(si <= ti), (si >= ti)], axis=1).astype(np.float32))
    return m


def swa_consts(TL):
    c = np.zeros((128, 128 + 1024), np.float32)
    pm = np.zeros((128, 128), np.float32)
    for p in range(128):
        if p % 64 < 32:
            pm[p, p + 32] = -1.0
        else:
            pm[p, p - 32] = 1.0
    c[:, 0:128] = pm.T
    sidx = np.arange(128)[:, None]
    qidx = np.arange(128)[None, :]
    c[:, 128:640] = np.tile((sidx >= qidx).astype(np.float32), (1, 4))
    c[:, 640:1152] = np.tile((sidx <= qidx).astype(np.float32), (1, 4))
    inv = (10000.0 ** (-np.arange(32, dtype=np.float32) / 32.0)).astype(np.float32)
    tpos = np.arange(TL)
    row = (tpos // 64).astype(np.float32)
    col = (tpos % 64).astype(np.float32)
    ang = np.zeros((128, TL), np.float32)
    for p in range(128):
        base = row if p < 64 else col
        ang[p] = base * inv[p % 32]
    return c, np.cos(ang).astype(np.float32), np.sin(ang).astype(np.float32)


def run(inputs, kinds, ncores):
    x = np.asarray(inputs["x"], np.float32)
    B, TL, _ = x.shape
    nc = build(TL, kinds)
    args = {k: np.asarray(v, np.float32) for k, v in inputs.items()}
    in_maps = [make_inputs(b % B, kinds=kinds, **args) for b in range(ncores)]
    res = run_bass_kernel_spmd(nc, in_maps, core_ids=list(range(ncores)))
    out = np.empty((B, TL, D), np.float32)
    for b in range(B):
        o = res.results[b]["outT"]
        out[b] = o.reshape(D, TL).T
    return out


def kernel(**inputs):
    kinds = ["A", "B", "C", "A"]
    return run(inputs, kinds, 4)
```

```python
import math
import os
import numpy as np
from contextlib import ExitStack
import concourse.bass as bass
import concourse.mybir as mybir
from concourse.bass_utils import run_bass_kernel_spmd

F32 = mybir.dt.float32
BF16 = mybir.dt.bfloat16
AF = mybir.ActivationFunctionType
ALU = mybir.AluOpType

D = 2048
KC = 16
DFF = 8192
TCTX = 256
W = 256
EPS = 1e-6
KR = 16
CH = 16
NCH = 128 // CH


class Buf:
    __slots__ = ("t", "wtok", "readers")

    def __init__(self, t=None):
        self.t = t
        self.wtok = None
        self.readers = {}

    def __getitem__(self, k):
        return self.t[k]


class Rot:
    def __init__(self, bufs):
        self.bufs = bufs
        self.i = 0

    def next(self):
        b = self.bufs[self.i % len(self.bufs)]
        self.i += 1
        return b


class Arena:
    def __init__(self, P, name, n, dt):
        self.buf = P.sb(name, [128, n], dt)
        self.n = n
        self.off = 0

    def reset(self):
        self.off = 0

    def take(self, shape):
        n = int(np.prod(shape))
        assert self.off + n <= self.n, (self.off, n, self.n)
        ap = self.buf.t[:, self.off:self.off + n]
        self.off += n
        if len(shape) == 2:
            ap = ap.rearrange("p (a b) -> p a b", b=shape[1])
        elif len(shape) == 3:
            ap = ap.rearrange("p (a b c) -> p a b c", b=shape[1], c=shape[2])
        return Buf(ap)

    def takes(self, shape, n):
        return Rot([self.take(shape) for _ in range(n)])


class Prog:
    ENG = ("pe", "act", "dve", "pool", "sp")
    DMAQ = ("sp", "act", "pool")

    def __init__(self, nc, es):
        self.nc = nc
        self.es = es
        self.semh = {}
        for k in self.ENG:
            self.semh[k] = es.enter_context(nc.semaphore("s_" + k))
        for q in self.DMAQ:
            for i in range(KR):
                self.semh[("d", q, i)] = es.enter_context(nc.semaphore(f"d_{q}_{i}"))
        self.cnt = {k: 0 for k in self.ENG}
        self.dcnt = {q: 0 for q in self.DMAQ}
        self.seen = {k: {} for k in self.ENG}
        self.ops = {k: [] for k in self.ENG}
        self.nuniq = 0

    def sb(self, name, shape, dt):
        self.nuniq += 1
        return Buf(self.nc.alloc_sbuf_tensor(f"sb_{name}_{self.nuniq}", list(shape), dt))

    def sbs(self, name, shape, dt, n):
        return Rot([self.sb(name, shape, dt) for _ in range(n)])

    def ps(self, name, shape, dt):
        self.nuniq += 1
        return Buf(self.nc.alloc_psum_tensor(f"ps_{name}_{self.nuniq}", list(shape), dt))

    def _deps(self, eng, r, w):
        need = {}
        for b in r:
            if b.wtok is not None:
                k, v = b.wtok
                if need.get(k, 0) < v:
                    need[k] = v
        for b in w:
            if b.wtok is not None:
                k, v = b.wtok
                if need.get(k, 0) < v:
                    need[k] = v
            for k, v in b.readers.items():
                if need.get(k, 0) < v:
                    need[k] = v
        seen = self.seen[eng]
        waits = []
        for k, v in need.items():
            if k == eng and eng == "pe":
                continue
            if seen.get(k, 0) >= v:
                continue
            seen[k] = v
            waits.append((k, v))
        return waits

    def op(self, eng, fn, r=(), w=()):
        waits = self._deps(eng, r, w)
        self.cnt[eng] += 1
        tok = (eng, self.cnt[eng])
        self.ops[eng].append((waits, fn, (eng, 1)))
        for b in r:
            if b.readers.get(eng, 0) < tok[1]:
                b.readers[eng] = tok[1]
        for b in w:
            b.wtok = tok
            b.readers = {}
        return tok

    def dma(self, q, out, in_, r=(), w=(), **kw):
        i = self.dcnt[q]
        self.dcnt[q] += 1
        sk = ("d", q, i % KR)
        val = 16 * (i // KR + 1)
        waits = self._deps(q, r, w)
        if i >= KR and self.seen[q].get(sk, 0) < val - 16:
            self.seen[q][sk] = val - 16
            waits.append((sk, val - 16))
        self.ops[q].append((waits, (lambda e: e.dma_start(out=out, in_=in_, **kw)), (sk, 16)))
        tok = (sk, val)
        for b in r:
            if b.readers.get(sk, 0) < val:
                b.readers[sk] = val
        for b in w:
            b.wtok = tok
            b.readers = {}
        return tok

    def barrier(self):
        toks = [(k, self.cnt[k]) for k in ("pe", "act", "dve", "pool") if self.cnt[k] > 0]
        for q in self.DMAQ:
            n = self.dcnt[q]
            for j in range(KR):
                if n > j:
                    last = ((n - 1 - j) // KR) * KR + j
                    toks.append((("d", q, j), 16 * (last // KR + 1)))
        for eng in self.ENG:
            seen = self.seen[eng]
            waits = []
            for k, v in toks:
                if seen.get(k, 0) < v:
                    seen[k] = v
                    waits.append((k, v))
            if waits:
                self.ops[eng].append((waits, None, None))

    def finish(self):
        waits = []
        for q in self.DMAQ:
            n = self.dcnt[q]
            for j in range(KR):
                if n > j:
                    last = ((n - 1 - j) // KR) * KR + j
                    waits.append((("d", q, j), 16 * (last // KR + 1)))
        for k in ("pe", "act", "dve", "pool"):
            if self.cnt[k] > 0:
                waits.append((k, self.cnt[k]))
        self.ops["sp"].append((waits, None, None))

    def emit(self):
        nc = self.nc
        block = self.es.enter_context(nc.Block())
        semh = self.semh

        def replay(key, e):
            for waits, fn, inc in self.ops[key]:
                if fn is None:
                    for (k, v) in waits:
                        e.wait_ge(semh[k], v)
                    continue
                for (k, v) in waits[1:]:
                    e.wait_ge(semh[k], v)
                ins = fn(e)
                if waits:
                    ins._wait_ge(semh[waits[0][0]], waits[0][1])
                ins.then_inc(semh[inc[0]], inc[1])

        @block.tensor
        def _(e):
            replay("pe", e)

        @block.scalar
        def _(e):
            replay("act", e)

        @block.vector
        def _(e):
            replay("dve", e)

        @block.gpsimd
        def _(e):
            replay("pool", e)

        @block.sync
        def _(e):
            replay("sp", e)

    def tt(self, eng, out, in0, in1, op, r, w):
        return self.op(eng, lambda e: e.tensor_tensor(out=out, in0=in0, in1=in1, op=op), r, w)

    def ts(self, eng, out, in0, s1, op0, r, w, s2=None, op1=None):
        if op1 is None:
            return self.op(eng, lambda e: e.tensor_scalar(out=out, in0=in0, scalar1=s1, scalar2=None, op0=op0), r, w)
        return self.op(eng, lambda e: e.tensor_scalar(out=out, in0=in0, scalar1=s1, scalar2=s2, op0=op0, op1=op1), r, w)

    def stt(self, eng, out, in0, scalar, in1, op0, op1, r, w):
        return self.op(eng, lambda e: e.scalar_tensor_tensor(out=out, in0=in0, scalar=scalar, in1=in1, op0=op0, op1=op1), r, w)

    def act(self, out, in_, func, r, w, scale=None, bias=None):
        kw = {}
        if scale is not None:
            kw["scale"] = scale
        if bias is not None:
            kw["bias"] = bias
        return self.op("act", lambda e: e.activation(out=out, in_=in_, func=func, **kw), r, w)

    def cp(self, eng, out, in_, r, w):
        if eng == "act":
            return self.op("act", lambda e: e.copy(out=out, in_=in_), r, w)
        return self.op(eng, lambda e: e.tensor_copy(out=out, in_=in_), r, w)

    def mm(self, out, lhsT, rhs, start, stop, r, w, skip=False):
        if skip:
            return self.op("pe", lambda e: e.matmul(out, lhsT=lhsT, rhs=rhs, start=start, stop=stop,
                                                    skip_group_check=True), r, w)
        return self.op("pe", lambda e: e.matmul(out, lhsT=lhsT, rhs=rhs, start=start, stop=stop), r, w)

    def tr(self, out, in_, ident, r, w):
        return self.op("pe", lambda e: e.transpose(out=out, in_=in_, identity=ident), r, w)

    def memset(self, eng, ap, val, w):
        return self.op(eng, lambda e: e.memset(ap, val), (), w)

    def scan(self, out, d0, d1, r, w):
        return self.op("dve", lambda e: e.tensor_tensor_scan(out=out, data0=d0, data1=d1, initial=0.0,
                                                              op0=ALU.mult, op1=ALU.add), r, w)


class Ctx:
    pass


def build(TL, kinds):
    depth = len(kinds)
    nA = sum(1 for k in kinds if k == "A")
    NT = TCTX + TL
    NG = NT // W
    nc = bass.Bass("TRN2", target_bir_lowering=False)
    es = ExitStack()
    P = Prog(nc, es)
    C = Ctx()
    C.P, C.nc, C.NT, C.NG, C.TL, C.depth = P, nc, NT, NG, TL, depth

    def din(name, shape, dt=F32):
        return nc.dram_tensor(name, list(shape), dt, kind="ExternalInput").ap()

    def dscr(name, shape, dt):
        return nc.dram_tensor(name, list(shape), dt, kind="Internal").ap()

    hT0 = din("hT0", [KC, 128, NT])
    ccol_d = din("ccol", [128, KC, 2])
    adaw_d = din("ada_w", [depth, D, 6 * D])
    adab_d = din("adab", [128, depth, 96])
    ng_d = din("ng", [128, depth, 4, KC])
    w1_d = din("mlp_w1", [depth, D, DFF])
    w2_d = din("mlp_w2", [depth, DFF, D])
    consts_d = din("consts", [128, 128 + W + 1024 + NCH])
    trib_d = din("trib", [128, 512])
    if nA:
        lbl_d = din("lbl", [128, depth, 2, KC])
        aon_d = din("aon", [128, nA, KC])
        awin_d = din("a_w_in", [nA, D, 5 * D])
        awout_d = din("a_w_out", [nA, D, D])
    nB = sum(1 for k in kinds if k == "B")
    if nB:
        C.bwqkv_d = din("b_w_qkv", [nB, D, 3072])
        C.bwout_d = din("b_w_out", [nB, D, D])
        C.bsink_d = din("bsink", [128, nB * 16])
        C.swac_d = din("swac", [128, 128 + 1024])
        C.ropec_d = din("ropec", [128, TL])
        C.ropes_d = din("ropes", [128, TL])
        C.QT = dscr("qT", [KC, 128, NT], BF16)
        C.KT = dscr("kT", [4, 128, NT], BF16)
        C.VTM = dscr("vtm", [NT, 512], BF16)
        C.qbuf = [[Buf() for _ in range(4)] for _ in range(NG)]
        C.kbuf = [Buf() for _ in range(NG)]
        C.vbuf = [[Buf() for _ in range(2)] for _ in range(NG)]
    nC = sum(1 for k in kinds if k == "C")
    if nC:
        C.cwin_d = din("c_w_in", [nC, D, 10368])
        C.cwout_d = din("c_w_out", [nC, 4096, D])
        C.cdtb_d = din("cdtb", [128, nC])
        C.calog_d = din("calog", [128, nC])
        C.cd_d = din("cdrep", [128, nC * 64])
        C.cconvw_d = din("cconvw", [128, nC, 48, 5])
        C.cconvb_d = din("cconvb", [128, nC, 48])
        C.conorm_d = din("conorm", [128, nC, 32])
        tri128_d = din("tri128", [128, 256])
        C.XP = dscr("xp", [48, 128, NT], F32)
        C.SZ = dscr("sz", [NT, 4096], BF16)
        C.DTM = dscr("dtm", [NT, 256], F32)
        C.XTM = dscr("xtm", [NT, 4096], BF16)
        C.BTM = dscr("btm", [NT, 1024], BF16)
        C.BT = dscr("bT", [8, 128, NT], BF16)
        C.CT = dscr("cT", [8, 128, NT], BF16)
        C.Y = [dscr(f"ysc{d_}", [NT, 4096], F32) for d_ in range(2)]
        C.xpbuf = [[Buf() for _ in range(12)] for _ in range(NG)]
        C.dtmbuf = [[Buf() for _ in range(2)] for _ in range(NG)]
        C.szbuf = [[[Buf() for _ in range(8)] for _ in range(2)] for _ in range(NG)]
        C.xtbuf = [[[Buf() for _ in range(4)] for _ in range(2)] for _ in range(NG)]
        C.btmbuf = [[Buf() for _ in range(2)] for _ in range(NG)]
        C.btbuf = [Buf() for _ in range(NG)]
        C.ctbuf = [Buf() for _ in range(NG)]
        C.ybuf = [[[[Buf() for _ in range(8)] for _ in range(2)] for _ in range(NG)] for _ in range(2)]
    outT = nc.dram_tensor("outT", [KC, 128, TL], F32, kind="ExternalOutput").ap()

    hT = dscr("hT", [KC, 128, NT], F32)
    C.hbuf = [Buf() for _ in range(NG)]
    if nA:
        SCR = [dscr(f"scrA{ty}", [KC, 128, NT], BF16) for ty in range(8)]
        DEC = dscr("decA", [128, 2, KC, NT // CH], F32)
        OSC = dscr("oA", [2, KC, 128, NT], F32)
        C.scrbuf = [[[Buf() for _ in range(KC)] for _ in range(8)] for _ in range(NG)]
        C.decbuf = [Buf() for _ in range(NG)]
        C.obuf = [[[Buf() for _ in range(4)] for _ in range(NG)] for _ in range(2)]

    ones = P.sb("ones", [128, 128], F32)
    identf = P.sb("identf", [128, 128], F32)
    identb = P.sb("identb", [128, 128], BF16)
    mskr = P.sb("mskr", [128, W], F32)
    trif = P.sb("trif", [128, 512], F32)
    trib = P.sb("trib", [128, 512], F32)
    rowm = P.sb("rowm", [128, NCH], F32)
    sT = P.sb("sT", [128, KC, 2], F32)
    adab = P.sb("adab", [128, depth, 96], F32)
    ngs = P.sb("ngs", [128, depth, 4, KC], F32)
    mcol = P.sb("mcol", [128, 96, 2], F32)
    mods = P.sb("mods", [128, 2, 6, KC], F32)

    P.memset("pool", ones[:], 1.0, [ones])
    P.dma("sp", identf[:], consts_d[:, 0:128], w=[identf])
    P.dma("sp", mskr[:], consts_d[:, 128:128 + W], w=[mskr])
    P.dma("sp", trif[:], consts_d[:, 128 + W:128 + W + 512], w=[trif])
    P.dma("sp", rowm[:], consts_d[:, 128 + W + 1024:128 + W + 1024 + NCH], w=[rowm])
    P.dma("sp", trib[:], trib_d, w=[trib])
    P.cp("dve", identb[:], identf[:], [identf], [identb])
    P.dma("sp", sT[:], ccol_d, w=[sT])
    P.act(sT[:], sT[:], AF.Silu, [sT], [sT])
    P.dma("sp", adab[:], adab_d, w=[adab])
    P.dma("sp", ngs[:], ng_d, w=[ngs])
    P.ts("dve", ngs[:], ngs[:], math.sqrt(D), ALU.mult, [ngs], [ngs])

    C.hg = P.sb("hg", [128, KC, W], F32)
    C.yb = P.sb("yb", [128, KC, W], F32)
    C.uT = P.sb("uT", [128, KC, W], BF16)
    C.h1T = P.sb("h1T", [128, 64, W], BF16)
    C.wb = P.sbs("wb", [128, KC * 512], BF16, 2)
    C.rs = P.sbs("rs", [128, W], F32, 2)
    C.tmpf = P.sbs("tmpf", [128, W], F32, 4)
    C.sq = P.sbs("sq", [128, W], F32, 3)
    C.psA = Rot([P.ps("psA", [128, 512], F32) for _ in range(2)])
    C.psN = C.psA.bufs[0]
    C.ones, C.identb, C.mskr, C.trif, C.trib, C.rowm = ones, identb, mskr, trif, trib, rowm
    C.mods, C.mcol, C.sT, C.adab, C.ngs = mods, mcol, sT, adab, ngs
    C.hT, C.hT0, C.outT = hT, hT0, outT
    C.wada = P.sbs("wada", [128, 1536], F32, 2)

    if nA:
        lbl = P.sb("lbl", [128, depth, 2, KC], F32)
        lbe = P.sb("lbe", [128, depth, 2, KC], F32)
        lbs = P.sb("lbs", [128, 2, KC], F32)
        C.lb = P.sb("lb", [128, depth, 2, KC], F32)
        C.oml = P.sb("oml", [128, depth, 2, KC], F32)
        C.aon = P.sb("aon", [128, nA, KC], F32)
        P.dma("sp", lbl[:], lbl_d, w=[lbl])
        P.dma("sp", C.aon[:], aon_d, w=[C.aon])
        P.ts("dve", C.aon[:], C.aon[:], math.sqrt(128.0), ALU.mult, [C.aon], [C.aon])
        P.act(lbe[:], lbl[:], AF.Exp, [lbl], [lbe])
        P.cp("dve", lbs[:], lbe[:, 0], [lbe], [lbs])
        for i in range(1, depth):
            P.tt("dve", lbs[:], lbs[:], lbe[:, i], ALU.add, [lbe, lbs], [lbs])
        P.op("dve", lambda e: e.reciprocal(out=lbs[:], in_=lbs[:]), [lbs], [lbs])
        for i in range(depth):
            P.tt("dve", lbe[:, i], lbe[:, i], lbs[:], ALU.mult, [lbe, lbs], [lbe])
        P.memset("dve", C.lb[:, 0], 0.0, [C.lb])
        for i in range(1, depth):
            P.tt("dve", C.lb[:, i], C.lb[:, i - 1], lbe[:, i - 1], ALU.add, [C.lb, lbe], [C.lb])
        P.ts("dve", C.oml[:], C.lb[:], -1.0, ALU.mult, [C.lb], [C.oml], s2=1.0, op1=ALU.add)
        C.SCR, C.DEC, C.OSC = SCR, DEC, OSC
        C.awin_d, C.awout_d = awin_d, awout_d

    C.identf = identf
    if nC:
        t0_ = P.sb("tri128f", [128, 128], F32)
        t1_ = P.sb("tri128b", [128, 128], F32)
        P.dma("sp", t0_[:], tri128_d[:, 0:128], w=[t0_])
        P.dma("sp", t1_[:], tri128_d[:, 128:256], w=[t1_])
        C.tri128 = [t0_, t1_]
    C.arf = Arena(P, "arf", 9216, F32)
    C.arb = Arena(P, "arb", 14336, BF16)
    C.psO = [P.ps(f"psO{h}", [128, 8, 128], F32) for h in range(2)]
    C.psS = P.ps("psS", [128, 4, 128], F32)
    C.psT = P.ps("psT", [128, 1024], BF16)

    C.S_w1, C.S_w2, C.S_in, C.S_out = [], [], [], []
    ja = jb = jc = 0
    for i, kind in enumerate(kinds):
        if kind == "A":
            C.S_in.append(prep_weight(C, f"S_in{i}", awin_d[ja], D, 5 * D))
            C.S_out.append(prep_weight(C, f"S_out{i}", awout_d[ja], D, D))
            ja += 1
        elif kind == "B":
            C.S_in.append(prep_weight(C, f"S_in{i}", C.bwqkv_d[jb], D, 3072))
            C.S_out.append(prep_weight(C, f"S_out{i}", C.bwout_d[jb], D, D))
            jb += 1
        else:
            C.S_in.append(prep_weight(C, f"S_in{i}", C.cwin_d[jc], D, 10368))
            C.S_out.append(prep_weight(C, f"S_out{i}", C.cwout_d[jc], 4096, D))
            jc += 1
        C.S_w1.append(prep_weight(C, f"S_w1{i}", w1_d[i], D, DFF))
        C.S_w2.append(prep_weight(C, f"S_w2{i}", w2_d[i], DFF, D))

    jA = 0
    jB = 0
    jC = 0
    for i, kind in enumerate(kinds):
        emit_ctx = i < depth - 1
        src = hT0 if i == 0 else hT
        last = (i == depth - 1)
        ada_phase(C, i, adaw_d)
        if os.environ.get("KSTOP") == "ada":
            break
        if kind == "A":
            hgrn2_layer(C, i, jA, src, last, emit_ctx, w1_d, w2_d)
            jA += 1
        elif kind == "B":
            swa_layer(C, i, jB, src, last, emit_ctx, w1_d, w2_d)
            jB += 1
        elif kind == "C":
            ssd_layer(C, i, jC, src, last, emit_ctx, w1_d, w2_d)
            jC += 1
        else:
            raise NotImplementedError(kind)
        mlp_phase(C, i, emit_ctx, last, w1_d, w2_d)
    P.finish()
    P.emit()
    return nc


def ada_phase(C, i, adaw_d):
    P = C.P
    for nb in range(8):
        for kc in range(KC):
            wt = C.wada.next()
            P.dma("sp", wt[:], adaw_d[i, kc * 128:(kc + 1) * 128, nb * 1536:(nb + 1) * 1536], w=[wt])
            for j in range(12):
                P.mm(C.psN[:, 2 * j:2 * j + 2], wt[:, j * 128:(j + 1) * 128], C.sT[:, kc, :],
                     kc == 0 and j == 0, kc == KC - 1, [wt, C.sT], [C.psN], skip=True)
        P.tt("dve", C.mcol[:, nb * 12:(nb + 1) * 12, :],
             C.psN[:, 0:24].rearrange("p (j s) -> p j s", s=2),
             C.adab[:, i, nb * 12:(nb + 1) * 12].unsqueeze(2).to_broadcast([128, 12, 2]),
             ALU.add, [C.psN, C.adab], [C.mcol])
    m = lambda j, s: C.mcol[:, j * 16:(j + 1) * 16, s]
    g = lambda n: C.ngs[:, i, n, :]
    rr = [C.mcol, C.ngs]
    for s in range(2):
        P.stt("dve", C.mods[:, s, 0, :], m(1, s), 1.0, g(0), ALU.add, ALU.mult, rr, [C.mods])
        P.cp("dve", C.mods[:, s, 1, :], m(0, s), rr, [C.mods])
        P.tt("dve", C.mods[:, s, 2, :], m(2, s), g(1), ALU.mult, rr, [C.mods])
        P.stt("dve", C.mods[:, s, 3, :], m(4, s), 1.0, g(2), ALU.add, ALU.mult, rr, [C.mods])
        P.cp("dve", C.mods[:, s, 4, :], m(3, s), rr, [C.mods])
        P.tt("dve", C.mods[:, s, 5, :], m(5, s), g(3), ALU.mult, rr, [C.mods])


def rstd_of(C, srcbuf, n_chunks, addc):
    P = C.P
    psn = C.psA.next()
    for kc in range(n_chunks):
        sq = C.sq.next()
        P.act(sq[:], srcbuf[:, kc, :], AF.Square, [srcbuf], [sq])
        P.mm(psn[:, 0:W], C.ones[:], sq[:], kc == 0, kc == n_chunks - 1, [C.ones, sq], [psn])
    rs = C.rs.next()
    P.ts("dve", rs[:], psn[:, 0:W], addc, ALU.add, [psn], [rs])
    P.act(rs[:], rs[:], AF.Ln, [rs], [rs])
    P.act(rs[:], rs[:], AF.Exp, [rs], [rs], scale=-0.5)
    return rs


def modulate(C, src, s, ia, ish):
    P = C.P
    rs = rstd_of(C, src, KC, D * EPS)
    for kc in range(KC):
        t = C.tmpf.next()
        P.tt("dve", t[:], src[:, kc, :], rs[:], ALU.mult, [src, rs], [t])
        P.ts("pool", C.uT[:, kc, :], t[:], C.mods[:, s, ia, kc:kc + 1], ALU.mult, [t, C.mods], [C.uT],
             s2=C.mods[:, s, ish, kc:kc + 1], op1=ALU.add)


def prep_weight(C, name, w2d, K, N):
    nk, nch = K // 128, N // 128
    S = C.nc.dram_tensor(name, [nch, 128, nk, 128], BF16, kind="Internal").ap()
    bufs = [Buf() for _ in range(nch)]
    for j in range(nch):
        C.P.dma("pool", S[j], w2d[:, j * 128:(j + 1) * 128].rearrange("(k p) n -> p k n", p=128), w=[bufs[j]])
    return (S, bufs, nk)


def load_w(C, wt, cslot, Sw, j0, nch, k0=0, nk=None):
    S, bufs, nkfull = Sw
    if nk is None:
        nk = nkfull
    wv = wt[:].rearrange("p (c k n) -> p c k n", k=nk, n=128)
    C.P.dma("sp", wv[:, cslot:cslot + nch], S[j0:j0 + nch, :, k0:k0 + nk, :].rearrange("c p k n -> p c k n"),
            r=bufs[j0:j0 + nch], w=[wt])
    return wv


def load_w_cols(C, wt, slot, w2d, col0, ncols, nk=KC):
    P = C.P
    P.dma("pool", wt[:].rearrange("p (k n) -> p k n", k=nk)[:, :, slot:slot + ncols],
          w2d[:, col0:col0 + ncols].rearrange("(k p) n -> p k n", p=128), w=[wt])


def resid_update(C, g, s, igg, hsrc, dst_ap):
    P = C.P
    rs = rstd_of(C, C.yb, KC, D * EPS)
    for kc in range(KC):
        t = C.tmpf.next()
        P.tt("dve", t[:], C.yb[:, kc, :], rs[:], ALU.mult, [C.yb, rs], [t])
        P.stt("dve", C.hg[:, kc, :], t[:], C.mods[:, s, igg, kc:kc + 1], C.hg[:, kc, :], ALU.mult, ALU.add,
              [t, C.mods, C.hg], [C.hg])


def mlp_sublayer(C, i, s, w1_d, w2_d):
    P = C.P
    modulate(C, C.hg, s, 3, 4)
    for jb in range(16):
        wt = C.wb.next()
        load_w_cols(C, wt, 0, w1_d[i], jb * 512, 512)
        wv = wt[:].rearrange("p (k n) -> p k n", k=KC)
        for jj in range(4):
            ps = C.psA.next()
            for kc in range(KC):
                P.mm(ps[:, 0:W], wv[:, kc, jj * 128:(jj + 1) * 128], C.uT[:, kc, :], kc == 0, kc == KC - 1,
                     [wt, C.uT], [ps])
            t = C.tmpf.next()
            P.act(t[:], ps[:, 0:W], AF.Relu, [ps], [t])
            P.tt("pool", C.h1T[:, jb * 4 + jj, :], t[:], t[:], ALU.mult, [t], [C.h1T])
    for fo in range(KC):
        wt = C.wb.next()
        load_w_cols(C, wt, 0, w2_d[i], fo * 128, 128, nk=64)
        wv = wt[:].rearrange("p (k n) -> p k n", k=64)
        ps = C.psA.next()
        for fc in range(64):
            P.mm(ps[:, 0:W], wv[:, fc, 0:128], C.h1T[:, fc, :], fc == 0, fc == 63, [wt, C.h1T], [ps])
        P.cp("act", C.yb[:, fo, :], ps[:, 0:W], [ps], [C.yb])
    resid_update(C, None, s, 5, None, None)


def layer_tail(C, i, g, s, src, wout, last, w1_d, w2_d, nkc=KC):
    P = C.P
    c0 = g * W
    ncol = 8192 // nkc
    for fb in range(D // ncol):
        wt = C.wb.next()
        wv = load_w(C, wt, 0, C.S_out[i], fb * (ncol // 128), ncol // 128)
        for jj in range(ncol // 128):
            ps = C.psA.next()
            for kc in range(nkc):
                P.mm(ps[:, 0:W], wv[:, jj, kc, :], C.uTx[:, kc, :], kc == 0, kc == nkc - 1,
                     [wt, C.uTx], [ps])
            P.cp("act", C.yb[:, fb * (ncol // 128) + jj, :], ps[:, 0:W], [ps], [C.yb])
    P.dma("sp", C.hg[:], src[:, :, c0:c0 + W].rearrange("k p n -> p k n"), r=[C.hbuf[g]], w=[C.hg])
    resid_update(C, g, s, 2, None, None)
    P.dma("act", C.hT[:, :, c0:c0 + W].rearrange("k p n -> p k n"), C.hg[:], r=[C.hg], w=[C.hbuf[g]])


def mlp_phase(C, i, emit_ctx, last, w1_d, w2_d):
    P = C.P
    P.barrier()
    WW = 512
    H = Buf(C.arf.buf.t[:, 0:8192].rearrange("p (k n) -> p k n", n=WW))
    U = Buf(C.arb.buf.t[:, 0:8192].rearrange("p (k n) -> p k n", n=WW))
    Ylo = Buf(C.hg.t[:].rearrange("p k n -> p (k n)").rearrange("p (k n) -> p k n", n=WW))
    Yhi = Buf(C.yb.t[:].rearrange("p k n -> p (k n)").rearrange("p (k n) -> p k n", n=WW))
    H1 = Buf(C.h1T.t[:].rearrange("p a b -> p (a b)").rearrange("p (k n) -> p k n", n=WW))
    w0, w1_ = C.wada.bufs[0].t, C.wada.bufs[1].t
    rs = Buf(w0[:, 0:512])
    tmps = Rot([Buf(w0[:, 512:1024]), Buf(w0[:, 1024:1536]), Buf(w1_[:, 1024:1536])])
    sqs = Rot([Buf(w1_[:, 0:512]), Buf(w1_[:, 512:1024])])
    groups = []
    if emit_ctx:
        groups.append((0, 256, 1))
    for j in range(C.TL // WW):
        groups.append((TCTX + j * WW, WW, 0))

    def Y(fo):
        return (Ylo if fo < 8 else Yhi), fo % 8

    def norm(getchunk, rbufs):
        psn = C.psA.next()
        for kc in range(KC):
            sq = sqs.next()
            P.act(sq[:, 0:wd], getchunk(kc), AF.Square, rbufs, [sq])
            P.mm(psn[:, 0:wd], C.ones[:], sq[:, 0:wd], kc == 0, kc == KC - 1, [C.ones, sq], [psn])
        P.ts("dve", rs[:, 0:wd], psn[:, 0:wd], D * EPS, ALU.add, [psn], [rs])
        P.act(rs[:, 0:wd], rs[:, 0:wd], AF.Ln, [rs], [rs])
        P.act(rs[:, 0:wd], rs[:, 0:wd], AF.Exp, [rs], [rs], scale=-0.5)

    for (c0, wd, s) in groups:
        hb = [C.hbuf[gg] for gg in range(c0 // W, (c0 + wd - 1) // W + 1)]
        P.dma("sp", H[:, :, 0:wd], C.hT[:, :, c0:c0 + wd].rearrange("k p n -> p k n"), r=hb, w=[H])
        norm(lambda kc: H[:, kc, 0:wd], [H])
        for kc in range(KC):
            t = tmps.next()
            P.tt("dve", t[:, 0:wd], H[:, kc, 0:wd], rs[:, 0:wd], ALU.mult, [H, rs], [t])
            P.ts("pool", U[:, kc, 0:wd], t[:, 0:wd], C.mods[:, s, 3, kc:kc + 1], ALU.mult, [t, C.mods], [U],
                 s2=C.mods[:, s, 4, kc:kc + 1], op1=ALU.add)
        for half in range(2):
            for jb in range(8):
                wt = C.wb.next()
                wv = load_w(C, wt, 0, C.S_w1[i], half * 32 + jb * 4, 4)
                for jj in range(4):
                    ps = C.psA.next()
                    for kc in range(KC):
                        P.mm(ps[:, 0:wd], wv[:, jj, kc, :], U[:, kc, 0:wd], kc == 0, kc == KC - 1,
                             [wt, U], [ps])
                    t = tmps.next()
                    P.act(t[:, 0:wd], ps[:, 0:wd], AF.Relu, [ps], [t])
                    P.tt("pool", H1[:, jb * 4 + jj, 0:wd], t[:, 0:wd], t[:, 0:wd], ALU.mult, [t], [H1])
            for fo in range(KC):
                wt = C.wb.next()
                wv = load_w(C, wt, 0, C.S_w2[i], fo, 1, k0=half * 32, nk=32)
                ps = C.psA.next()
                for fc in range(32):
                    P.mm(ps[:, 0:wd], wv[:, 0, fc, :], H1[:, fc, 0:wd], fc == 0, fc == 31, [wt, H1], [ps])
                yb_, yi = Y(fo)
                if half == 0:
                    P.cp("act", yb_[:, yi, 0:wd], ps[:, 0:wd], [ps], [yb_])
                else:
                    P.tt("dve", yb_[:, yi, 0:wd], yb_[:, yi, 0:wd], ps[:, 0:wd], ALU.add, [yb_, ps], [yb_])
        norm(lambda kc: Y(kc)[0][:, Y(kc)[1], 0:wd], [Ylo, Yhi])
        for kc in range(KC):
            yb_, yi = Y(kc)
            t = tmps.next()
            P.tt("dve", t[:, 0:wd], yb_[:, yi, 0:wd], rs[:, 0:wd], ALU.mult, [yb_, rs], [t])
            P.stt("dve", H[:, kc, 0:wd], t[:, 0:wd], C.mods[:, s, 5, kc:kc + 1], H[:, kc, 0:wd], ALU.mult, ALU.add,
                  [t, C.mods, H], [H])
        if last:
            P.dma("act", C.outT[:, :, c0 - TCTX:c0 - TCTX + wd].rearrange("k p n -> p k n"), H[:, :, 0:wd], r=[H])
        else:
            P.dma("act", C.hT[:, :, c0:c0 + wd].rearrange("k p n -> p k n"), H[:, :, 0:wd], r=[H], w=hb)
    P.barrier()


SCALE_B = 128.0 ** -0.5


def swa_layer(C, i, jB, src, last, emit_ctx, w1_d, w2_d):
    P = C.P
    NG, NT = C.NG, C.NT
    P.barrier()
    C.arf.reset()
    C.arb.reset()
    wqkv = C.bwqkv_d[jB]
    wout = C.bwout_d[jB]
    A, Bf = C.arf, C.arb
    rd = A.takes([512], 2)
    cosb = A.takes([W], 2)
    sinb = A.takes([W], 2)
    xf = A.takes([W], 2)
    mprev = A.take([512])
    mnext = A.take([512])
    sinkexp = A.take([16])
    pmTf = A.take([128])
    sq = Bf.take([16 * 128])
    kctx = Bf.take([4, 256])
    vctx = Bf.take([2, 512])
    kloc = Bf.take([4, 384])
    vloc = Bf.take([3, 512])
    eb = Bf.takes([512], 3)
    xb = Bf.takes([W], 2)
    stq = Bf.takes([4, W], 2)
    stv = Bf.take([4, W])
    vt = Bf.takes([512], 2)
    pmTb = Bf.take([128])
    onesb = Bf.take([128])
    psO0 = C.psO[0].t[:].rearrange("p a b -> p (a b)")
    psD = Buf(psO0[:, 0:512])
    psOt = Buf(psO0[:, 512:1024])
    P.dma("sp", pmTf[:], C.swac_d[:, 0:128], w=[pmTf])
    P.dma("sp", mprev[:], C.swac_d[:, 128:640], w=[mprev])
    P.dma("sp", mnext[:], C.swac_d[:, 640:1152], w=[mnext])
    P.dma("sp", sinkexp[:], C.bsink_d[:, jB * 16:(jB + 1) * 16], w=[sinkexp])
    P.act(sinkexp[:], sinkexp[:], AF.Exp, [sinkexp], [sinkexp])
    P.cp("dve", pmTb[:], pmTf[:], [pmTf], [pmTb])
    P.memset("pool", onesb[:], 1.0, [onesb])

    for g in range(NG):
        s = 1 if g == 0 else 0
        c0 = g * W
        P.dma("sp", C.hg[:], src[:, :, c0:c0 + W].rearrange("k p n -> p k n"), r=[C.hbuf[g]], w=[C.hg])
        modulate(C, C.hg, s, 0, 1)
        if g > 0:
            cb = cosb.next()
            sb_ = sinb.next()
            P.dma("sp", cb[:], C.ropec_d[:, c0 - TCTX:c0 - TCTX + W], w=[cb])
            P.dma("sp", sb_[:], C.ropes_d[:, c0 - TCTX:c0 - TCTX + W], w=[sb_])
        for blk in range(6):
            wt = C.wb.next()
            wv = load_w(C, wt, 0, C.S_in[i], blk * 4, 4)
            st = stq.next() if blk < 5 else stv
            for jj in range(4):
                ps = C.psA.next()
                for kc in range(KC):
                    P.mm(ps[:, 0:W], wv[:, jj, kc, :], C.uT[:, kc, :], kc == 0, kc == KC - 1,
                         [wt, C.uT], [ps])
                if blk == 5 or g == 0:
                    P.cp("act", st[:, jj, :], ps[:, 0:W], [ps], [st])
                else:
                    x = xf.next()
                    P.cp("act", x[:], ps[:, 0:W], [ps], [x])
                    xbb = xb.next()
                    P.cp("pool", xbb[:], x[:], [x], [xbb])
                    ps2 = C.psA.next()
                    P.mm(ps2[:, 0:W], pmTb[:], xbb[:], True, True, [pmTb, xbb], [ps2])
                    t1 = C.tmpf.next()
                    P.tt("dve", t1[:], x[:], cb[:], ALU.mult, [x, cb], [t1])
                    t2 = C.tmpf.next()
                    P.tt("dve", t2[:], ps2[:, 0:W], sb_[:], ALU.mult, [ps2, sb_], [t2])
                    P.tt("pool", st[:, jj, :], t1[:], t2[:], ALU.add, [t1, t2], [st])
            if blk < 4:
                P.dma("act", C.QT[blk * 4:(blk + 1) * 4, :, c0:c0 + W].rearrange("h p n -> p h n"), st[:],
                      r=[st], w=[C.qbuf[g][blk]])
            elif blk == 4:
                P.dma("act", C.KT[:, :, c0:c0 + W].rearrange("h p n -> p h n"), st[:], r=[st], w=[C.kbuf[g]])
            else:
                for t in range(2):
                    for gk in range(4):
                        P.tr(C.psT[:, gk * 128:(gk + 1) * 128], stv[:, gk, t * 128:(t + 1) * 128], C.identb[:],
                             [stv, C.identb], [C.psT])
                    v = vt.next()
                    P.cp("act", v[:], C.psT[:, 0:512], [C.psT], [v])
                    P.dma("act", C.VTM[c0 + t * 128:c0 + (t + 1) * 128, :], v[:], r=[v], w=[C.vbuf[g][t]])

    P.dma("sp", kctx[:], C.KT[:, :, 0:TCTX].rearrange("h p n -> p h n"), r=[C.kbuf[0]], w=[kctx])
    P.dma("sp", vctx[:], C.VTM[0:TCTX, :].rearrange("(t p) n -> p t n", p=128), r=C.vbuf[0], w=[vctx])
    for g in range(NG):
        if g == 0 and not emit_ctx:
            continue
        s = 1 if g == 0 else 0
        c0 = g * W
        for t in range(2):
            qc0 = c0 + t * 128
            P.dma("sp", sq[:].rearrange("p (h n) -> p h n", h=16),
                  C.QT[:, :, qc0:qc0 + 128].rearrange("h p n -> p h n"), r=C.qbuf[g], w=[sq])
            blocks = [("c", 0, 0), ("c", 1, 0)]
            if g > 0:
                lo = max(TCTX, qc0 - 128)
                hi = min(NT, qc0 + 256)
                nl = (hi - lo) // 128
                P.dma("sp", kloc[:, :, 0:hi - lo], C.KT[:, :, lo:hi].rearrange("h p n -> p h n"),
                      r=[C.kbuf[gg] for gg in range(lo // W, (hi - 1) // W + 1)], w=[kloc])
                P.dma("sp", vloc[:, 0:nl, :], C.VTM[lo:hi, :].rearrange("(t p) n -> p t n", p=128),
                      r=[C.vbuf[tt // 2][tt % 2] for tt in range(lo // 128, (hi - 1) // 128 + 1)], w=[vloc])
                for bi in range(nl):
                    blocks.append(("l", bi, (lo + bi * 128 - qc0) // 128))
            for gk in range(4):
                for bi, (kind, idx, rel) in enumerate(blocks):
                    if kind == "c":
                        kap = kctx[:, gk, idx * 128:(idx + 1) * 128]
                        vap = vctx[:, idx, gk * 128:(gk + 1) * 128]
                        rk, rv = kctx, vctx
                    else:
                        kap = kloc[:, gk, idx * 128:(idx + 1) * 128]
                        vap = vloc[:, idx, gk * 128:(gk + 1) * 128]
                        rk, rv = kloc, vloc
                    ps = C.psA.next()
                    P.mm(ps[:, 0:512], kap, sq[:, gk * 512:(gk + 1) * 512], True, True, [rk, sq], [ps])
                    e = eb.next()
                    P.act(e[:], ps[:, 0:512], AF.Exp, [ps], [e], scale=SCALE_B)
                    if rel == -1:
                        P.tt("pool", e[:], e[:], mprev[:], ALU.mult, [e, mprev], [e])
                    elif rel == 1:
                        P.tt("pool", e[:], e[:], mnext[:], ALU.mult, [e, mnext], [e])
                    first = bi == 0
                    lastb = bi == len(blocks) - 1
                    P.mm(psD[:], onesb[:], e[:], first, lastb, [onesb, e], [psD])
                    for hq in range(4):
                        P.mm(psOt[:, hq * 128:(hq + 1) * 128], vap, e[:, hq * 128:(hq + 1) * 128],
                             first and hq == 0, lastb, [rv, e], [psOt], skip=True)
                r = rd.next()
                for hq in range(4):
                    h = gk * 4 + hq
                    P.ts("dve", r[:, hq * 128:(hq + 1) * 128], psD[:, hq * 128:(hq + 1) * 128],
                         sinkexp[:, h:h + 1], ALU.add, [psD, sinkexp], [r])
                P.op("dve", (lambda rr: (lambda e_: e_.reciprocal(out=rr, in_=rr)))(r[:]), [r], [r])
                P.tt("dve", C.uT[:, gk * 4:(gk + 1) * 4, t * 128:(t + 1) * 128],
                     psOt[:].rearrange("p (a b) -> p a b", b=128), r[:].rearrange("p (a b) -> p a b", b=128),
                     ALU.mult, [psOt, r], [C.uT])
        C.uTx = C.uT
        layer_tail(C, i, g, s, src, wout, last, w1_d, w2_d)


def ssd_layer(C, i, jC, src, last, emit_ctx, w1_d, w2_d):
    P = C.P
    NG, NT = C.NG, C.NT
    win = C.cwin_d[jC]
    wout = C.cwout_d[jC]
    A, Bf = C.arf, C.arb
    P.barrier()
    A.reset()
    Bf.reset()
    dtb = A.take([1])
    aneg = A.take([1])
    cdt = A.take([64])
    convw = A.take([48, 5])
    convb = A.take([48])
    onc = A.take([32])
    P.dma("sp", dtb[:], C.cdtb_d[:, jC:jC + 1], w=[dtb])
    P.dma("sp", aneg[:], C.calog_d[:, jC:jC + 1], w=[aneg])
    P.act(aneg[:], aneg[:], AF.Exp, [aneg], [aneg])
    P.ts("dve", aneg[:], aneg[:], -1.0, ALU.mult, [aneg], [aneg])
    P.dma("sp", cdt[:], C.cd_d[:, jC * 64:(jC + 1) * 64], w=[cdt])
    P.dma("sp", convw[:], C.cconvw_d[:, jC], w=[convw])
    P.dma("sp", convb[:], C.cconvb_d[:, jC], w=[convb])
    P.dma("sp", onc[:], C.conorm_d[:, jC], w=[onc])
    base_a, base_b = A.off, Bf.off

    xp = A.takes([4, W], 2)
    dtl = A.takes([2, W], 2)
    dtm = A.takes([256], 2)
    ee = A.takes([W], 2)
    zt = Bf.takes([512], 3)
    for g in range(NG):
        s = 1 if g == 0 else 0
        c0 = g * W
        P.dma("sp", C.hg[:], src[:, :, c0:c0 + W].rearrange("k p n -> p k n"), r=[C.hbuf[g]], w=[C.hg])
        modulate(C, C.hg, s, 0, 1)
        for blk in range(12):
            wt = C.wb.next()
            wv = load_w(C, wt, 0, C.S_in[i], 32 + blk * 4, 4)
            st = xp.next()
            for jj in range(4):
                ps = C.psA.next()
                for kc in range(KC):
                    P.mm(ps[:, 0:W], wv[:, jj, kc, :], C.uT[:, kc, :], kc == 0, kc == KC - 1,
                         [wt, C.uT], [ps])
                P.cp("act", st[:, jj, :], ps[:, 0:W], [ps], [st])
            P.dma("act", C.XP[blk * 4:(blk + 1) * 4, :, c0:c0 + W].rearrange("c p n -> p c n"), st[:],
                  r=[st], w=[C.xpbuf[g][blk]])
        wt = C.wb.next()
        wv = load_w(C, wt, 0, C.S_in[i], 80, 1)
        ps = C.psA.next()
        for kc in range(KC):
            P.mm(ps[:, 0:W], wv[:, 0, kc, :], C.uT[:, kc, :], kc == 0, kc == KC - 1, [wt, C.uT], [ps])
        e1 = ee.next()
        P.act(e1[:], ps[:, 0:W], AF.Exp, [ps, dtb], [e1], bias=dtb[:, 0:1])
        P.ts("dve", e1[:], e1[:], 1.0, ALU.add, [e1], [e1])
        dl = dtl.next()
        P.act(dl[:, 0, :], e1[:], AF.Ln, [e1], [dl])
        P.ts("dve", dl[:, 1, :], dl[:, 0, :], aneg[:, 0:1], ALU.mult, [dl, aneg], [dl])
        for t in range(2):
            ps = C.psA.next()
            for a in range(2):
                P.mm(ps[:, a * 128:(a + 1) * 128], dl[:, a, t * 128:(t + 1) * 128], C.identf[:], True, True,
                     [dl, C.identf], [ps])
            dm = dtm.next()
            P.cp("act", dm[:], ps[:, 0:256], [ps], [dm])
            P.dma("act", C.DTM[c0 + t * 128:c0 + (t + 1) * 128, :], dm[:], r=[dm], w=[C.dtmbuf[g][t]])
        for nb in range(8):
            wt = C.wb.next()
            wv = load_w(C, wt, 0, C.S_in[i], nb * 4, 4)
            for t in range(2):
                ps = C.psA.next()
                for kc in range(KC):
                    P.mm(ps[:, 0:512].rearrange("p (c n) -> p c n", n=128), C.uT[:, kc, t * 128:(t + 1) * 128],
                         wv[:, :, kc, :], kc == 0, kc == KC - 1, [wt, C.uT], [ps])
                z = zt.next()
                P.act(z[:], ps[:, 0:512], AF.Silu, [ps], [z])
                P.dma("act", C.SZ[c0 + t * 128:c0 + (t + 1) * 128, nb * 512:(nb + 1) * 512], z[:], r=[z],
                      w=[C.szbuf[g][t][nb]])

    P.barrier()
    A.off, Bf.off = base_a, base_b
    xin = A.takes([W + 4], 3)
    acc = A.takes([W], 2)
    tms = Bf.takes([1024], 2)
    for g in range(NG):
        c0 = g * W
        seq_lo = 0 if g == 0 else TCTX
        seq_hi = TCTX if g == 0 else NT
        lo = max(seq_lo, c0 - 2)
        hi = min(seq_hi, c0 + W + 2)
        gl = list(range(max(0, g - 1), min(NG, g + 2)))
        for cc in range(48):
            xi = xin.next()
            if lo > c0 - 2:
                P.memset("pool", xi[:, 0:2], 0.0, [xi])
            if hi < c0 + W + 2:
                P.memset("pool", xi[:, W + 2:W + 4], 0.0, [xi])
            P.dma("sp", xi[:, lo - (c0 - 2):hi - (c0 - 2)], C.XP[cc, :, lo:hi],
                  r=[C.xpbuf[gg][cc // 4] for gg in gl], w=[xi])
            ac = acc.next()
            P.ts("dve", ac[:], xi[:, 0:W], convw[:, cc, 0:1], ALU.mult, [xi, convw], [ac])
            for j in range(1, 5):
                P.stt("dve", ac[:], xi[:, j:j + W], convw[:, cc, j:j + 1], ac[:], ALU.mult, ALU.add,
                      [xi, convw, ac], [ac])
            P.act(C.h1T[:, cc, :], ac[:], AF.Silu, [ac, convb], [C.h1T], bias=convb[:, cc:cc + 1])
        P.dma("act", C.BT[:, :, c0:c0 + W].rearrange("g p n -> p g n"), C.h1T[:, 32:40, :], r=[C.h1T],
              w=[C.btbuf[g]])
        P.dma("act", C.CT[:, :, c0:c0 + W].rearrange("g p n -> p g n"), C.h1T[:, 40:48, :], r=[C.h1T],
              w=[C.ctbuf[g]])
        for t in range(2):
            for b8 in range(5):
                for j in range(8):
                    P.tr(C.psT[:, j * 128:(j + 1) * 128], C.h1T[:, b8 * 8 + j, t * 128:(t + 1) * 128], C.identb[:],
                         [C.h1T, C.identb], [C.psT])
                tm = tms.next()
                P.cp("act", tm[:], C.psT[:], [C.psT], [tm])
                if b8 < 4:
                    P.dma("act", C.XTM[c0 + t * 128:c0 + (t + 1) * 128, b8 * 1024:(b8 + 1) * 1024], tm[:], r=[tm],
                          w=[C.xtbuf[g][t][b8]])
                else:
                    P.dma("act", C.BTM[c0 + t * 128:c0 + (t + 1) * 128, :], tm[:], r=[tm], w=[C.btmbuf[g][t]])

    P.barrier()
    A.off, Bf.off = base_a, base_b
    h1 = C.h1T.t[:].rearrange("p a b -> p (a b)")
    x_tm, xdt, xw = Buf(h1[:, 0:4096]), Buf(h1[:, 4096:8192]), Buf(h1[:, 8192:12288])
    hSb = Buf(h1[:, 12288:16384].rearrange("p (g n) -> p g n", g=8))
    hS = A.take([8, 512])
    seg = A.takes([1024], 2)
    cbm = A.takes([128], 2)
    ysb = A.takes([512], 2)
    dtm = A.takes([256], 2)
    sm = {n: A.takes([64], 2) for n in ("cum", "tot", "ecum", "ecend", "fdec", "dtf")}
    btm = Bf.takes([1024], 2)
    bT = Bf.takes([8, 128], 2)
    cT = Bf.takes([8, 128], 2)
    wm = Bf.takes([1024], 2)
    psO0 = Buf(C.psO[0].t)
    psO1 = C.psO[1].t[:].rearrange("p a b -> p (a b)")
    psY, psI = Buf(psO1[:, 0:512]), Buf(psO1[:, 512:1024])
    psS = Buf(C.psS.t[:].rearrange("p a b -> p (a b)"))
    for d in range(2):
        order = list(range(NG)) if d == 0 else [0] + list(range(NG - 1, 0, -1))
        TRI = C.trif if d == 0 else C.trib
        TRI = C.tri128[d]
        P.memset("dve", hS[:], 0.0, [hS])
        P.memset("pool", hSb[:], 0.0, [hSb])
        for g in order:
            c0 = g * W
            for t in ([0, 1] if d == 0 else [1, 0]):
                r0 = c0 + t * 128
                P.dma("sp", x_tm[:], C.XTM[r0:r0 + 128, :], r=C.xtbuf[g][t], w=[x_tm])
                bm = btm.next()
                P.dma("sp", bm[:], C.BTM[r0:r0 + 128, :], r=[C.btmbuf[g][t]], w=[bm])
                bt_, ct_ = bT.next(), cT.next()
                P.dma("sp", bt_[:], C.BT[:, :, r0:r0 + 128].rearrange("g p n -> p g n"), r=[C.btbuf[g]], w=[bt_])
                P.dma("sp", ct_[:], C.CT[:, :, r0:r0 + 128].rearrange("g p n -> p g n"), r=[C.ctbuf[g]], w=[ct_])
                dm = dtm.next()
                P.dma("sp", dm[:], C.DTM[r0:r0 + 128, :], r=[C.dtmbuf[g][t]], w=[dm])
                dt_t = dm[:, d * 64:(d + 1) * 64]
                la_t = dm[:, 128 + d * 64:128 + (d + 1) * 64]
                ps = C.psA.next()
                P.mm(ps[:, 0:64], TRI[:], la_t, True, True, [TRI, dm], [ps])
                P.mm(ps[:, 64:128], C.ones[:], la_t, True, True, [C.ones, dm], [ps])
                cum, tot = sm["cum"].next(), sm["tot"].next()
                P.cp("act", cum[:], ps[:, 0:64], [ps], [cum])
                P.cp("act", tot[:], ps[:, 64:128], [ps], [tot])
                ecum, ecend, fdec, dtf = sm["ecum"].next(), sm["ecend"].next(), sm["fdec"].next(), sm["dtf"].next()
                P.act(ecum[:], cum[:], AF.Exp, [cum], [ecum])
                P.act(ecend[:], tot[:], AF.Exp, [tot], [ecend])
                P.tt("dve", fdec[:], tot[:], cum[:], ALU.subtract, [tot, cum], [fdec])
                P.act(fdec[:], fdec[:], AF.Exp, [fdec], [fdec])
                P.tt("dve", dtf[:], fdec[:], dt_t, ALU.mult, [fdec, dm], [dtf])
                x3 = x_tm[:].rearrange("p (h q) -> p h q", q=64)
                P.tt("pool", xdt[:].rearrange("p (h q) -> p h q", q=64), x3,
                     dt_t.unsqueeze(2).to_broadcast([128, 64, 64]), ALU.mult, [x_tm, dm], [xdt])
                P.tt("dve", xw[:].rearrange("p (h q) -> p h q", q=64), x3,
                     dtf[:].unsqueeze(2).to_broadcast([128, 64, 64]), ALU.mult, [x_tm, dtf], [xw])
                for gi in range(8):
                    psc = C.psA.next()
                    P.mm(psc[:, 0:128], bt_[:, gi, :], ct_[:, gi, :], True, True, [bt_, ct_], [psc])
                    cb = cbm.next()
                    P.tt("dve", cb[:], psc[:, 0:128], TRI[:], ALU.mult, [psc, TRI], [cb])
                    for e in range(8):
                        hh = gi * 8 + e
                        P.mm(psO0[:, e, :], la_t[:, hh:hh + 1].to_broadcast([128, 128]), TRI[:], True, True,
                             [dm, TRI], [psO0])
                    sg1 = seg.next()
                    P.tt("dve", sg1[:].rearrange("p (h q) -> p h q", q=128), psO0[:],
                         cum[:, gi * 8:(gi + 1) * 8].unsqueeze(2).to_broadcast([128, 8, 128]), ALU.subtract,
                         [psO0, cum], [sg1])
                    P.ts("pool", sg1[:], sg1[:], 0.0, ALU.min, [sg1], [sg1])
                    P.act(sg1[:], sg1[:], AF.Exp, [sg1], [sg1])
                    w_ = wm.next()
                    P.tt("pool", w_[:].rearrange("p (h q) -> p h q", q=128),
                         sg1[:].rearrange("p (h q) -> p h q", q=128),
                         cb[:].unsqueeze(1).to_broadcast([128, 8, 128]), ALU.mult, [sg1, cb], [w_])
                    for e in range(8):
                        hh = gi * 8 + e
                        P.mm(psY[:, e * 64:(e + 1) * 64], w_[:, e * 128:(e + 1) * 128], xdt[:, hh * 64:(hh + 1) * 64],
                             e == 0, True, [w_, xdt], [psY], skip=True)
                    P.mm(psI[:], ct_[:, gi, :], hSb[:, gi, :], True, True, [ct_, hSb], [psI])
                    yb_ = ysb.next()
                    P.tt("dve", yb_[:].rearrange("p (h q) -> p h q", q=64),
                         psI[:].rearrange("p (h q) -> p h q", q=64),
                         ecum[:, gi * 8:(gi + 1) * 8].unsqueeze(2).to_broadcast([128, 8, 64]), ALU.mult,
                         [psI, ecum], [yb_])
                    P.tt("dve", yb_[:], yb_[:], psY[:], ALU.add, [yb_, psY], [yb_])
                    P.dma("act", C.Y[d][r0:r0 + 128, gi * 512:(gi + 1) * 512], yb_[:], r=[yb_],
                          w=[C.ybuf[d][g][t][gi]])
                    P.mm(psS[:], bm[:, gi * 128:(gi + 1) * 128], xw[:, gi * 512:(gi + 1) * 512], True, True,
                         [bm, xw], [psS])
                    P.tt("pool", hS[:, gi, :].rearrange("p (h q) -> p h q", q=64),
                         hS[:, gi, :].rearrange("p (h q) -> p h q", q=64),
                         ecend[:, gi * 8:(gi + 1) * 8].unsqueeze(2).to_broadcast([128, 8, 64]), ALU.mult,
                         [hS, ecend], [hS])
                    P.tt("dve", hS[:, gi, :], hS[:, gi, :], psS[:], ALU.add, [hS, psS], [hS])
                    P.cp("pool", hSb[:, gi, :], hS[:, gi, :], [hS], [hSb])

    P.barrier()
    A.off, Bf.off = base_a, base_b
    y0q = A.takes([1024], 2)
    y1q = A.takes([1024], 2)
    tq = A.takes([1024], 2)
    ss = A.takes([8], 2)
    xq = Bf.takes([1024], 2)
    szq = Bf.takes([1024], 2)
    sqt = Bf.takes([512], 2)
    yyb = Bf.take([4096])
    yy2 = Bf.take([4096])
    for g in range(NG):
        if g == 0 and not emit_ctx:
            continue
        s = 1 if g == 0 else 0
        c0 = g * W
        for t in range(2):
            r0 = c0 + t * 128
            ssv = ss.next()
            for q in range(4):
                cs = slice(q * 1024, (q + 1) * 1024)
                y0, y1, xq_, sz_ = y0q.next(), y1q.next(), xq.next(), szq.next()
                P.dma("sp", y0[:], C.Y[0][r0:r0 + 128, cs], r=C.ybuf[0][g][t][2 * q:2 * q + 2], w=[y0])
                P.dma("sp", y1[:], C.Y[1][r0:r0 + 128, cs], r=C.ybuf[1][g][t][2 * q:2 * q + 2], w=[y1])
                P.dma("sp", xq_[:], C.XTM[r0:r0 + 128, cs], r=[C.xtbuf[g][t][q]], w=[xq_])
                P.dma("sp", sz_[:], C.SZ[r0:r0 + 128, cs], r=C.szbuf[g][t][2 * q:2 * q + 2], w=[sz_])
                P.tt("dve", y0[:], y0[:], y1[:], ALU.add, [y0, y1], [y0])
                tt_ = tq.next()
                P.tt("pool", tt_[:].rearrange("p (h q) -> p h q", q=64), xq_[:].rearrange("p (h q) -> p h q", q=64),
                     cdt[:, q * 16:(q + 1) * 16].unsqueeze(2).to_broadcast([128, 16, 64]), ALU.mult,
                     [xq_, cdt], [tt_])
                P.tt("dve", y0[:], y0[:], tt_[:], ALU.add, [y0, tt_], [y0])
                P.tt("dve", yyb[:, cs], y0[:], sz_[:], ALU.mult, [y0, sz_], [yyb])
                for gg in range(2):
                    sq_ = sqt.next()
                    gcs = slice(q * 1024 + gg * 512, q * 1024 + (gg + 1) * 512)
                    P.act(sq_[:], yyb[:, gcs], AF.Square, [yyb], [sq_])
                    P.op("dve", (lambda o_, i_: (lambda e_: e_.reduce_sum(out=o_, in_=i_, axis=mybir.AxisListType.X)))(
                        ssv[:, 2 * q + gg:2 * q + gg + 1], sq_[:]), [sq_], [ssv])
            P.ts("dve", ssv[:], ssv[:], 1.0 / 512.0, ALU.mult, [ssv], [ssv], s2=EPS, op1=ALU.add)
            P.act(ssv[:], ssv[:], AF.Ln, [ssv], [ssv])
            P.act(ssv[:], ssv[:], AF.Exp, [ssv], [ssv], scale=-0.5)
            for gi in range(8):
                P.ts("pool", yy2[:, gi * 512:(gi + 1) * 512], yyb[:, gi * 512:(gi + 1) * 512], ssv[:, gi:gi + 1],
                     ALU.mult, [yyb, ssv], [yy2])
            for b8 in range(4):
                for j in range(8):
                    cc = b8 * 8 + j
                    P.tr(C.psT[:, j * 128:(j + 1) * 128], yy2[:, cc * 128:(cc + 1) * 128], C.identb[:],
                         [yy2, C.identb], [C.psT])
                P.tt("dve", C.h1T[:, b8 * 8:(b8 + 1) * 8, t * 128:(t + 1) * 128],
                     C.psT[:].rearrange("p (c n) -> p c n", n=128),
                     onc[:, b8 * 8:(b8 + 1) * 8].unsqueeze(2).to_broadcast([128, 8, 128]), ALU.mult,
                     [C.psT, onc], [C.h1T])
        C.uTx = C.h1T
        layer_tail(C, i, g, s, src, wout, last, w1_d, w2_d, nkc=32)


def hgrn2_layer(C, i, jA, src, last, emit_ctx, w1_d, w2_d):
    P = C.P
    NG = C.NG
    win = C.awin_d[jA]
    wout = C.awout_d[jA]
    P.barrier()
    C.arf.reset()
    C.arb.reset()
    C.e1 = {n: C.arf.takes([W], 2) for n in
            ("qt", "sg", "f", "lf", "k", "pfx", "g", "eg", "egn", "kgf")}
    C.decst = C.arf.take([2, KC, W // CH])
    C.sDEC = C.arf.takes([KC, W // CH], 2)
    C.S = [C.arf.take([8, 128]) for h in range(2)]
    C.osb = C.arf.takes([8, 128], 1)
    C.st8 = C.arb.takes([8, W], 2)
    C.sQ, C.sK, C.sKE, C.sV = Buf(), Buf(), Buf(), Buf()
    C.ketm = C.arb.takes([8 * 128], 2)
    C.vtm = C.arb.takes([8 * 128], 2)
    C.vm = C.arb.takes([8 * 128], 3)
    C.am = C.arb.takes([512], 2)
    C.Sb = [C.arb.take([8, 128]) for h in range(2)]
    E = C.e1
    for g in range(NG):
        s = 1 if g == 0 else 0
        c0 = g * W
        P.dma("sp", C.hg[:], src[:, :, c0:c0 + W].rearrange("k p n -> p k n"), r=[C.hbuf[g]], w=[C.hg])
        modulate(C, C.hg, s, 0, 1)
        for h in range(KC):
            wa = C.wb.next()
            load_w(C, wa, 0, C.S_in[i], 0 * 16 + h, 1)
            load_w(C, wa, 1, C.S_in[i], 3 * 16 + h, 1)
            load_w(C, wa, 2, C.S_in[i], 4 * 16 + h, 1)
            wav = load_w(C, wa, 3, C.S_in[i], 2 * 16 + h, 1)
            wg = C.wb.next()
            wgv = load_w(C, wg, 0, C.S_in[i], 1 * 16 + h, 1)
            st = C.st8.next()

            def proj(wbuf, wview, slot):
                ps = C.psA.next()
                for kc in range(KC):
                    P.mm(ps[:, 0:W], wview[:, slot, kc, :], C.uT[:, kc, :], kc == 0, kc == KC - 1,
                         [wbuf, C.uT], [ps])
                return ps
            psq = proj(wa, wav, 0)
            qt = E["qt"].next()
            P.act(qt[:], psq[:, 0:W], AF.Silu, [psq], [qt])
            Dv = [dict(), dict()]

            def s0(d, v):
                v["psf"] = proj(wa, wav, 1 + d)
                v["sg"] = E["sg"].next()
                P.act(v["sg"][:], v["psf"][:, 0:W], AF.Sigmoid, [v["psf"]], [v["sg"]])

            def s1(d, v):
                v["f"] = E["f"].next()
                P.ts("dve", v["f"][:], v["sg"][:], C.oml[:, i, d, h:h + 1], ALU.mult, [v["sg"], C.oml, C.lb],
                     [v["f"]], s2=C.lb[:, i, d, h:h + 1], op1=ALU.add)

            def s2(d, v):
                v["lf"] = E["lf"].next()
                P.act(v["lf"][:], v["f"][:], AF.Ln, [v["f"]], [v["lf"]])
                v["k"] = E["k"].next()
                P.ts("pool", v["k"][:], v["f"][:], -1.0, ALU.mult, [v["f"]], [v["k"]], s2=1.0, op1=ALU.add)

            def s3(d, v):
                pfx = E["pfx"].next()
                v["pfx"] = pfx
                P.scan(pfx[:], C.mskr[:], v["lf"][:], [C.mskr, v["lf"]], [pfx])
                pfx3 = pfx[:].rearrange("p (c k) -> p c k", k=CH)
                v["pfx3"] = pfx3
                tot_b = pfx3[:, :, CH - 1:CH].to_broadcast([128, W // CH, CH])
                if d == 0:
                    v["G"] = pfx
                else:
                    G = E["g"].next()
                    v["G"] = G
                    P.tt("dve", G[:], v["lf"][:], pfx[:], ALU.subtract, [v["lf"], pfx], [G])
                    P.tt("dve", G[:].rearrange("p (c k) -> p c k", k=CH),
                         G[:].rearrange("p (c k) -> p c k", k=CH), tot_b, ALU.add, [G, pfx], [G])

            def s4(d, v):
                G = v["G"]
                v["eg"] = E["eg"].next()
                P.act(v["eg"][:], G[:], AF.Exp, [G], [v["eg"]])
                v["egn"] = E["egn"].next()
                P.act(v["egn"][:], G[:], AF.Exp, [G], [v["egn"]], scale=-1.0)
                P.act(C.decst[:, d, h, :], v["pfx3"][:, :, CH - 1], AF.Exp, [v["pfx"]], [C.decst])

            def s5(d, v):
                P.tt("pool", st[:, 3 * d + 0, :], qt[:], v["eg"][:], ALU.mult, [qt, v["eg"]], [st])
                v["kgf"] = E["kgf"].next()
                P.tt("dve", v["kgf"][:], v["k"][:], v["egn"][:], ALU.mult, [v["k"], v["egn"]], [v["kgf"]])

            def s6(d, v):
                kgf = v["kgf"]
                P.cp("pool", st[:, 3 * d + 1, :], kgf[:], [kgf], [st])
                P.tt("dve", st[:, 3 * d + 2, :].rearrange("p (c k) -> p c k", k=CH),
                     kgf[:].rearrange("p (c k) -> p c k", k=CH),
                     C.decst[:, d, h, :].unsqueeze(2).to_broadcast([128, W // CH, CH]), ALU.mult,
                     [kgf, C.decst], [st])

            for step in (s0, s1, s2, s3, s4, s5, s6):
                for d in range(2):
                    step(d, Dv[d])
            psv = proj(wa, wav, 3)
            P.cp("act", st[:, 6, :], psv[:, 0:W], [psv], [st])
            psg = proj(wg, wgv, 0)
            P.act(st[:, 7, :], psg[:, 0:W], AF.Silu, [psg], [st])
            for ty in range(8):
                P.dma("act", C.SCR[ty][h, :, c0:c0 + W], st[:, ty, :], r=[st], w=[C.scrbuf[g][ty][h]])
        P.dma("act", C.DEC[:, :, :, g * (W // CH):(g + 1) * (W // CH)], C.decst[:], r=[C.decst], w=[C.decbuf[g]])

    if os.environ.get("KSTOP") == "ph1":
        return
    P.barrier()
    h1 = C.h1T.t
    sQ, sK, sKE, sV = C.sQ, C.sK, C.sKE, C.sV
    sQ.t, sK.t, sKE.t, sV.t = h1[:, 0:16, :], h1[:, 16:32, :], h1[:, 32:48, :], h1[:, 48:64, :]
    for d in range(2):
        order = list(range(NG)) if d == 0 else [0] + list(range(NG - 1, 0, -1))
        for hh in range(2):
            P.memset("dve", C.S[hh][:], 0.0, [C.S[hh]])
            P.memset("pool", C.Sb[hh][:], 0.0, [C.Sb[hh]])
        mask = C.trif if d == 0 else C.trib
        for g in order:
            c0 = g * W
            sDEC = C.sDEC.next()
            for buf, ty in ((sQ, 3 * d + 0), (sK, 3 * d + 1), (sKE, 3 * d + 2), (sV, 6)):
                P.dma("sp", buf.t, C.SCR[ty][:, :, c0:c0 + W].rearrange("h p n -> p h n"),
                      r=C.scrbuf[g][ty], w=[buf])
            P.dma("sp", sDEC[:], C.DEC[:, d, :, g * (W // CH):(g + 1) * (W // CH)], r=[C.decbuf[g]], w=[sDEC])
            tiles = [0, 1] if d == 0 else [1, 0]
            for t in tiles:
                tc0 = t * 128
                chunks = list(range(NCH)) if d == 0 else list(range(NCH - 1, -1, -1))
                ketm = [None, None]
                vms = [None, None]
                vtms = [None, None]
                for hh in range(2):
                    for hq in range(8):
                        h = hh * 8 + hq
                        P.tr(C.psT[:, hq * 128:(hq + 1) * 128], sKE[:, h, tc0:tc0 + 128], C.identb[:],
                             [sKE, C.identb], [C.psT])
                    ketm[hh] = C.ketm.next()
                    P.cp("act", ketm[hh][:], C.psT[:], [C.psT], [ketm[hh]])
                    for hq in range(8):
                        h = hh * 8 + hq
                        P.tr(C.psT[:, hq * 128:(hq + 1) * 128], sV[:, h, tc0:tc0 + 128], C.identb[:],
                             [sV, C.identb], [C.psT])
                    vtms[hh] = C.vtm.next()
                    P.cp("act", vtms[hh][:], C.psT[:], [C.psT], [vtms[hh]])
                    for q4 in range(2):
                        psa = C.psA.next()
                        for j in range(4):
                            h = hh * 8 + q4 * 4 + j
                            P.mm(psa[:, j * 128:(j + 1) * 128], sK[:, h, tc0:tc0 + 128], sQ[:, h, tc0:tc0 + 128],
                                 True, True, [sK, sQ], [psa])
                        am = C.am.next()
                        P.tt("dve", am[:], psa[:], mask[:], ALU.mult, [psa, mask], [am])
                        for j in range(4):
                            hq = q4 * 4 + j
                            P.mm(C.psO[hh][:, hq, :], vtms[hh][:, hq * 128:(hq + 1) * 128],
                                 am[:, j * 128:(j + 1) * 128], j == 0, False, [vtms[hh], am], [C.psO[hh]],
                                 skip=True)
                for c in chunks:
                    for hh in range(2):
                        for hq in range(8):
                            h = hh * 8 + hq
                            P.mm(C.psO[hh][:, hq, c * CH:(c + 1) * CH], C.Sb[hh][:, hq, :],
                                 sQ[:, h, tc0 + c * CH:tc0 + (c + 1) * CH], False, True,
                                 [C.Sb[hh], sQ], [C.psO[hh]], skip=True)
                        ci = t * NCH + c
                        vm = C.vm.next()
                        P.ts("pool", vm[:], vtms[hh][:], C.rowm[:, c:c + 1], ALU.mult, [vtms[hh], C.rowm], [vm])
                        P.tt("dve", C.S[hh][:], C.S[hh][:],
                             sDEC[:, hh * 8:(hh + 1) * 8, ci].unsqueeze(2).to_broadcast([128, 8, 128]),
                             ALU.mult, [C.S[hh], sDEC], [C.S[hh]])
                        for q4 in range(2):
                            for j in range(4):
                                hq = q4 * 4 + j
                                P.mm(C.psS[:, j, :], ketm[hh][:, hq * 128:(hq + 1) * 128],
                                     vm[:, hq * 128:(hq + 1) * 128], True, True,
                                     [ketm[hh], vm], [C.psS])
                            P.tt("dve", C.S[hh][:, q4 * 4:(q4 + 1) * 4, :], C.S[hh][:, q4 * 4:(q4 + 1) * 4, :],
                                 C.psS[:], ALU.add, [C.S[hh], C.psS], [C.S[hh]])
                        P.cp("pool", C.Sb[hh][:], C.S[hh][:], [C.S[hh]], [C.Sb[hh]])
                for hh in range(2):
                    ob = C.osb.next()
                    P.cp("act", ob[:], C.psO[hh][:], [C.psO[hh]], [ob])
                    P.dma("act", C.OSC[d, hh * 8:(hh + 1) * 8, :, c0 + tc0:c0 + tc0 + 128].rearrange("h p n -> p h n"),
                          ob[:], r=[ob], w=[C.obuf[d][g][t * 2 + hh]])

    if os.environ.get("KSTOP") == "ph3":
        return
    P.barrier()
    for g in range(NG):
        if g == 0 and not emit_ctx:
            continue
        s = 1 if g == 0 else 0
        c0 = g * W
        P.dma("sp", C.yb[:], C.OSC[0, :, :, c0:c0 + W].rearrange("h p n -> p h n"), r=C.obuf[0][g], w=[C.yb])
        P.dma("sp", C.hg[:], C.OSC[1, :, :, c0:c0 + W].rearrange("h p n -> p h n"), r=C.obuf[1][g], w=[C.hg])
        sgbs = []
        for hh in range(2):
            sgb = C.st8.next()
            P.dma("sp", sgb[:], C.SCR[7][hh * 8:(hh + 1) * 8, :, c0:c0 + W].rearrange("h p n -> p h n"),
                  r=C.scrbuf[g][7], w=[sgb])
            sgbs.append(sgb)
        pend = None
        for h in range(KC):
            sgb = sgbs[h // 8]
            P.tt("dve", C.yb[:, h, :], C.yb[:, h, :], C.hg[:, h, :], ALU.add, [C.yb, C.hg], [C.yb])
            sq = C.sq.next()
            P.act(sq[:], C.yb[:, h, :], AF.Square, [C.yb], [sq])
            psn = C.psA.next()
            P.mm(psn[:, 0:W], C.ones[:], sq[:], True, True, [C.ones, sq], [psn])
            rs = C.rs.next()
            P.ts("dve", rs[:], psn[:, 0:W], 128.0 * EPS, ALU.add, [psn], [rs])
            P.act(rs[:], rs[:], AF.Ln, [rs], [rs])
            P.act(rs[:], rs[:], AF.Exp, [rs], [rs], scale=-0.5)
            if pend is not None:
                pend()

            def fin(h=h, rs=rs, sgb=sgb):
                t = C.tmpf.next()
                P.tt("dve", t[:], C.yb[:, h, :], rs[:], ALU.mult, [C.yb, rs], [t])
                P.stt("dve", C.uT[:, h, :], t[:], C.aon[:, jA, h:h + 1], sgb[:, h % 8, :], ALU.mult, ALU.mult,
                      [t, C.aon, sgb], [C.uT])
            pend = fin
        pend()
        C.uTx = C.uT
        layer_tail(C, i, g, s, src, wout, last, w1_d, w2_d)


def host_consts():
    c = np.zeros((128, 128 + W + 1024 + NCH), np.float32)
    c[:, 0:128] = np.eye(128, dtype=np.float32)
    m = np.ones((128, W), np.float32)
    m[:, 0::CH] = 0.0
    c[:, 128:128 + W] = m
    s = np.arange(128)[:, None]
    t = np.arange(128)[None, :]
    same = (s // CH) == (t // CH)
    trif = (same & (s <= t)).astype(np.float32)
    trib_ = (same & (s >= t)).astype(np.float32)
    c[:, 128 + W:128 + W + 512] = np.tile(trif, (1, 4))
    c[:, 128 + W + 512:128 + W + 1024] = np.tile(trib_, (1, 4))
    for cc in range(NCH):
        c[cc * CH:(cc + 1) * CH, 128 + W + 1024 + cc] = 1.0
    return c, np.tile(trib_, (1, 4)).astype(np.float32)


def colmajor(v):
    v = np.asarray(v, np.float32)
    lead = v.shape[:-1]
    a = v.reshape(*lead, KC, 128)
    a = np.moveaxis(a, -1, 0)
    return np.ascontiguousarray(a)


def make_inputs(b, x, c, ctx, c_ctx, ada_w, ada_b, norm_g, mlp_w1, mlp_w2, a_w_in, a_lb_logits, a_onorm,
                a_w_out, kinds, b_w_qkv=None, b_sink=None, b_w_out=None, c_w_in=None, c_conv_w=None,
                c_conv_b=None, c_dt_bias=None, c_a_log=None, c_d=None, c_onorm=None, c_w_out=None):
    depth = len(kinds)
    TL = x.shape[1]
    seq = np.concatenate([ctx[b], x[b]], axis=0)
    hT0 = np.ascontiguousarray(seq.T.reshape(KC, 128, seq.shape[0]))
    ccol = np.stack([colmajor(c[b]), colmajor(c_ctx)], axis=-1)
    adab = np.ascontiguousarray(np.moveaxis(np.asarray(ada_b[:depth]).reshape(depth, 96, 128), -1, 0))
    ng = colmajor(norm_g[:depth])
    consts, trib_ = host_consts()
    m = {
        "hT0": hT0, "ccol": np.ascontiguousarray(ccol), "ada_w": np.ascontiguousarray(ada_w[:depth]),
        "adab": adab, "ng": ng, "mlp_w1": np.ascontiguousarray(mlp_w1[:depth]),
        "mlp_w2": np.ascontiguousarray(mlp_w2[:depth]), "consts": consts, "trib": trib_,
    }
    nA = sum(1 for k in kinds if k == "A")
    if nA:
        m["lbl"] = colmajor(a_lb_logits)
        m["aon"] = colmajor(a_onorm[:nA])
        m["a_w_in"] = np.ascontiguousarray(a_w_in[:nA])
        m["a_w_out"] = np.ascontiguousarray(a_w_out[:nA])
    nB = sum(1 for k in kinds if k == "B")
    if nB:
        m["b_w_qkv"] = np.ascontiguousarray(b_w_qkv[:nB])
        m["b_w_out"] = np.ascontiguousarray(b_w_out[:nB])
        m["bsink"] = np.ascontiguousarray(np.broadcast_to(np.asarray(b_sink[:nB], np.float32).reshape(1, -1),
                                                          (128, nB * 16)))
        m["swac"], m["ropec"], m["ropes"] = swa_consts(TL)
    nC = sum(1 for k in kinds if k == "C")
    if nC:
        m["c_w_in"] = np.ascontiguousarray(c_w_in[:nC])
        m["c_w_out"] = np.ascontiguousarray(c_w_out[:nC])
        m["cdtb"] = np.ascontiguousarray(np.asarray(c_dt_bias[:nC], np.float32).reshape(nC, 128).T)
        m["calog"] = np.ascontiguousarray(np.asarray(c_a_log[:nC], np.float32).reshape(nC, 128).T)
        m["cdrep"] = np.ascontiguousarray(np.broadcast_to(np.asarray(c_d[:nC], np.float32).reshape(1, -1),
                                                          (128, nC * 64)))
        cw = np.asarray(c_conv_w[:nC], np.float32).reshape(nC, 5, 48, 128)
        m["cconvw"] = np.ascontiguousarray(cw.transpose(3, 0, 2, 1))
        cb = np.asarray(c_conv_b[:nC], np.float32).reshape(nC, 48, 128)
        m["cconvb"] = np.ascontiguousarray(cb.transpose(2, 0, 1))
        on = np.asarray(c_onorm[:nC], np.float32).reshape(nC, 32, 128)
        m["conorm"] = np.ascontiguousarray(on.transpose(2, 0, 1))
        si = np.arange(128)[:, None]
        ti = np.arange(128)[None, :]
        m["tri128"] = np.ascontiguousarray(np.concatenate([(si <= ti), (si >= ti)], axis=1).astype(np.float32))
    return m


def swa_consts(TL):
    c = np.zeros((128, 128 + 1024), np.float32)
    pm = np.zeros((128, 128), np.float32)
    for p in range(128):
        if p % 64 < 32:
            pm[p, p + 32] = -1.0
        else:
            pm[p, p - 32] = 1.0
    c[:, 0:128] = pm.T
    sidx = np.arange(128)[:, None]
    qidx = np.arange(128)[None, :]
    c[:, 128:640] = np.tile((sidx >= qidx).astype(np.float32), (1, 4))
    c[:, 640:1152] = np.tile((sidx <= qidx).astype(np.float32), (1, 4))
    inv = (10000.0 ** (-np.arange(32, dtype=np.float32) / 32.0)).astype(np.float32)
    tpos = np.arange(TL)
    row = (tpos // 64).astype(np.float32)
    col = (tpos % 64).astype(np.float32)
    ang = np.zeros((128, TL), np.float32)
    for p in range(128):
        base = row if p < 64 else col
        ang[p] = base * inv[p % 32]
    return c, np.cos(ang).astype(np.float32), np.sin(ang).astype(np.float32)


def run(inputs, kinds, ncores):
    x = np.asarray(inputs["x"], np.float32)
    B, TL, _ = x.shape
    nc = build(TL, kinds)
    args = {k: np.asarray(v, np.float32) for k, v in inputs.items()}
    in_maps = [make_inputs(b % B, kinds=kinds, **args) for b in range(ncores)]
    res = run_bass_kernel_spmd(nc, in_maps, core_ids=list(range(ncores)))
    out = np.empty((B, TL, D), np.float32)
    for b in range(B):
        o = res.results[b]["outT"]
        out[b] = o.reshape(D, TL).T
    return out


def kernel(**inputs):
    kinds = ["A", "B", "C", "A"]
    return run(inputs, kinds, 4)
```

```python
import math
import os
import numpy as np
from contextlib import ExitStack
import concourse.bass as bass
import concourse.mybir as mybir
from concourse.bass_utils import run_bass_kernel_spmd

F32 = mybir.dt.float32
BF16 = mybir.dt.bfloat16
AF = mybir.ActivationFunctionType
ALU = mybir.AluOpType

D = 2048
KC = 16
DFF = 8192
TCTX = 256
W = 256
EPS = 1e-6
KR = 8
CH = 16
NCH = 128 // CH


class Buf:
    __slots__ = ("t", "wtok", "readers")

    def __init__(self, t=None):
        self.t = t
        self.wtok = None
        self.readers = {}

    def __getitem__(self, k):
        return self.t[k]


class Rot:
    def __init__(self, bufs):
        self.bufs = bufs
        self.i = 0

    def next(self):
        b = self.bufs[self.i % len(self.bufs)]
        self.i += 1
        return b


class Arena:
    def __init__(self, P, name, n, dt):
        self.buf = P.sb(name, [128, n], dt)
        self.n = n
        self.off = 0

    def reset(self):
        self.off = 0

    def take(self, shape):
        n = int(np.prod(shape))
        assert self.off + n <= self.n, (self.off, n, self.n)
        ap = self.buf.t[:, self.off:self.off + n]
        self.off += n
        if len(shape) == 2:
            ap = ap.rearrange("p (a b) -> p a b", b=shape[1])
        elif len(shape) == 3:
            ap = ap.rearrange("p (a b c) -> p a b c", b=shape[1], c=shape[2])
        return Buf(ap)

    def takes(self, shape, n):
        return Rot([self.take(shape) for _ in range(n)])


class Prog:
    ENG = ("pe", "act", "dve", "pool", "sp")
    DMAQ = ("sp", "act", "pool")

    def __init__(self, nc, es):
        self.nc = nc
        self.es = es
        self.semh = {}
        for k in self.ENG:
            self.semh[k] = es.enter_context(nc.semaphore("s_" + k))
        for q in self.DMAQ:
            for i in range(KR):
                self.semh[("d", q, i)] = es.enter_context(nc.semaphore(f"d_{q}_{i}"))
        self.cnt = {k: 0 for k in self.ENG}
        self.dcnt = {q: 0 for q in self.DMAQ}
        self.seen = {k: {} for k in self.ENG}
        self.ops = {k: [] for k in self.ENG}
        self.nuniq = 0

    def sb(self, name, shape, dt):
        self.nuniq += 1
        return Buf(self.nc.alloc_sbuf_tensor(f"sb_{name}_{self.nuniq}", list(shape), dt))

    def sbs(self, name, shape, dt, n):
        return Rot([self.sb(name, shape, dt) for _ in range(n)])

    def ps(self, name, shape, dt):
        self.nuniq += 1
        return Buf(self.nc.alloc_psum_tensor(f"ps_{name}_{self.nuniq}", list(shape), dt))

    def _deps(self, eng, r, w):
        need = {}
        for b in r:
            if b.wtok is not None:
                k, v = b.wtok
                if need.get(k, 0) < v:
                    need[k] = v
        for b in w:
            if b.wtok is not None:
                k, v = b.wtok
                if need.get(k, 0) < v:
                    need[k] = v
            for k, v in b.readers.items():
                if need.get(k, 0) < v:
                    need[k] = v
        seen = self.seen[eng]
        waits = []
        for k, v in need.items():
            if k == eng and eng == "pe":
                continue
            if seen.get(k, 0) >= v:
                continue
            seen[k] = v
            waits.append((k, v))
        return waits

    def op(self, eng, fn, r=(), w=()):
        waits = self._deps(eng, r, w)
        self.cnt[eng] += 1
        tok = (eng, self.cnt[eng])
        self.ops[eng].append((waits, fn, (eng, 1)))
        for b in r:
            if b.readers.get(eng, 0) < tok[1]:
                b.readers[eng] = tok[1]
        for b in w:
            b.wtok = tok
            b.readers = {}
        return tok

    def dma(self, q, out, in_, r=(), w=(), **kw):
        i = self.dcnt[q]
        self.dcnt[q] += 1
        sk = ("d", q, i % KR)
        val = 16 * (i // KR + 1)
        waits = self._deps(q, r, w)
        if i >= KR and self.seen[q].get(sk, 0) < val - 16:
            self.seen[q][sk] = val - 16
            waits.append((sk, val - 16))
        self.ops[q].append((waits, (lambda e: e.dma_start(out=out, in_=in_, **kw)), (sk, 16)))
        tok = (sk, val)
        for b in r:
            if b.readers.get(sk, 0) < val:
                b.readers[sk] = val
        for b in w:
            b.wtok = tok
            b.readers = {}
        return tok

    def barrier(self):
        toks = [(k, self.cnt[k]) for k in ("pe", "act", "dve", "pool") if self.cnt[k] > 0]
        for q in self.DMAQ:
            n = self.dcnt[q]
            for j in range(KR):
                if n > j:
                    last = ((n - 1 - j) // KR) * KR + j
                    toks.append((("d", q, j), 16 * (last // KR + 1)))
        for eng in self.ENG:
            seen = self.seen[eng]
            waits = []
            for k, v in toks:
                if seen.get(k, 0) < v:
                    seen[k] = v
                    waits.append((k, v))
            if waits:
                self.ops[eng].append((waits, None, None))

    def finish(self):
        waits = []
        for q in self.DMAQ:
            n = self.dcnt[q]
            for j in range(KR):
                if n > j:
                    last = ((n - 1 - j) // KR) * KR + j
                    waits.append((("d", q, j), 16 * (last // KR + 1)))
        for k in ("pe", "act", "dve", "pool"):
            if self.cnt[k] > 0:
                waits.append((k, self.cnt[k]))
        self.ops["sp"].append((waits, None, None))

    def emit(self):
        nc = self.nc
        block = self.es.enter_context(nc.Block())
        semh = self.semh

        def replay(key, e):
            for waits, fn, inc in self.ops[key]:
                if fn is None:
                    for (k, v) in waits:
                        e.wait_ge(semh[k], v)
                    continue
                for (k, v) in waits[1:]:
                    e.wait_ge(semh[k], v)
                ins = fn(e)
                if waits:
                    ins._wait_ge(semh[waits[0][0]], waits[0][1])
                ins.then_inc(semh[inc[0]], inc[1])

        @block.tensor
        def _(e):
            replay("pe", e)

        @block.scalar
        def _(e):
            replay("act", e)

        @block.vector
        def _(e):
            replay("dve", e)

        @block.gpsimd
        def _(e):
            replay("pool", e)

        @block.sync
        def _(e):
            replay("sp", e)

    def tt(self, eng, out, in0, in1, op, r, w):
        return self.op(eng, lambda e: e.tensor_tensor(out=out, in0=in0, in1=in1, op=op), r, w)

    def ts(self, eng, out, in0, s1, op0, r, w, s2=None, op1=None):
        if op1 is None:
            return self.op(eng, lambda e: e.tensor_scalar(out=out, in0=in0, scalar1=s1, scalar2=None, op0=op0), r, w)
        return self.op(eng, lambda e: e.tensor_scalar(out=out, in0=in0, scalar1=s1, scalar2=s2, op0=op0, op1=op1), r, w)

    def stt(self, eng, out, in0, scalar, in1, op0, op1, r, w):
        return self.op(eng, lambda e: e.scalar_tensor_tensor(out=out, in0=in0, scalar=scalar, in1=in1, op0=op0, op1=op1), r, w)

    def act(self, out, in_, func, r, w, scale=None, bias=None):
        kw = {}
        if scale is not None:
            kw["scale"] = scale
        if bias is not None:
            kw["bias"] = bias
        return self.op("act", lambda e: e.activation(out=out, in_=in_, func=func, **kw), r, w)

    def cp(self, eng, out, in_, r, w):
        if eng == "act":
            return self.op("act", lambda e: e.copy(out=out, in_=in_), r, w)
        return self.op(eng, lambda e: e.tensor_copy(out=out, in_=in_), r, w)

    def mm(self, out, lhsT, rhs, start, stop, r, w, skip=False):
        if skip:
            return self.op("pe", lambda e: e.matmul(out, lhsT=lhsT, rhs=rhs, start=start, stop=stop,
                                                    skip_group_check=True), r, w)
        return self.op("pe", lambda e: e.matmul(out, lhsT=lhsT, rhs=rhs, start=start, stop=stop), r, w)

    def tr(self, out, in_, ident, r, w):
        return self.op("pe", lambda e: e.transpose(out=out, in_=in_, identity=ident), r, w)

    def memset(self, eng, ap, val, w):
        return self.op(eng, lambda e: e.memset(ap, val), (), w)

    def scan(self, out, d0, d1, r, w):
        return self.op("dve", lambda e: e.tensor_tensor_scan(out=out, data0=d0, data1=d1, initial=0.0,
                                                              op0=ALU.mult, op1=ALU.add), r, w)


class Ctx:
    pass


def build(TL, kinds):
    depth = len(kinds)
    nA = sum(1 for k in kinds if k == "A")
    NT = TCTX + TL
    NG = NT // W
    nc = bass.Bass("TRN2", target_bir_lowering=False)
    es = ExitStack()
    P = Prog(nc, es)
    C = Ctx()
    C.P, C.nc, C.NT, C.NG, C.TL, C.depth = P, nc, NT, NG, TL, depth

    def din(name, shape, dt=F32):
        return nc.dram_tensor(name, list(shape), dt, kind="ExternalInput").ap()

    def dscr(name, shape, dt):
        return nc.dram_tensor(name, list(shape), dt, kind="Internal").ap()

    hT0 = din("hT0", [KC, 128, NT])
    ccol_d = din("ccol", [128, KC, 2])
    adaw_d = din("ada_w", [depth, D, 6 * D])
    adab_d = din("adab", [128, depth, 96])
    ng_d = din("ng", [128, depth, 4, KC])
    w1_d = din("mlp_w1", [depth, D, DFF])
    w2_d = din("mlp_w2", [depth, DFF, D])
    consts_d = din("consts", [128, 128 + W + 1024 + NCH])
    trib_d = din("trib", [128, 512])
    if nA:
        lbl_d = din("lbl", [128, depth, 2, KC])
        aon_d = din("aon", [128, nA, KC])
        awin_d = din("a_w_in", [nA, D, 5 * D])
        awout_d = din("a_w_out", [nA, D, D])
    nB = sum(1 for k in kinds if k == "B")
    if nB:
        C.bwqkv_d = din("b_w_qkv", [nB, D, 3072])
        C.bwout_d = din("b_w_out", [nB, D, D])
        C.bsink_d = din("bsink", [128, nB * 16])
        C.swac_d = din("swac", [128, 128 + 1024])
        C.ropec_d = din("ropec", [128, TL])
        C.ropes_d = din("ropes", [128, TL])
        C.QT = dscr("qT", [KC, 128, NT], BF16)
        C.KT = dscr("kT", [4, 128, NT], BF16)
        C.VTM = dscr("vtm", [NT, 512], BF16)
        C.qbuf = [[Buf() for _ in range(4)] for _ in range(NG)]
        C.kbuf = [Buf() for _ in range(NG)]
        C.vbuf = [[Buf() for _ in range(2)] for _ in range(NG)]
    nC = sum(1 for k in kinds if k == "C")
    if nC:
        C.cwin_d = din("c_w_in", [nC, D, 10368])
        C.cwout_d = din("c_w_out", [nC, 4096, D])
        C.cdtb_d = din("cdtb", [128, nC])
        C.calog_d = din("calog", [128, nC])
        C.cd_d = din("cdrep", [128, nC * 64])
        C.cconvw_d = din("cconvw", [128, nC, 48, 5])
        C.cconvb_d = din("cconvb", [128, nC, 48])
        C.conorm_d = din("conorm", [128, nC, 32])
        tri128_d = din("tri128", [128, 256])
        C.XP = dscr("xp", [48, 128, NT], F32)
        C.SZ = dscr("sz", [NT, 4096], BF16)
        C.DTM = dscr("dtm", [NT, 256], F32)
        C.XTM = dscr("xtm", [NT, 4096], BF16)
        C.BTM = dscr("btm", [NT, 1024], BF16)
        C.BT = dscr("bT", [8, 128, NT], BF16)
        C.CT = dscr("cT", [8, 128, NT], BF16)
        C.Y = [dscr(f"ysc{d_}", [NT, 4096], F32) for d_ in range(2)]
        C.xpbuf = [[Buf() for _ in range(12)] for _ in range(NG)]
        C.dtmbuf = [[Buf() for _ in range(2)] for _ in range(NG)]
        C.szbuf = [[[Buf() for _ in range(8)] for _ in range(2)] for _ in range(NG)]
        C.xtbuf = [[[Buf() for _ in range(4)] for _ in range(2)] for _ in range(NG)]
        C.btmbuf = [[Buf() for _ in range(2)] for _ in range(NG)]
        C.btbuf = [Buf() for _ in range(NG)]
        C.ctbuf = [Buf() for _ in range(NG)]
        C.ybuf = [[[[Buf() for _ in range(8)] for _ in range(2)] for _ in range(NG)] for _ in range(2)]
    outT = nc.dram_tensor("outT", [KC, 128, TL], F32, kind="ExternalOutput").ap()

    hT = dscr("hT", [KC, 128, NT], F32)
    C.hbuf = [Buf() for _ in range(NG)]
    if nA:
        SCR = [dscr(f"scrA{ty}", [KC, 128, NT], BF16) for ty in range(8)]
        DEC = dscr("decA", [128, 2, KC, NT // CH], F32)
        OSC = dscr("oA", [2, KC, 128, NT], F32)
        C.scrbuf = [[[Buf() for _ in range(KC)] for _ in range(8)] for _ in range(NG)]
        C.decbuf = [Buf() for _ in range(NG)]
        C.obuf = [[[Buf() for _ in range(4)] for _ in range(NG)] for _ in range(2)]

    ones = P.sb("ones", [128, 128], F32)
    identf = P.sb("identf", [128, 128], F32)
    identb = P.sb("identb", [128, 128], BF16)
    mskr = P.sb("mskr", [128, W], F32)
    trif = P.sb("trif", [128, 512], F32)
    trib = P.sb("trib", [128, 512], F32)
    rowm = P.sb("rowm", [128, NCH], F32)
    sT = P.sb("sT", [128, KC, 2], F32)
    adab = P.sb("adab", [128, depth, 96], F32)
    ngs = P.sb("ngs", [128, depth, 4, KC], F32)
    mcol = P.sb("mcol", [128, 96, 2], F32)
    mods_all = P.sb("mods", [128, 2 * depth, 6, KC], F32)
    mods_l = [Buf(mods_all.t[:, 2 * li:2 * li + 2]) for li in range(depth)]
    mods = mods_l[0]

    P.memset("pool", ones[:], 1.0, [ones])
    P.dma("sp", identf[:], consts_d[:, 0:128], w=[identf])
    P.dma("sp", mskr[:], consts_d[:, 128:128 + W], w=[mskr])
    P.dma("sp", trif[:], consts_d[:, 128 + W:128 + W + 512], w=[trif])
    P.dma("sp", rowm[:], consts_d[:, 128 + W + 1024:128 + W + 1024 + NCH], w=[rowm])
    P.dma("sp", trib[:], trib_d, w=[trib])
    P.cp("dve", identb[:], identf[:], [identf], [identb])
    P.dma("sp", sT[:], ccol_d, w=[sT])
    P.act(sT[:], sT[:], AF.Silu, [sT], [sT])
    P.dma("sp", adab[:], adab_d, w=[adab])
    P.dma("sp", ngs[:], ng_d, w=[ngs])
    P.ts("dve", ngs[:], ngs[:], math.sqrt(D), ALU.mult, [ngs], [ngs])

    C.hg = P.sb("hg", [128, KC, W], F32)
    C.yb = P.sb("yb", [128, KC, W], F32)
    C.uT = P.sb("uT", [128, KC, W], BF16)
    C.h1T = P.sb("h1T", [128, 64, W], BF16)
    C.wb = P.sbs("wb", [128, KC * 512], BF16, 2)
    C.rs = P.sbs("rs", [128, W], F32, 2)
    C.tmpf = P.sbs("tmpf", [128, W], F32, 4)
    C.sq = P.sbs("sq", [128, W], F32, 3)
    C.psA = Rot([P.ps("psA", [128, 512], F32) for _ in range(2)])
    C.psN = C.psA.bufs[0]
    C.ones, C.identb, C.mskr, C.trif, C.trib, C.rowm = ones, identb, mskr, trif, trib, rowm
    C.mods, C.mcol, C.sT, C.adab, C.ngs = mods, mcol, sT, adab, ngs
    C.hT, C.hT0, C.outT = hT, hT0, outT
    C.wada = P.sbs("wada", [128, 1536], F32, 2)

    if nA:
        lbl = P.sb("lbl", [128, depth, 2, KC], F32)
        lbe = P.sb("lbe", [128, depth, 2, KC], F32)
        lbs = P.sb("lbs", [128, 2, KC], F32)
        C.lb = P.sb("lb", [128, depth, 2, KC], F32)
        C.oml = P.sb("oml", [128, depth, 2, KC], F32)
        C.aon = P.sb("aon", [128, nA, KC], F32)
        P.dma("sp", lbl[:], lbl_d, w=[lbl])
        P.dma("sp", C.aon[:], aon_d, w=[C.aon])
        P.ts("dve", C.aon[:], C.aon[:], math.sqrt(128.0), ALU.mult, [C.aon], [C.aon])
        P.act(lbe[:], lbl[:], AF.Exp, [lbl], [lbe])
        P.cp("dve", lbs[:], lbe[:, 0], [lbe], [lbs])
        for i in range(1, depth):
            P.tt("dve", lbs[:], lbs[:], lbe[:, i], ALU.add, [lbe, lbs], [lbs])
        P.op("dve", lambda e: e.reciprocal(out=lbs[:], in_=lbs[:]), [lbs], [lbs])
        for i in range(depth):
            P.tt("dve", lbe[:, i], lbe[:, i], lbs[:], ALU.mult, [lbe, lbs], [lbe])
        P.memset("dve", C.lb[:, 0], 0.0, [C.lb])
        for i in range(1, depth):
            P.tt("dve", C.lb[:, i], C.lb[:, i - 1], lbe[:, i - 1], ALU.add, [C.lb, lbe], [C.lb])
        P.ts("dve", C.oml[:], C.lb[:], -1.0, ALU.mult, [C.lb], [C.oml], s2=1.0, op1=ALU.add)
        C.SCR, C.DEC, C.OSC = SCR, DEC, OSC
        C.awin_d, C.awout_d = awin_d, awout_d

    C.identf = identf
    if nC:
        t0_ = P.sb("tri128f", [128, 128], F32)
        t1_ = P.sb("tri128b", [128, 128], F32)
        P.dma("sp", t0_[:], tri128_d[:, 0:128], w=[t0_])
        P.dma("sp", t1_[:], tri128_d[:, 128:256], w=[t1_])
        C.tri128 = [t0_, t1_]
    C.arf = Arena(P, "arf", 9216, F32)
    C.arb = Arena(P, "arb", 14336, BF16)
    C.psO = [P.ps(f"psO{h}", [128, 8, 128], F32) for h in range(2)]
    C.psS = P.ps("psS", [128, 4, 128], F32)
    C.psT = P.ps("psT", [128, 1024], BF16)

    C.S_w1, C.S_w2, C.S_in, C.S_out = [], [], [], []
    ja = jb = jc = 0
    for i, kind in enumerate(kinds):
        if kind == "A":
            C.S_in.append(prep_weight(C, f"S_in{i}", awin_d[ja], D, 5 * D))
            C.S_out.append(prep_weight(C, f"S_out{i}", awout_d[ja], D, D))
            ja += 1
        elif kind == "B":
            C.S_in.append(prep_weight(C, f"S_in{i}", C.bwqkv_d[jb], D, 3072))
            C.S_out.append(prep_weight(C, f"S_out{i}", C.bwout_d[jb], D, D))
            jb += 1
        else:
            C.S_in.append(prep_weight(C, f"S_in{i}", C.cwin_d[jc], D, 10368))
            C.S_out.append(prep_weight(C, f"S_out{i}", C.cwout_d[jc], 4096, D))
            jc += 1
        C.S_w1.append(prep_weight(C, f"S_w1{i}", w1_d[i], D, DFF))
        C.S_w2.append(prep_weight(C, f"S_w2{i}", w2_d[i], DFF, D))

    for i in range(depth):
        C.mods = mods_l[i]
        ada_phase(C, i, adaw_d)
    P.barrier()
    jA = 0
    jB = 0
    jC = 0
    for i, kind in enumerate(kinds):
        C.mods = mods_l[i]
        emit_ctx = i < depth - 1
        src = hT0 if i == 0 else hT
        last = (i == depth - 1)
        if kind == "A":
            hgrn2_layer(C, i, jA, src, last, emit_ctx, w1_d, w2_d)
            jA += 1
        elif kind == "B":
            swa_layer(C, i, jB, src, last, emit_ctx, w1_d, w2_d)
            jB += 1
        elif kind == "C":
            ssd_layer(C, i, jC, src, last, emit_ctx, w1_d, w2_d)
            jC += 1
        else:
            raise NotImplementedError(kind)
        mlp_phase(C, i, emit_ctx, last, w1_d, w2_d)
    P.finish()
    P.emit()
    return nc


def ada_phase(C, i, adaw_d):
    P = C.P
    for nb in range(8):
        for kc in range(KC):
            wt = C.wada.next()
            P.dma("sp", wt[:], adaw_d[i, kc * 128:(kc + 1) * 128, nb * 1536:(nb + 1) * 1536], w=[wt])
            for j in range(12):
                P.mm(C.psN[:, 2 * j:2 * j + 2], wt[:, j * 128:(j + 1) * 128], C.sT[:, kc, :],
                     kc == 0 and j == 0, kc == KC - 1, [wt, C.sT], [C.psN], skip=True)
        P.tt("dve", C.mcol[:, nb * 12:(nb + 1) * 12, :],
             C.psN[:, 0:24].rearrange("p (j s) -> p j s", s=2),
             C.adab[:, i, nb * 12:(nb + 1) * 12].unsqueeze(2).to_broadcast([128, 12, 2]),
             ALU.add, [C.psN, C.adab], [C.mcol])
    m = lambda j, s: C.mcol[:, j * 16:(j + 1) * 16, s]
    g = lambda n: C.ngs[:, i, n, :]
    rr = [C.mcol, C.ngs]
    for s in range(2):
        P.stt("dve", C.mods[:, s, 0, :], m(1, s), 1.0, g(0), ALU.add, ALU.mult, rr, [C.mods])
        P.cp("dve", C.mods[:, s, 1, :], m(0, s), rr, [C.mods])
        P.tt("dve", C.mods[:, s, 2, :], m(2, s), g(1), ALU.mult, rr, [C.mods])
        P.stt("dve", C.mods[:, s, 3, :], m(4, s), 1.0, g(2), ALU.add, ALU.mult, rr, [C.mods])
        P.cp("dve", C.mods[:, s, 4, :], m(3, s), rr, [C.mods])
        P.tt("dve", C.mods[:, s, 5, :], m(5, s), g(3), ALU.mult, rr, [C.mods])


def rstd_of(C, srcbuf, n_chunks, addc):
    P = C.P
    psn = C.psA.next()
    for kc in range(n_chunks):
        sq = C.sq.next()
        P.act(sq[:], srcbuf[:, kc, :], AF.Square, [srcbuf], [sq])
        P.mm(psn[:, 0:W], C.ones[:], sq[:], kc == 0, kc == n_chunks - 1, [C.ones, sq], [psn])
    rs = C.rs.next()
    P.ts("dve", rs[:], psn[:, 0:W], addc, ALU.add, [psn], [rs])
    P.act(rs[:], rs[:], AF.Ln, [rs], [rs])
    P.act(rs[:], rs[:], AF.Exp, [rs], [rs], scale=-0.5)
    return rs


def modulate(C, src, s, ia, ish):
    P = C.P
    rs = rstd_of(C, src, KC, D * EPS)
    for kc in range(KC):
        t = C.tmpf.next()
        P.tt("dve", t[:], src[:, kc, :], rs[:], ALU.mult, [src, rs], [t])
        P.ts("pool", C.uT[:, kc, :], t[:], C.mods[:, s, ia, kc:kc + 1], ALU.mult, [t, C.mods], [C.uT],
             s2=C.mods[:, s, ish, kc:kc + 1], op1=ALU.add)


def prep_weight(C, name, w2d, K, N):
    nk, nch = K // 128, N // 128
    S = C.nc.dram_tensor(name, [nch, 128, nk, 128], BF16, kind="Internal").ap()
    bufs = [Buf() for _ in range(nch)]
    for j in range(nch):
        C.P.dma("pool", S[j], w2d[:, j * 128:(j + 1) * 128].rearrange("(k p) n -> p k n", p=128), w=[bufs[j]])
    return (S, bufs, nk)


def load_w(C, wt, cslot, Sw, j0, nch, k0=0, nk=None):
    S, bufs, nkfull = Sw
    if nk is None:
        nk = nkfull
    wv = wt[:].rearrange("p (c k n) -> p c k n", k=nk, n=128)
    C.P.dma("sp", wv[:, cslot:cslot + nch], S[j0:j0 + nch, :, k0:k0 + nk, :].rearrange("c p k n -> p c k n"),
            r=bufs[j0:j0 + nch], w=[wt])
    return wv


def load_w_cols(C, wt, slot, w2d, col0, ncols, nk=KC):
    P = C.P
    P.dma("pool", wt[:].rearrange("p (k n) -> p k n", k=nk)[:, :, slot:slot + ncols],
          w2d[:, col0:col0 + ncols].rearrange("(k p) n -> p k n", p=128), w=[wt])


def resid_update(C, g, s, igg, hsrc, dst_ap):
    P = C.P
    rs = rstd_of(C, C.yb, KC, D * EPS)
    for kc in range(KC):
        t = C.tmpf.next()
        P.tt("dve", t[:], C.yb[:, kc, :], rs[:], ALU.mult, [C.yb, rs], [t])
        P.stt("dve", C.hg[:, kc, :], t[:], C.mods[:, s, igg, kc:kc + 1], C.hg[:, kc, :], ALU.mult, ALU.add,
              [t, C.mods, C.hg], [C.hg])


def mlp_sublayer(C, i, s, w1_d, w2_d):
    P = C.P
    modulate(C, C.hg, s, 3, 4)
    for jb in range(16):
        wt = C.wb.next()
        load_w_cols(C, wt, 0, w1_d[i], jb * 512, 512)
        wv = wt[:].rearrange("p (k n) -> p k n", k=KC)
        for jj in range(4):
            ps = C.psA.next()
            for kc in range(KC):
                P.mm(ps[:, 0:W], wv[:, kc, jj * 128:(jj + 1) * 128], C.uT[:, kc, :], kc == 0, kc == KC - 1,
                     [wt, C.uT], [ps])
            t = C.tmpf.next()
            P.act(t[:], ps[:, 0:W], AF.Relu, [ps], [t])
            P.tt("pool", C.h1T[:, jb * 4 + jj, :], t[:], t[:], ALU.mult, [t], [C.h1T])
    for fo in range(KC):
        wt = C.wb.next()
        load_w_cols(C, wt, 0, w2_d[i], fo * 128, 128, nk=64)
        wv = wt[:].rearrange("p (k n) -> p k n", k=64)
        ps = C.psA.next()
        for fc in range(64):
            P.mm(ps[:, 0:W], wv[:, fc, 0:128], C.h1T[:, fc, :], fc == 0, fc == 63, [wt, C.h1T], [ps])
        P.cp("act", C.yb[:, fo, :], ps[:, 0:W], [ps], [C.yb])
    resid_update(C, None, s, 5, None, None)


def layer_tail(C, i, g, s, src, wout, last, w1_d, w2_d, nkc=KC):
    P = C.P
    c0 = g * W
    ncol = 8192 // nkc
    for fb in range(D // ncol):
        wt = C.wb.next()
        wv = load_w(C, wt, 0, C.S_out[i], fb * (ncol // 128), ncol // 128)
        for jj in range(ncol // 128):
            ps = C.psA.next()
            for kc in range(nkc):
                P.mm(ps[:, 0:W], wv[:, jj, kc, :], C.uTx[:, kc, :], kc == 0, kc == nkc - 1,
                     [wt, C.uTx], [ps])
            P.cp("act", C.yb[:, fb * (ncol // 128) + jj, :], ps[:, 0:W], [ps], [C.yb])
    P.dma("sp", C.hg[:], src[:, :, c0:c0 + W].rearrange("k p n -> p k n"), r=[C.hbuf[g]], w=[C.hg])
    resid_update(C, g, s, 2, None, None)
    P.dma("act", C.hT[:, :, c0:c0 + W].rearrange("k p n -> p k n"), C.hg[:], r=[C.hg], w=[C.hbuf[g]])


def mlp_phase(C, i, emit_ctx, last, w1_d, w2_d):
    P = C.P
    P.barrier()
    WW = 512
    H = Buf(C.arf.buf.t[:, 0:8192].rearrange("p (k n) -> p k n", n=WW))
    U = Buf(C.arb.buf.t[:, 0:8192].rearrange("p (k n) -> p k n", n=WW))
    Ylo = Buf(C.hg.t[:].rearrange("p k n -> p (k n)").rearrange("p (k n) -> p k n", n=WW))
    Yhi = Buf(C.yb.t[:].rearrange("p k n -> p (k n)").rearrange("p (k n) -> p k n", n=WW))
    H1 = Buf(C.h1T.t[:].rearrange("p a b -> p (a b)").rearrange("p (k n) -> p k n", n=WW))
    w0, w1_ = C.wada.bufs[0].t, C.wada.bufs[1].t
    rs = Buf(w0[:, 0:512])
    tmps = Rot([Buf(w0[:, 512:1024]), Buf(w0[:, 1024:1536]), Buf(w1_[:, 1024:1536])])
    sqs = Rot([Buf(w1_[:, 0:512]), Buf(w1_[:, 512:1024])])
    groups = []
    if emit_ctx:
        groups.append((0, 256, 1))
    for j in range(C.TL // WW):
        groups.append((TCTX + j * WW, WW, 0))

    def Y(fo):
        return (Ylo if fo < 8 else Yhi), fo % 8

    def norm(getchunk, rbufs):
        psn = C.psA.next()
        for kc in range(KC):
            sq = sqs.next()
            P.act(sq[:, 0:wd], getchunk(kc), AF.Square, rbufs, [sq])
            P.mm(psn[:, 0:wd], C.ones[:], sq[:, 0:wd], kc == 0, kc == KC - 1, [C.ones, sq], [psn])
        P.ts("dve", rs[:, 0:wd], psn[:, 0:wd], D * EPS, ALU.add, [psn], [rs])
        P.act(rs[:, 0:wd], rs[:, 0:wd], AF.Ln, [rs], [rs])
        P.act(rs[:, 0:wd], rs[:, 0:wd], AF.Exp, [rs], [rs], scale=-0.5)

    for (c0, wd, s) in groups:
        hb = [C.hbuf[gg] for gg in range(c0 // W, (c0 + wd - 1) // W + 1)]
        P.dma("sp", H[:, :, 0:wd], C.hT[:, :, c0:c0 + wd].rearrange("k p n -> p k n"), r=hb, w=[H])
        norm(lambda kc: H[:, kc, 0:wd], [H])
        for kc in range(KC):
            t = tmps.next()
            P.tt("dve", t[:, 0:wd], H[:, kc, 0:wd], rs[:, 0:wd], ALU.mult, [H, rs], [t])
            P.ts("pool", U[:, kc, 0:wd], t[:, 0:wd], C.mods[:, s, 3, kc:kc + 1], ALU.mult, [t, C.mods], [U],
                 s2=C.mods[:, s, 4, kc:kc + 1], op1=ALU.add)
        for half in range(2):
            for jb in range(8):
                wt = C.wb.next()
                wv = load_w(C, wt, 0, C.S_w1[i], half * 32 + jb * 4, 4)
                for jj in range(4):
                    ps = C.psA.next()
                    for kc in range(KC):
                        P.mm(ps[:, 0:wd], wv[:, jj, kc, :], U[:, kc, 0:wd], kc == 0, kc == KC - 1,
                             [wt, U], [ps])
                    t = tmps.next()
                    P.act(t[:, 0:wd], ps[:, 0:wd], AF.Relu, [ps], [t])
                    P.tt("pool", H1[:, jb * 4 + jj, 0:wd], t[:, 0:wd], t[:, 0:wd], ALU.mult, [t], [H1])
            for fo in range(KC):
                wt = C.wb.next()
                wv = load_w(C, wt, 0, C.S_w2[i], fo, 1, k0=half * 32, nk=32)
                ps = C.psA.next()
                for fc in range(32):
                    P.mm(ps[:, 0:wd], wv[:, 0, fc, :], H1[:, fc, 0:wd], fc == 0, fc == 31, [wt, H1], [ps])
                yb_, yi = Y(fo)
                if half == 0:
                    P.cp("act", yb_[:, yi, 0:wd], ps[:, 0:wd], [ps], [yb_])
                else:
                    P.tt("dve", yb_[:, yi, 0:wd], yb_[:, yi, 0:wd], ps[:, 0:wd], ALU.add, [yb_, ps], [yb_])
        norm(lambda kc: Y(kc)[0][:, Y(kc)[1], 0:wd], [Ylo, Yhi])
        for kc in range(KC):
            yb_, yi = Y(kc)
            t = tmps.next()
            P.tt("dve", t[:, 0:wd], yb_[:, yi, 0:wd], rs[:, 0:wd], ALU.mult, [yb_, rs], [t])
            P.stt("dve", H[:, kc, 0:wd], t[:, 0:wd], C.mods[:, s, 5, kc:kc + 1], H[:, kc, 0:wd], ALU.mult, ALU.add,
                  [t, C.mods, H], [H])
        if last:
            P.dma("act", C.outT[:, :, c0 - TCTX:c0 - TCTX + wd].rearrange("k p n -> p k n"), H[:, :, 0:wd], r=[H])
        else:
            P.dma("act", C.hT[:, :, c0:c0 + wd].rearrange("k p n -> p k n"), H[:, :, 0:wd], r=[H], w=hb)
    P.barrier()


SCALE_B = 128.0 ** -0.5


def swa_layer(C, i, jB, src, last, emit_ctx, w1_d, w2_d):
    P = C.P
    NG, NT = C.NG, C.NT
    P.barrier()
    C.arf.reset()
    C.arb.reset()
    wqkv = C.bwqkv_d[jB]
    wout = C.bwout_d[jB]
    A, Bf = C.arf, C.arb
    rd = A.takes([512], 2)
    cosb = A.takes([W], 2)
    sinb = A.takes([W], 2)
    xf = A.takes([W], 2)
    mprev = A.take([512])
    mnext = A.take([512])
    sinkexp = A.take([16])
    pmTf = A.take([128])
    sq = Bf.take([16 * 128])
    kctx = Bf.take([4, 256])
    vctx = Bf.take([2, 512])
    kloc = Bf.take([4, 384])
    vloc = Bf.take([3, 512])
    eb = Bf.takes([512], 3)
    xb = Bf.takes([W], 2)
    stq = Bf.takes([4, W], 2)
    stv = Bf.take([4, W])
    vt = Bf.takes([512], 2)
    pmTb = Bf.take([128])
    onesb = Bf.take([128])
    psO0 = C.psO[0].t[:].rearrange("p a b -> p (a b)")
    psD = Buf(psO0[:, 0:512])
    psOt = Buf(psO0[:, 512:1024])
    P.dma("sp", pmTf[:], C.swac_d[:, 0:128], w=[pmTf])
    P.dma("sp", mprev[:], C.swac_d[:, 128:640], w=[mprev])
    P.dma("sp", mnext[:], C.swac_d[:, 640:1152], w=[mnext])
    P.dma("sp", sinkexp[:], C.bsink_d[:, jB * 16:(jB + 1) * 16], w=[sinkexp])
    P.act(sinkexp[:], sinkexp[:], AF.Exp, [sinkexp], [sinkexp])
    P.cp("dve", pmTb[:], pmTf[:], [pmTf], [pmTb])
    P.memset("pool", onesb[:], 1.0, [onesb])

    for g in range(NG):
        s = 1 if g == 0 else 0
        c0 = g * W
        P.dma("sp", C.hg[:], src[:, :, c0:c0 + W].rearrange("k p n -> p k n"), r=[C.hbuf[g]], w=[C.hg])
        modulate(C, C.hg, s, 0, 1)
        if g > 0:
            cb = cosb.next()
            sb_ = sinb.next()
            P.dma("sp", cb[:], C.ropec_d[:, c0 - TCTX:c0 - TCTX + W], w=[cb])
            P.dma("sp", sb_[:], C.ropes_d[:, c0 - TCTX:c0 - TCTX + W], w=[sb_])
        for blk in range(6):
            wt = C.wb.next()
            wv = load_w(C, wt, 0, C.S_in[i], blk * 4, 4)
            st = stq.next() if blk < 5 else stv
            for jj in range(4):
                ps = C.psA.next()
                for kc in range(KC):
                    P.mm(ps[:, 0:W], wv[:, jj, kc, :], C.uT[:, kc, :], kc == 0, kc == KC - 1,
                         [wt, C.uT], [ps])
                if blk == 5 or g == 0:
                    P.cp("act", st[:, jj, :], ps[:, 0:W], [ps], [st])
                else:
                    x = xf.next()
                    P.cp("act", x[:], ps[:, 0:W], [ps], [x])
                    xbb = xb.next()
                    P.cp("pool", xbb[:], x[:], [x], [xbb])
                    ps2 = C.psA.next()
                    P.mm(ps2[:, 0:W], pmTb[:], xbb[:], True, True, [pmTb, xbb], [ps2])
                    t1 = C.tmpf.next()
                    P.tt("dve", t1[:], x[:], cb[:], ALU.mult, [x, cb], [t1])
                    t2 = C.tmpf.next()
                    P.tt("dve", t2[:], ps2[:, 0:W], sb_[:], ALU.mult, [ps2, sb_], [t2])
                    P.tt("pool", st[:, jj, :], t1[:], t2[:], ALU.add, [t1, t2], [st])
            if blk < 4:
                P.dma("act", C.QT[blk * 4:(blk + 1) * 4, :, c0:c0 + W].rearrange("h p n -> p h n"), st[:],
                      r=[st], w=[C.qbuf[g][blk]])
            elif blk == 4:
                P.dma("act", C.KT[:, :, c0:c0 + W].rearrange("h p n -> p h n"), st[:], r=[st], w=[C.kbuf[g]])
            else:
                for t in range(2):
                    for gk in range(4):
                        P.tr(C.psT[:, gk * 128:(gk + 1) * 128], stv[:, gk, t * 128:(t + 1) * 128], C.identb[:],
                             [stv, C.identb], [C.psT])
                    v = vt.next()
                    P.cp("act", v[:], C.psT[:, 0:512], [C.psT], [v])
                    P.dma("act", C.VTM[c0 + t * 128:c0 + (t + 1) * 128, :], v[:], r=[v], w=[C.vbuf[g][t]])

    P.dma("sp", kctx[:], C.KT[:, :, 0:TCTX].rearrange("h p n -> p h n"), r=[C.kbuf[0]], w=[kctx])
    P.dma("sp", vctx[:], C.VTM[0:TCTX, :].rearrange("(t p) n -> p t n", p=128), r=C.vbuf[0], w=[vctx])
    for g in range(NG):
        if g == 0 and not emit_ctx:
            continue
        s = 1 if g == 0 else 0
        c0 = g * W
        for t in range(2):
            qc0 = c0 + t * 128
            P.dma("sp", sq[:].rearrange("p (h n) -> p h n", h=16),
                  C.QT[:, :, qc0:qc0 + 128].rearrange("h p n -> p h n"), r=C.qbuf[g], w=[sq])
            blocks = [("c", 0, 0), ("c", 1, 0)]
            if g > 0:
                lo = max(TCTX, qc0 - 128)
                hi = min(NT, qc0 + 256)
                nl = (hi - lo) // 128
                P.dma("sp", kloc[:, :, 0:hi - lo], C.KT[:, :, lo:hi].rearrange("h p n -> p h n"),
                      r=[C.kbuf[gg] for gg in range(lo // W, (hi - 1) // W + 1)], w=[kloc])
                P.dma("sp", vloc[:, 0:nl, :], C.VTM[lo:hi, :].rearrange("(t p) n -> p t n", p=128),
                      r=[C.vbuf[tt // 2][tt % 2] for tt in range(lo // 128, (hi - 1) // 128 + 1)], w=[vloc])
                for bi in range(nl):
                    blocks.append(("l", bi, (lo + bi * 128 - qc0) // 128))
            for gk in range(4):
                for bi, (kind, idx, rel) in enumerate(blocks):
                    if kind == "c":
                        kap = kctx[:, gk, idx * 128:(idx + 1) * 128]
                        vap = vctx[:, idx, gk * 128:(gk + 1) * 128]
                        rk, rv = kctx, vctx
                    else:
                        kap = kloc[:, gk, idx * 128:(idx + 1) * 128]
                        vap = vloc[:, idx, gk * 128:(gk + 1) * 128]
                        rk, rv = kloc, vloc
                    ps = C.psA.next()
                    P.mm(ps[:, 0:512], kap, sq[:, gk * 512:(gk + 1) * 512], True, True, [rk, sq], [ps])
                    e = eb.next()
                    P.act(e[:], ps[:, 0:512], AF.Exp, [ps], [e], scale=SCALE_B)
                    if rel == -1:
                        P.tt("pool", e[:], e[:], mprev[:], ALU.mult, [e, mprev], [e])
                    elif rel == 1:
                        P.tt("pool", e[:], e[:], mnext[:], ALU.mult, [e, mnext], [e])
                    first = bi == 0
                    lastb = bi == len(blocks) - 1
                    P.mm(psD[:], onesb[:], e[:], first, lastb, [onesb, e], [psD])
                    for hq in range(4):
                        P.mm(psOt[:, hq * 128:(hq + 1) * 128], vap, e[:, hq * 128:(hq + 1) * 128],
                             first and hq == 0, lastb, [rv, e], [psOt], skip=True)
                r = rd.next()
                for hq in range(4):
                    h = gk * 4 + hq
                    P.ts("dve", r[:, hq * 128:(hq + 1) * 128], psD[:, hq * 128:(hq + 1) * 128],
                         sinkexp[:, h:h + 1], ALU.add, [psD, sinkexp], [r])
                P.op("dve", (lambda rr: (lambda e_: e_.reciprocal(out=rr, in_=rr)))(r[:]), [r], [r])
                P.tt("dve", C.uT[:, gk * 4:(gk + 1) * 4, t * 128:(t + 1) * 128],
                     psOt[:].rearrange("p (a b) -> p a b", b=128), r[:].rearrange("p (a b) -> p a b", b=128),
                     ALU.mult, [psOt, r], [C.uT])
        C.uTx = C.uT
        layer_tail(C, i, g, s, src, wout, last, w1_d, w2_d)


def ssd_layer(C, i, jC, src, last, emit_ctx, w1_d, w2_d):
    P = C.P
    NG, NT = C.NG, C.NT
    win = C.cwin_d[jC]
    wout = C.cwout_d[jC]
    A, Bf = C.arf, C.arb
    P.barrier()
    A.reset()
    Bf.reset()
    dtb = A.take([1])
    aneg = A.take([1])
    cdt = A.take([64])
    convw = A.take([48, 5])
    convb = A.take([48])
    onc = A.take([32])
    P.dma("sp", dtb[:], C.cdtb_d[:, jC:jC + 1], w=[dtb])
    P.dma("sp", aneg[:], C.calog_d[:, jC:jC + 1], w=[aneg])
    P.act(aneg[:], aneg[:], AF.Exp, [aneg], [aneg])
    P.ts("dve", aneg[:], aneg[:], -1.0, ALU.mult, [aneg], [aneg])
    P.dma("sp", cdt[:], C.cd_d[:, jC * 64:(jC + 1) * 64], w=[cdt])
    P.dma("sp", convw[:], C.cconvw_d[:, jC], w=[convw])
    P.dma("sp", convb[:], C.cconvb_d[:, jC], w=[convb])
    P.dma("sp", onc[:], C.conorm_d[:, jC], w=[onc])
    base_a, base_b = A.off, Bf.off

    xp = A.takes([4, W], 2)
    dtl = A.takes([2, W], 2)
    dtm = A.takes([256], 2)
    ee = A.takes([W], 2)
    zt = Bf.takes([512], 3)
    for g in range(NG):
        s = 1 if g == 0 else 0
        c0 = g * W
        P.dma("sp", C.hg[:], src[:, :, c0:c0 + W].rearrange("k p n -> p k n"), r=[C.hbuf[g]], w=[C.hg])
        modulate(C, C.hg, s, 0, 1)
        for blk in range(12):
            wt = C.wb.next()
            wv = load_w(C, wt, 0, C.S_in[i], 32 + blk * 4, 4)
            st = xp.next()
            for jj in range(4):
                ps = C.psA.next()
                for kc in range(KC):
                    P.mm(ps[:, 0:W], wv[:, jj, kc, :], C.uT[:, kc, :], kc == 0, kc == KC - 1,
                         [wt, C.uT], [ps])
                P.cp("act", st[:, jj, :], ps[:, 0:W], [ps], [st])
            P.dma("act", C.XP[blk * 4:(blk + 1) * 4, :, c0:c0 + W].rearrange("c p n -> p c n"), st[:],
                  r=[st], w=[C.xpbuf[g][blk]])
        wt = C.wb.next()
        wv = load_w(C, wt, 0, C.S_in[i], 80, 1)
        ps = C.psA.next()
        for kc in range(KC):
            P.mm(ps[:, 0:W], wv[:, 0, kc, :], C.uT[:, kc, :], kc == 0, kc == KC - 1, [wt, C.uT], [ps])
        e1 = ee.next()
        P.act(e1[:], ps[:, 0:W], AF.Exp, [ps, dtb], [e1], bias=dtb[:, 0:1])
        P.ts("dve", e1[:], e1[:], 1.0, ALU.add, [e1], [e1])
        dl = dtl.next()
        P.act(dl[:, 0, :], e1[:], AF.Ln, [e1], [dl])
        P.ts("dve", dl[:, 1, :], dl[:, 0, :], aneg[:, 0:1], ALU.mult, [dl, aneg], [dl])
        for t in range(2):
            ps = C.psA.next()
            for a in range(2):
                P.mm(ps[:, a * 128:(a + 1) * 128], dl[:, a, t * 128:(t + 1) * 128], C.identf[:], True, True,
                     [dl, C.identf], [ps])
            dm = dtm.next()
            P.cp("act", dm[:], ps[:, 0:256], [ps], [dm])
            P.dma("act", C.DTM[c0 + t * 128:c0 + (t + 1) * 128, :], dm[:], r=[dm], w=[C.dtmbuf[g][t]])
        for nb in range(8):
            wt = C.wb.next()
            wv = load_w(C, wt, 0, C.S_in[i], nb * 4, 4)
            for t in range(2):
                ps = C.psA.next()
                for kc in range(KC):
                    P.mm(ps[:, 0:512].rearrange("p (c n) -> p c n", n=128), C.uT[:, kc, t * 128:(t + 1) * 128],
                         wv[:, :, kc, :], kc == 0, kc == KC - 1, [wt, C.uT], [ps])
                z = zt.next()
                P.act(z[:], ps[:, 0:512], AF.Silu, [ps], [z])
                P.dma("act", C.SZ[c0 + t * 128:c0 + (t + 1) * 128, nb * 512:(nb + 1) * 512], z[:], r=[z],
                      w=[C.szbuf[g][t][nb]])

    P.barrier()
    A.off, Bf.off = base_a, base_b
    xin = A.takes([W + 4], 3)
    acc = A.takes([W], 2)
    tms = Bf.takes([1024], 2)
    for g in range(NG):
        c0 = g * W
        seq_lo = 0 if g == 0 else TCTX
        seq_hi = TCTX if g == 0 else NT
        lo = max(seq_lo, c0 - 2)
        hi = min(seq_hi, c0 + W + 2)
        gl = list(range(max(0, g - 1), min(NG, g + 2)))
        for cc in range(48):
            xi = xin.next()
            if lo > c0 - 2:
                P.memset("pool", xi[:, 0:2], 0.0, [xi])
            if hi < c0 + W + 2:
                P.memset("pool", xi[:, W + 2:W + 4], 0.0, [xi])
            P.dma("sp", xi[:, lo - (c0 - 2):hi - (c0 - 2)], C.XP[cc, :, lo:hi],
                  r=[C.xpbuf[gg][cc // 4] for gg in gl], w=[xi])
            ac = acc.next()
            P.ts("dve", ac[:], xi[:, 0:W], convw[:, cc, 0:1], ALU.mult, [xi, convw], [ac])
            for j in range(1, 5):
                P.stt("dve", ac[:], xi[:, j:j + W], convw[:, cc, j:j + 1], ac[:], ALU.mult, ALU.add,
                      [xi, convw, ac], [ac])
            P.act(C.h1T[:, cc, :], ac[:], AF.Silu, [ac, convb], [C.h1T], bias=convb[:, cc:cc + 1])
        P.dma("act", C.BT[:, :, c0:c0 + W].rearrange("g p n -> p g n"), C.h1T[:, 32:40, :], r=[C.h1T],
              w=[C.btbuf[g]])
        P.dma("act", C.CT[:, :, c0:c0 + W].rearrange("g p n -> p g n"), C.h1T[:, 40:48, :], r=[C.h1T],
              w=[C.ctbuf[g]])
        for t in range(2):
            for b8 in range(5):
                for j in range(8):
                    P.tr(C.psT[:, j * 128:(j + 1) * 128], C.h1T[:, b8 * 8 + j, t * 128:(t + 1) * 128], C.identb[:],
                         [C.h1T, C.identb], [C.psT])
                tm = tms.next()
                P.cp("act", tm[:], C.psT[:], [C.psT], [tm])
                if b8 < 4:
                    P.dma("act", C.XTM[c0 + t * 128:c0 + (t + 1) * 128, b8 * 1024:(b8 + 1) * 1024], tm[:], r=[tm],
                          w=[C.xtbuf[g][t][b8]])
                else:
                    P.dma("act", C.BTM[c0 + t * 128:c0 + (t + 1) * 128, :], tm[:], r=[tm], w=[C.btmbuf[g][t]])

    P.barrier()
    A.off, Bf.off = base_a, base_b
    h1 = C.h1T.t[:].rearrange("p a b -> p (a b)")
    x_tm, xdt, xw = Buf(h1[:, 0:4096]), Buf(h1[:, 4096:8192]), Buf(h1[:, 8192:12288])
    hSb = Buf(h1[:, 12288:16384].rearrange("p (g n) -> p g n", g=8))
    hS = A.take([8, 512])
    seg = A.takes([1024], 2)
    cbm = A.takes([128], 2)
    ysb = A.takes([512], 2)
    dtm = A.takes([256], 2)
    sm = {n: A.takes([64], 2) for n in ("cum", "tot", "ecum", "ecend", "fdec", "dtf")}
    btm = Bf.takes([1024], 2)
    bT = Bf.takes([8, 128], 2)
    cT = Bf.takes([8, 128], 2)
    wm = Bf.takes([1024], 2)
    psO0 = Buf(C.psO[0].t)
    psO1 = C.psO[1].t[:].rearrange("p a b -> p (a b)")
    psY, psI = Buf(psO1[:, 0:512]), Buf(psO1[:, 512:1024])
    psS = Buf(C.psS.t[:].rearrange("p a b -> p (a b)"))
    for d in range(2):
        order = list(range(NG)) if d == 0 else [0] + list(range(NG - 1, 0, -1))
        TRI = C.trif if d == 0 else C.trib
        TRI = C.tri128[d]
        P.memset("dve", hS[:], 0.0, [hS])
        P.memset("pool", hSb[:], 0.0, [hSb])
        for g in order:
            c0 = g * W
            for t in ([0, 1] if d == 0 else [1, 0]):
                r0 = c0 + t * 128
                P.dma("sp", x_tm[:], C.XTM[r0:r0 + 128, :], r=C.xtbuf[g][t], w=[x_tm])
                bm = btm.next()
                P.dma("sp", bm[:], C.BTM[r0:r0 + 128, :], r=[C.btmbuf[g][t]], w=[bm])
                bt_, ct_ = bT.next(), cT.next()
                P.dma("sp", bt_[:], C.BT[:, :, r0:r0 + 128].rearrange("g p n -> p g n"), r=[C.btbuf[g]], w=[bt_])
                P.dma("sp", ct_[:], C.CT[:, :, r0:r0 + 128].rearrange("g p n -> p g n"), r=[C.ctbuf[g]], w=[ct_])
                dm = dtm.next()
                P.dma("sp", dm[:], C.DTM[r0:r0 + 128, :], r=[C.dtmbuf[g][t]], w=[dm])
                dt_t = dm[:, d * 64:(d + 1) * 64]
                la_t = dm[:, 128 + d * 64:128 + (d + 1) * 64]
                ps = C.psA.next()
                P.mm(ps[:, 0:64], TRI[:], la_t, True, True, [TRI, dm], [ps])
                P.mm(ps[:, 64:128], C.ones[:], la_t, True, True, [C.ones, dm], [ps])
                cum, tot = sm["cum"].next(), sm["tot"].next()
                P.cp("act", cum[:], ps[:, 0:64], [ps], [cum])
                P.cp("act", tot[:], ps[:, 64:128], [ps], [tot])
                ecum, ecend, fdec, dtf = sm["ecum"].next(), sm["ecend"].next(), sm["fdec"].next(), sm["dtf"].next()
                P.act(ecum[:], cum[:], AF.Exp, [cum], [ecum])
                P.act(ecend[:], tot[:], AF.Exp, [tot], [ecend])
                P.tt("dve", fdec[:], tot[:], cum[:], ALU.subtract, [tot, cum], [fdec])
                P.act(fdec[:], fdec[:], AF.Exp, [fdec], [fdec])
                P.tt("dve", dtf[:], fdec[:], dt_t, ALU.mult, [fdec, dm], [dtf])
                x3 = x_tm[:].rearrange("p (h q) -> p h q", q=64)
                P.tt("pool", xdt[:].rearrange("p (h q) -> p h q", q=64), x3,
                     dt_t.unsqueeze(2).to_broadcast([128, 64, 64]), ALU.mult, [x_tm, dm], [xdt])
                P.tt("dve", xw[:].rearrange("p (h q) -> p h q", q=64), x3,
                     dtf[:].unsqueeze(2).to_broadcast([128, 64, 64]), ALU.mult, [x_tm, dtf], [xw])
                for gi in range(8):
                    psc = C.psA.next()
                    P.mm(psc[:, 0:128], bt_[:, gi, :], ct_[:, gi, :], True, True, [bt_, ct_], [psc])
                    cb = cbm.next()
                    P.tt("dve", cb[:], psc[:, 0:128], TRI[:], ALU.mult, [psc, TRI], [cb])
                    for e in range(8):
                        hh = gi * 8 + e
                        P.mm(psO0[:, e, :], la_t[:, hh:hh + 1].to_broadcast([128, 128]), TRI[:], True, True,
                             [dm, TRI], [psO0])
                    sg1 = seg.next()
                    P.tt("dve", sg1[:].rearrange("p (h q) -> p h q", q=128), psO0[:],
                         cum[:, gi * 8:(gi + 1) * 8].unsqueeze(2).to_broadcast([128, 8, 128]), ALU.subtract,
                         [psO0, cum], [sg1])
                    P.ts("pool", sg1[:], sg1[:], 0.0, ALU.min, [sg1], [sg1])
                    P.act(sg1[:], sg1[:], AF.Exp, [sg1], [sg1])
                    w_ = wm.next()
                    P.tt("pool", w_[:].rearrange("p (h q) -> p h q", q=128),
                         sg1[:].rearrange("p (h q) -> p h q", q=128),
                         cb[:].unsqueeze(1).to_broadcast([128, 8, 128]), ALU.mult, [sg1, cb], [w_])
                    for e in range(8):
                        hh = gi * 8 + e
                        P.mm(psY[:, e * 64:(e + 1) * 64], w_[:, e * 128:(e + 1) * 128], xdt[:, hh * 64:(hh + 1) * 64],
                             e == 0, True, [w_, xdt], [psY], skip=True)
                    P.mm(psI[:], ct_[:, gi, :], hSb[:, gi, :], True, True, [ct_, hSb], [psI])
                    yb_ = ysb.next()
                    P.tt("dve", yb_[:].rearrange("p (h q) -> p h q", q=64),
                         psI[:].rearrange("p (h q) -> p h q", q=64),
                         ecum[:, gi * 8:(gi + 1) * 8].unsqueeze(2).to_broadcast([128, 8, 64]), ALU.mult,
                         [psI, ecum], [yb_])
                    P.tt("dve", yb_[:], yb_[:], psY[:], ALU.add, [yb_, psY], [yb_])
                    P.dma("act", C.Y[d][r0:r0 + 128, gi * 512:(gi + 1) * 512], yb_[:], r=[yb_],
                          w=[C.ybuf[d][g][t][gi]])
                    P.mm(psS[:], bm[:, gi * 128:(gi + 1) * 128], xw[:, gi * 512:(gi + 1) * 512], True, True,
                         [bm, xw], [psS])
                    P.tt("pool", hS[:, gi, :].rearrange("p (h q) -> p h q", q=64),
                         hS[:, gi, :].rearrange("p (h q) -> p h q", q=64),
                         ecend[:, gi * 8:(gi + 1) * 8].unsqueeze(2).to_broadcast([128, 8, 64]), ALU.mult,
                         [hS, ecend], [hS])
                    P.tt("dve", hS[:, gi, :], hS[:, gi, :], psS[:], ALU.add, [hS, psS], [hS])
                    P.cp("pool", hSb[:, gi, :], hS[:, gi, :], [hS], [hSb])

    P.barrier()
    A.off, Bf.off = base_a, base_b
    y0q = A.takes([1024], 2)
    y1q = A.takes([1024], 2)
    tq = A.takes([1024], 2)
    ss = A.takes([8], 2)
    xq = Bf.takes([1024], 2)
    szq = Bf.takes([1024], 2)
    sqt = Bf.takes([512], 2)
    yyb = Bf.take([4096])
    yy2 = Bf.take([4096])
    for g in range(NG):
        if g == 0 and not emit_ctx:
            continue
        s = 1 if g == 0 else 0
        c0 = g * W
        for t in range(2):
            r0 = c0 + t * 128
            ssv = ss.next()
            for q in range(4):
                cs = slice(q * 1024, (q + 1) * 1024)
                y0, y1, xq_, sz_ = y0q.next(), y1q.next(), xq.next(), szq.next()
                P.dma("sp", y0[:], C.Y[0][r0:r0 + 128, cs], r=C.ybuf[0][g][t][2 * q:2 * q + 2], w=[y0])
                P.dma("sp", y1[:], C.Y[1][r0:r0 + 128, cs], r=C.ybuf[1][g][t][2 * q:2 * q + 2], w=[y1])
                P.dma("sp", xq_[:], C.XTM[r0:r0 + 128, cs], r=[C.xtbuf[g][t][q]], w=[xq_])
                P.dma("sp", sz_[:], C.SZ[r0:r0 + 128, cs], r=C.szbuf[g][t][2 * q:2 * q + 2], w=[sz_])
                P.tt("dve", y0[:], y0[:], y1[:], ALU.add, [y0, y1], [y0])
                tt_ = tq.next()
                P.tt("pool", tt_[:].rearrange("p (h q) -> p h q", q=64), xq_[:].rearrange("p (h q) -> p h q", q=64),
                     cdt[:, q * 16:(q + 1) * 16].unsqueeze(2).to_broadcast([128, 16, 64]), ALU.mult,
                     [xq_, cdt], [tt_])
                P.tt("dve", y0[:], y0[:], tt_[:], ALU.add, [y0, tt_], [y0])
                P.tt("dve", yyb[:, cs], y0[:], sz_[:], ALU.mult, [y0, sz_], [yyb])
                for gg in range(2):
                    sq_ = sqt.next()
                    gcs = slice(q * 1024 + gg * 512, q * 1024 + (gg + 1) * 512)
                    P.act(sq_[:], yyb[:, gcs], AF.Square, [yyb], [sq_])
                    P.op("dve", (lambda o_, i_: (lambda e_: e_.reduce_sum(out=o_, in_=i_, axis=mybir.AxisListType.X)))(
                        ssv[:, 2 * q + gg:2 * q + gg + 1], sq_[:]), [sq_], [ssv])
            P.ts("dve", ssv[:], ssv[:], 1.0 / 512.0, ALU.mult, [ssv], [ssv], s2=EPS, op1=ALU.add)
            P.act(ssv[:], ssv[:], AF.Ln, [ssv], [ssv])
            P.act(ssv[:], ssv[:], AF.Exp, [ssv], [ssv], scale=-0.5)
            for gi in range(8):
                P.ts("pool", yy2[:, gi * 512:(gi + 1) * 512], yyb[:, gi * 512:(gi + 1) * 512], ssv[:, gi:gi + 1],
                     ALU.mult, [yyb, ssv], [yy2])
            for b8 in range(4):
                for j in range(8):
                    cc = b8 * 8 + j
                    P.tr(C.psT[:, j * 128:(j + 1) * 128], yy2[:, cc * 128:(cc + 1) * 128], C.identb[:],
                         [yy2, C.identb], [C.psT])
                P.tt("dve", C.h1T[:, b8 * 8:(b8 + 1) * 8, t * 128:(t + 1) * 128],
                     C.psT[:].rearrange("p (c n) -> p c n", n=128),
                     onc[:, b8 * 8:(b8 + 1) * 8].unsqueeze(2).to_broadcast([128, 8, 128]), ALU.mult,
                     [C.psT, onc], [C.h1T])
        C.uTx = C.h1T
        layer_tail(C, i, g, s, src, wout, last, w1_d, w2_d, nkc=32)


def hgrn2_layer(C, i, jA, src, last, emit_ctx, w1_d, w2_d):
    P = C.P
    NG = C.NG
    win = C.awin_d[jA]
    wout = C.awout_d[jA]
    P.barrier()
    C.arf.reset()
    C.arb.reset()
    C.e1 = {n: C.arf.takes([W], 2) for n in
            ("qt", "sg", "f", "lf", "k", "pfx", "g", "eg", "egn", "kgf")}
    C.decst = C.arf.take([2, KC, W // CH])
    C.sDEC = C.arf.takes([KC, W // CH], 2)
    C.S = [C.arf.take([8, 128]) for h in range(2)]
    C.osb = C.arf.takes([8, 128], 1)
    C.st8 = C.arb.takes([8, W], 2)
    C.sQ, C.sK, C.sKE, C.sV = Buf(), Buf(), Buf(), Buf()
    C.ketm = C.arb.takes([8 * 128], 2)
    C.vtm = C.arb.takes([8 * 128], 2)
    C.vm = C.arb.takes([8 * 128], 3)
    C.am = C.arb.takes([512], 2)
    C.Sb = [C.arb.take([8, 128]) for h in range(2)]
    E = C.e1
    for g in range(NG):
        s = 1 if g == 0 else 0
        c0 = g * W
        P.dma("sp", C.hg[:], src[:, :, c0:c0 + W].rearrange("k p n -> p k n"), r=[C.hbuf[g]], w=[C.hg])
        modulate(C, C.hg, s, 0, 1)
        for h in range(KC):
            wa = C.wb.next()
            load_w(C, wa, 0, C.S_in[i], 0 * 16 + h, 1)
            load_w(C, wa, 1, C.S_in[i], 3 * 16 + h, 1)
            load_w(C, wa, 2, C.S_in[i], 4 * 16 + h, 1)
            wav = load_w(C, wa, 3, C.S_in[i], 2 * 16 + h, 1)
            wg = C.wb.next()
            wgv = load_w(C, wg, 0, C.S_in[i], 1 * 16 + h, 1)
            st = C.st8.next()

            def proj(wbuf, wview, slot):
                ps = C.psA.next()
                for kc in range(KC):
                    P.mm(ps[:, 0:W], wview[:, slot, kc, :], C.uT[:, kc, :], kc == 0, kc == KC - 1,
                         [wbuf, C.uT], [ps])
                return ps
            psq = proj(wa, wav, 0)
            qt = E["qt"].next()
            P.act(qt[:], psq[:, 0:W], AF.Silu, [psq], [qt])
            Dv = [dict(), dict()]

            def s0(d, v):
                v["psf"] = proj(wa, wav, 1 + d)
                v["sg"] = E["sg"].next()
                P.act(v["sg"][:], v["psf"][:, 0:W], AF.Sigmoid, [v["psf"]], [v["sg"]])

            def s1(d, v):
                v["f"] = E["f"].next()
                P.ts("dve", v["f"][:], v["sg"][:], C.oml[:, i, d, h:h + 1], ALU.mult, [v["sg"], C.oml, C.lb],
                     [v["f"]], s2=C.lb[:, i, d, h:h + 1], op1=ALU.add)

            def s2(d, v):
                v["lf"] = E["lf"].next()
                P.act(v["lf"][:], v["f"][:], AF.Ln, [v["f"]], [v["lf"]])
                v["k"] = E["k"].next()
                P.ts("pool", v["k"][:], v["f"][:], -1.0, ALU.mult, [v["f"]], [v["k"]], s2=1.0, op1=ALU.add)

            def s3(d, v):
                pfx = E["pfx"].next()
                v["pfx"] = pfx
                P.scan(pfx[:], C.mskr[:], v["lf"][:], [C.mskr, v["lf"]], [pfx])
                pfx3 = pfx[:].rearrange("p (c k) -> p c k", k=CH)
                v["pfx3"] = pfx3
                tot_b = pfx3[:, :, CH - 1:CH].to_broadcast([128, W // CH, CH])
                if d == 0:
                    v["G"] = pfx
                else:
                    G = E["g"].next()
                    v["G"] = G
                    P.tt("dve", G[:], v["lf"][:], pfx[:], ALU.subtract, [v["lf"], pfx], [G])
                    P.tt("dve", G[:].rearrange("p (c k) -> p c k", k=CH),
                         G[:].rearrange("p (c k) -> p c k", k=CH), tot_b, ALU.add, [G, pfx], [G])

            def s4(d, v):
                G = v["G"]
                v["eg"] = E["eg"].next()
                P.act(v["eg"][:], G[:], AF.Exp, [G], [v["eg"]])
                v["egn"] = E["egn"].next()
                P.act(v["egn"][:], G[:], AF.Exp, [G], [v["egn"]], scale=-1.0)
                P.act(C.decst[:, d, h, :], v["pfx3"][:, :, CH - 1], AF.Exp, [v["pfx"]], [C.decst])

            def s5(d, v):
                P.tt("pool", st[:, 3 * d + 0, :], qt[:], v["eg"][:], ALU.mult, [qt, v["eg"]], [st])
                v["kgf"] = E["kgf"].next()
                P.tt("dve", v["kgf"][:], v["k"][:], v["egn"][:], ALU.mult, [v["k"], v["egn"]], [v["kgf"]])

            def s6(d, v):
                kgf = v["kgf"]
                P.cp("pool", st[:, 3 * d + 1, :], kgf[:], [kgf], [st])
                P.tt("dve", st[:, 3 * d + 2, :].rearrange("p (c k) -> p c k", k=CH),
                     kgf[:].rearrange("p (c k) -> p c k", k=CH),
                     C.decst[:, d, h, :].unsqueeze(2).to_broadcast([128, W // CH, CH]), ALU.mult,
                     [kgf, C.decst], [st])

            for step in (s0, s1, s2, s3, s4, s5, s6):
                for d in range(2):
                    step(d, Dv[d])
            psv = proj(wa, wav, 3)
            P.cp("act", st[:, 6, :], psv[:, 0:W], [psv], [st])
            psg = proj(wg, wgv, 0)
            P.act(st[:, 7, :], psg[:, 0:W], AF.Silu, [psg], [st])
            for ty in range(8):
                P.dma("act", C.SCR[ty][h, :, c0:c0 + W], st[:, ty, :], r=[st], w=[C.scrbuf[g][ty][h]])
        P.dma("act", C.DEC[:, :, :, g * (W // CH):(g + 1) * (W // CH)], C.decst[:], r=[C.decst], w=[C.decbuf[g]])

    if os.environ.get("KSTOP") == "ph1":
        return
    P.barrier()
    h1 = C.h1T.t
    sQ, sK, sKE, sV = C.sQ, C.sK, C.sKE, C.sV
    sQ.t, sK.t, sKE.t, sV.t = h1[:, 0:16, :], h1[:, 16:32, :], h1[:, 32:48, :], h1[:, 48:64, :]
    for d in range(2):
        order = list(range(NG)) if d == 0 else [0] + list(range(NG - 1, 0, -1))
        for hh in range(2):
            P.memset("dve", C.S[hh][:], 0.0, [C.S[hh]])
            P.memset("pool", C.Sb[hh][:], 0.0, [C.Sb[hh]])
        mask = C.trif if d == 0 else C.trib
        for g in order:
            c0 = g * W
            sDEC = C.sDEC.next()
            for buf, ty in ((sQ, 3 * d + 0), (sK, 3 * d + 1), (sKE, 3 * d + 2), (sV, 6)):
                P.dma("sp", buf.t, C.SCR[ty][:, :, c0:c0 + W].rearrange("h p n -> p h n"),
                      r=C.scrbuf[g][ty], w=[buf])
            P.dma("sp", sDEC[:], C.DEC[:, d, :, g * (W // CH):(g + 1) * (W // CH)], r=[C.decbuf[g]], w=[sDEC])
            tiles = [0, 1] if d == 0 else [1, 0]
            for t in tiles:
                tc0 = t * 128
                chunks = list(range(NCH)) if d == 0 else list(range(NCH - 1, -1, -1))
                ketm = [None, None]
                vms = [None, None]
                vtms = [None, None]
                for hh in range(2):
                    for hq in range(8):
                        h = hh * 8 + hq
                        P.tr(C.psT[:, hq * 128:(hq + 1) * 128], sKE[:, h, tc0:tc0 + 128], C.identb[:],
                             [sKE, C.identb], [C.psT])
                    ketm[hh] = C.ketm.next()
                    P.cp("act", ketm[hh][:], C.psT[:], [C.psT], [ketm[hh]])
                    for hq in range(8):
                        h = hh * 8 + hq
                        P.tr(C.psT[:, hq * 128:(hq + 1) * 128], sV[:, h, tc0:tc0 + 128], C.identb[:],
                             [sV, C.identb], [C.psT])
                    vtms[hh] = C.vtm.next()
                    P.cp("act", vtms[hh][:], C.psT[:], [C.psT], [vtms[hh]])
                    for q4 in range(2):
                        psa = C.psA.next()
                        for j in range(4):
                            h = hh * 8 + q4 * 4 + j
                            P.mm(psa[:, j * 128:(j + 1) * 128], sK[:, h, tc0:tc0 + 128], sQ[:, h, tc0:tc0 + 128],
                                 True, True, [sK, sQ], [psa])
                        am = C.am.next()
                        P.tt("dve", am[:], psa[:], mask[:], ALU.mult, [psa, mask], [am])
                        for j in range(4):
                            hq = q4 * 4 + j
                            P.mm(C.psO[hh][:, hq, :], vtms[hh][:, hq * 128:(hq + 1) * 128],
                                 am[:, j * 128:(j + 1) * 128], j == 0, False, [vtms[hh], am], [C.psO[hh]],
                                 skip=True)
                for c in chunks:
                    for hh in range(2):
                        for hq in range(8):
                            h = hh * 8 + hq
                            P.mm(C.psO[hh][:, hq, c * CH:(c + 1) * CH], C.Sb[hh][:, hq, :],
                                 sQ[:, h, tc0 + c * CH:tc0 + (c + 1) * CH], False, True,
                                 [C.Sb[hh], sQ], [C.psO[hh]], skip=True)
                        ci = t * NCH + c
                        vm = C.vm.next()
                        P.ts("pool", vm[:], vtms[hh][:], C.rowm[:, c:c + 1], ALU.mult, [vtms[hh], C.rowm], [vm])
                        P.tt("dve", C.S[hh][:], C.S[hh][:],
                             sDEC[:, hh * 8:(hh + 1) * 8, ci].unsqueeze(2).to_broadcast([128, 8, 128]),
                             ALU.mult, [C.S[hh], sDEC], [C.S[hh]])
                        for q4 in range(2):
                            for j in range(4):
                                hq = q4 * 4 + j
                                P.mm(C.psS[:, j, :], ketm[hh][:, hq * 128:(hq + 1) * 128],
                                     vm[:, hq * 128:(hq + 1) * 128], True, True,
                                     [ketm[hh], vm], [C.psS])
                            P.tt("dve", C.S[hh][:, q4 * 4:(q4 + 1) * 4, :], C.S[hh][:, q4 * 4:(q4 + 1) * 4, :],
                                 C.psS[:], ALU.add, [C.S[hh], C.psS], [C.S[hh]])
                        P.cp("pool", C.Sb[hh][:], C.S[hh][:], [C.S[hh]], [C.Sb[hh]])
                for hh in range(2):
                    ob = C.osb.next()
                    P.cp("act", ob[:], C.psO[hh][:], [C.psO[hh]], [ob])
                    P.dma("act", C.OSC[d, hh * 8:(hh + 1) * 8, :, c0 + tc0:c0 + tc0 + 128].rearrange("h p n -> p h n"),
                          ob[:], r=[ob], w=[C.obuf[d][g][t * 2 + hh]])

    if os.environ.get("KSTOP") == "ph3":
        return
    P.barrier()
    for g in range(NG):
        if g == 0 and not emit_ctx:
            continue
        s = 1 if g == 0 else 0
        c0 = g * W
        P.dma("sp", C.yb[:], C.OSC[0, :, :, c0:c0 + W].rearrange("h p n -> p h n"), r=C.obuf[0][g], w=[C.yb])
        P.dma("sp", C.hg[:], C.OSC[1, :, :, c0:c0 + W].rearrange("h p n -> p h n"), r=C.obuf[1][g], w=[C.hg])
        sgbs = []
        for hh in range(2):
            sgb = C.st8.next()
            P.dma("sp", sgb[:], C.SCR[7][hh * 8:(hh + 1) * 8, :, c0:c0 + W].rearrange("h p n -> p h n"),
                  r=C.scrbuf[g][7], w=[sgb])
            sgbs.append(sgb)
        for h in range(KC):
            sgb = sgbs[h // 8]
            P.tt("dve", C.yb[:, h, :], C.yb[:, h, :], C.hg[:, h, :], ALU.add, [C.yb, C.hg], [C.yb])
            sq = C.sq.next()
            P.act(sq[:], C.yb[:, h, :], AF.Square, [C.yb], [sq])
            psn = C.psA.next()
            P.mm(psn[:, 0:W], C.ones[:], sq[:], True, True, [C.ones, sq], [psn])
            rs = C.rs.next()
            P.ts("dve", rs[:], psn[:, 0:W], 128.0 * EPS, ALU.add, [psn], [rs])
            P.act(rs[:], rs[:], AF.Ln, [rs], [rs])
            P.act(rs[:], rs[:], AF.Exp, [rs], [rs], scale=-0.5)
            t = C.tmpf.next()
            P.tt("dve", t[:], C.yb[:, h, :], rs[:], ALU.mult, [C.yb, rs], [t])
            P.stt("dve", C.uT[:, h, :], t[:], C.aon[:, jA, h:h + 1], sgb[:, h % 8, :], ALU.mult, ALU.mult,
                  [t, C.aon, sgb], [C.uT])
        C.uTx = C.uT
        layer_tail(C, i, g, s, src, wout, last, w1_d, w2_d)


def host_consts():
    c = np.zeros((128, 128 + W + 1024 + NCH), np.float32)
    c[:, 0:128] = np.eye(128, dtype=np.float32)
    m = np.ones((128, W), np.float32)
    m[:, 0::CH] = 0.0
    c[:, 128:128 + W] = m
    s = np.arange(128)[:, None]
    t = np.arange(128)[None, :]
    same = (s // CH) == (t // CH)
    trif = (same & (s <= t)).astype(np.float32)
    trib_ = (same & (s >= t)).astype(np.float32)
    c[:, 128 + W:128 + W + 512] = np.tile(trif, (1, 4))
    c[:, 128 + W + 512:128 + W + 1024] = np.tile(trib_, (1, 4))
    for cc in range(NCH):
        c[cc * CH:(cc + 1) * CH, 128 + W + 1024 + cc] = 1.0
    return c, np.tile(trib_, (1, 4)).astype(np.float32)


def colmajor(v):
    v = np.asarray(v, np.float32)
    lead = v.shape[:-1]
    a = v.reshape(*lead, KC, 128)
    a = np.moveaxis(a, -1, 0)
    return np.ascontiguousarray(a)


def make_inputs(b, x, c, ctx, c_ctx, ada_w, ada_b, norm_g, mlp_w1, mlp_w2, a_w_in, a_lb_logits, a_onorm,
                a_w_out, kinds, b_w_qkv=None, b_sink=None, b_w_out=None, c_w_in=None, c_conv_w=None,
                c_conv_b=None, c_dt_bias=None, c_a_log=None, c_d=None, c_onorm=None, c_w_out=None):
    depth = len(kinds)
    TL = x.shape[1]
    seq = np.concatenate([ctx[b], x[b]], axis=0)
    hT0 = np.ascontiguousarray(seq.T.reshape(KC, 128, seq.shape[0]))
    ccol = np.stack([colmajor(c[b]), colmajor(c_ctx)], axis=-1)
    adab = np.ascontiguousarray(np.moveaxis(np.asarray(ada_b[:depth]).reshape(depth, 96, 128), -1, 0))
    ng = colmajor(norm_g[:depth])
    consts, trib_ = host_consts()
    m = {
        "hT0": hT0, "ccol": np.ascontiguousarray(ccol), "ada_w": np.ascontiguousarray(ada_w[:depth]),
        "adab": adab, "ng": ng, "mlp_w1": np.ascontiguousarray(mlp_w1[:depth]),
        "mlp_w2": np.ascontiguousarray(mlp_w2[:depth]), "consts": consts, "trib": trib_,
    }
    nA = sum(1 for k in kinds if k == "A")
    if nA:
        m["lbl"] = colmajor(a_lb_logits)
        m["aon"] = colmajor(a_onorm[:nA])
        m["a_w_in"] = np.ascontiguousarray(a_w_in[:nA])
        m["a_w_out"] = np.ascontiguousarray(a_w_out[:nA])
    nB = sum(1 for k in kinds if k == "B")
    if nB:
        m["b_w_qkv"] = np.ascontiguousarray(b_w_qkv[:nB])
        m["b_w_out"] = np.ascontiguousarray(b_w_out[:nB])
        m["bsink"] = np.ascontiguousarray(np.broadcast_to(np.asarray(b_sink[:nB], np.float32).reshape(1, -1),
                                                          (128, nB * 16)))
        m["swac"], m["ropec"], m["ropes"] = swa_consts(TL)
    nC = sum(1 for k in kinds if k == "C")
    if nC:
        m["c_w_in"] = np.ascontiguousarray(c_w_in[:nC])
        m["c_w_out"] = np.ascontiguousarray(c_w_out[:nC])
        m["cdtb"] = np.ascontiguousarray(np.asarray(c_dt_bias[:nC], np.float32).reshape(nC, 128).T)
        m["calog"] = np.ascontiguousarray(np.asarray(c_a_log[:nC], np.float32).reshape(nC, 128).T)
        m["cdrep"] = np.ascontiguousarray(np.broadcast_to(np.asarray(c_d[:nC], np.float32).reshape(1, -1),
                                                          (128, nC * 64)))
        cw = np.asarray(c_conv_w[:nC], np.float32).reshape(nC, 5, 48, 128)
        m["cconvw"] = np.ascontiguousarray(cw.transpose(3, 0, 2, 1))
        cb = np.asarray(c_conv_b[:nC], np.float32).reshape(nC, 48, 128)
        m["cconvb"] = np.ascontiguousarray(cb.transpose(2, 0, 1))
        on = np.asarray(c_onorm[:nC], np.float32).reshape(nC, 32, 128)
        m["conorm"] = np.ascontiguousarray(on.transpose(2, 0, 1))
        si = np.arange(128)[:, None]
        ti = np.arange(128)[None, :]
        m["tri128"] = np.ascontiguousarray(np.concatenate([(si <= ti), (si >= ti)], axis=1).astype(np.float32))
    return m


def swa_consts(TL):
    c = np.zeros((128, 128 + 1024), np.float32)
    pm = np.zeros((128, 128), np.float32)
    for p in range(128):
        if p % 64 < 32:
            pm[p, p + 32] = -1.0
        else:
            pm[p, p - 32] = 1.0
    c[:, 0:128] = pm.T
    sidx = np.arange(128)[:, None]
    qidx = np.arange(128)[None, :]
    c[:, 128:640] = np.tile((sidx >= qidx).astype(np.float32), (1, 4))
    c[:, 640:1152] = np.tile((sidx <= qidx).astype(np.float32), (1, 4))
    inv = (10000.0 ** (-np.arange(32, dtype=np.float32) / 32.0)).astype(np.float32)
    tpos = np.arange(TL)
    row = (tpos // 64).astype(np.float32)
    col = (tpos % 64).astype(np.float32)
    ang = np.zeros((128, TL), np.float32)
    for p in range(128):
        base = row if p < 64 else col
        ang[p] = base * inv[p % 32]
    return c, np.cos(ang).astype(np.float32), np.sin(ang).astype(np.float32)


def run(inputs, kinds, ncores):
    x = np.asarray(inputs["x"], np.float32)
    B, TL, _ = x.shape
    nc = build(TL, kinds)
    args = {k: np.asarray(v, np.float32) for k, v in inputs.items()}
    in_maps = [make_inputs(b % B, kinds=kinds, **args) for b in range(ncores)]
    res = run_bass_kernel_spmd(nc, in_maps, core_ids=list(range(ncores)))
    out = np.empty((B, TL, D), np.float32)
    for b in range(B):
        o = res.results[b]["outT"]
        out[b] = o.reshape(D, TL).T
    return out


def kernel(**inputs):
    kinds = ["A", "B", "C", "A"]
    return run(inputs, kinds, 4)
```
